# Optimizing a Trainium2 kernel written in Bass

```python
import math
import jax
import jax.numpy as jnp
from jax import lax
import numpy as np


D_MODEL = 2048
BATCH = 4
SEQ = 4096
DEPTH = 2

N_MIXERS = 2
N_S5_LAYERS = (DEPTH + N_MIXERS - 1) // N_MIXERS
N_ATTN_LAYERS = DEPTH // N_MIXERS
S5_GROUP = 16
S5_GROUPS = D_MODEL // S5_GROUP
S5_STATE = 64
S5_DT_MIN = 1e-3
S5_DT_MAX = 1e-1
N_HEADS = 16
HEAD_DIM = D_MODEL // N_HEADS
MOBA_BLOCK = 256
MOBA_TOPK = 3
MOBA_QCHUNK = 16
REL_BUCKETS = 32
REL_MAX_DIST = 128
FFN_HIDDEN = (8 * D_MODEL + 3 * 256 - 1) // (3 * 256) * 256
RMS_EPS = 1e-6
NEG_INF = -1e30

kernel_name = 'hybrid_s5_moba_swiglu'


def rmsnorm(x, g):
    xf = x.astype(jnp.float32)
    y = xf * lax.rsqrt(jnp.mean(xf * xf, axis=-1, keepdims=True) + RMS_EPS)
    return (y * g.astype(jnp.float32)).astype(x.dtype)


def swiglu_ffn(u, w_in, w_out):
    gate, up = jnp.split(u @ w_in, 2, axis=-1)
    return (jax.nn.silu(gate) * up) @ w_out


def t5_bucket(rel):
    n = jnp.maximum(rel, 0)
    max_exact = REL_BUCKETS // 2
    nf = jnp.maximum(n, 1).astype(jnp.float32)
    large = max_exact + (jnp.log(nf / max_exact) / math.log(REL_MAX_DIST / max_exact)
                         * (REL_BUCKETS - max_exact)).astype(jnp.int32)
    large = jnp.minimum(large, REL_BUCKETS - 1)
    return jnp.where(n < max_exact, n, large)


def s5_mixer(u, a_re, a_im, log_step, b_re, b_im, c_re, c_im, d_skip, w_glu):
    bsz, seq, _ = u.shape
    f32 = jnp.float32
    ug = u.astype(f32).reshape(bsz, seq, S5_GROUPS, S5_GROUP)
    lam = lax.complex(a_re.astype(f32), a_im.astype(f32))
    step = jnp.exp(log_step.astype(f32))[:, None]
    a_bar = jnp.exp(lam * step)
    b = lax.complex(b_re.astype(f32), b_im.astype(f32))
    b_bar = ((a_bar - 1.0) / lam)[..., None] * b
    c = lax.complex(c_re.astype(f32), c_im.astype(f32))
    bu = jnp.einsum('blgh,gph->blgp', ug.astype(jnp.complex64), b_bar)
    a_seq = jnp.broadcast_to(a_bar, (1, seq) + a_bar.shape)

    def combine(e1, e2):
        a1, s1 = e1
        a2, s2 = e2
        return a2 * a1, a2 * s1 + s2

    _, states = lax.associative_scan(combine, (a_seq, bu), axis=1)
    y = jnp.real(jnp.einsum('blgp,ghp->blgh', states, c)) \
        + d_skip.astype(f32).reshape(S5_GROUPS, S5_GROUP) * ug
    z = jax.nn.gelu(y.reshape(bsz, seq, D_MODEL)).astype(u.dtype)
    val, gate = jnp.split(z @ w_glu, 2, axis=-1)
    return val * jax.nn.sigmoid(gate)


def moba_mixer(u, w_qkv, w_o, rel_bias):
    bsz, seq, _ = u.shape
    f32 = jnp.float32
    n_blk = -(-seq // MOBA_BLOCK)
    seq_p = n_blk * MOBA_BLOCK
    n_chunk = seq_p // MOBA_QCHUNK
    k_sel = min(MOBA_TOPK, n_blk)
    qkv = jnp.pad((u @ w_qkv).astype(f32), ((0, 0), (0, seq_p - seq), (0, 0)))
    qkv = qkv.reshape(bsz, seq_p, 3, N_HEADS, HEAD_DIM).transpose(2, 0, 3, 1, 4)
    q = qkv[0] * (HEAD_DIM ** -0.5)
    kb = qkv[1].reshape(bsz, N_HEADS, n_blk, MOBA_BLOCK, HEAD_DIM)
    vb = qkv[2].reshape(bsz, N_HEADS, n_blk, MOBA_BLOCK, HEAD_DIM)
    bias_tab = rel_bias.astype(f32)

    k_mean = jnp.mean(kb, axis=3)
    gate = jnp.einsum('bhtd,bhnd->bhtn', q, k_mean)
    pos = jnp.arange(seq_p)
    past = jnp.arange(n_blk)[None, :] < (pos // MOBA_BLOCK)[:, None]
    past = jnp.broadcast_to(past, gate.shape)
    _, sel = lax.top_k(jnp.where(past, gate, -jnp.inf), k_sel)
    sel_valid = jnp.take_along_axis(past, sel, axis=-1)

    def to_chunks(a):
        return jnp.moveaxis(a.reshape(bsz, N_HEADS, n_chunk, MOBA_QCHUNK, a.shape[-1]), 2, 0)

    gather_blocks = jax.vmap(jax.vmap(lambda blocks, idx: blocks[idx]))
    head_idx = jnp.arange(N_HEADS)[None, :, None, None, None]
    blk_off = jnp.arange(MOBA_BLOCK)

    def attend_chunk(args):
        qc, selc, validc, ci = args
        t = ci * MOBA_QCHUNK + jnp.arange(MOBA_QCHUNK)
        kg = gather_blocks(kb, selc)
        vg = gather_blocks(vb, selc)
        key_pos = selc[..., None] * MOBA_BLOCK + blk_off
        s_sel = jnp.einsum('bhqd,bhqnkd->bhqnk', qc, kg) \
            + bias_tab[t5_bucket(t[None, None, :, None, None] - key_pos), head_idx]
        s_sel = jnp.where(validc[..., None], s_sel, NEG_INF)
        blk = (ci * MOBA_QCHUNK) // MOBA_BLOCK
        k_own = lax.dynamic_index_in_dim(kb, blk, axis=2, keepdims=False)
        v_own = lax.dynamic_index_in_dim(vb, blk, axis=2, keepdims=False)
        rel = t[:, None] - (blk * MOBA_BLOCK + blk_off)[None, :]
        s_own = jnp.einsum('bhqd,bhkd->bhqk', qc, k_own) \
            + jnp.transpose(bias_tab[t5_bucket(rel)], (2, 0, 1))[None]
        s_own = jnp.where(rel >= 0, s_own, NEG_INF)
        logits = jnp.concatenate(
            [s_sel.reshape(bsz, N_HEADS, MOBA_QCHUNK, k_sel * MOBA_BLOCK), s_own], axis=-1)
        p = jax.nn.softmax(logits, axis=-1)
        p_sel = p[..., :k_sel * MOBA_BLOCK].reshape(bsz, N_HEADS, MOBA_QCHUNK, k_sel, MOBA_BLOCK)
        return jnp.einsum('bhqnk,bhqnkd->bhqd', p_sel, vg) \
            + jnp.einsum('bhqk,bhkd->bhqd', p[..., k_sel * MOBA_BLOCK:], v_own)

    out = lax.map(attend_chunk, (to_chunks(q), to_chunks(sel), to_chunks(sel_valid),
                                 jnp.arange(n_chunk)))
    out = jnp.moveaxis(out, 0, 2).reshape(bsz, N_HEADS, seq_p, HEAD_DIM)[:, :, :seq]
    out = out.transpose(0, 2, 1, 3).reshape(bsz, seq, D_MODEL).astype(u.dtype)
    return out @ w_o


def setup_inputs(seed: int = 0) -> dict:
    key = jax.random.key(seed)
    ks = jax.random.split(key, 18)
    f32 = jnp.float32
    nrm = lambda k, shape, s: jax.random.normal(k, shape, f32) * s
    x = jax.random.normal(ks[0], (BATCH, SEQ, D_MODEL), f32)
    norm_mix_g = 1.0 + nrm(ks[1], (DEPTH, D_MODEL), 0.01)
    norm_ffn_g = 1.0 + nrm(ks[2], (DEPTH, D_MODEL), 0.01)
    norm_final_g = 1.0 + nrm(ks[3], (D_MODEL,), 0.01)
    shp_gp = (N_S5_LAYERS, S5_GROUPS, S5_STATE)
    s5_a_re = -0.5 + nrm(ks[4], shp_gp, 0.01)
    s5_a_im = jnp.pi * jnp.arange(S5_STATE, dtype=f32) + nrm(ks[5], shp_gp, 0.01)
    s5_log_step = jax.random.uniform(ks[6], (N_S5_LAYERS, S5_GROUPS), f32,
                                     math.log(S5_DT_MIN), math.log(S5_DT_MAX))
    shp_b = (N_S5_LAYERS, S5_GROUPS, S5_STATE, S5_GROUP)
    s5_b_re = nrm(ks[7], shp_b, (2 * S5_GROUP) ** -0.5)
    s5_b_im = nrm(ks[8], shp_b, (2 * S5_GROUP) ** -0.5)
    shp_c = (N_S5_LAYERS, S5_GROUPS, S5_GROUP, S5_STATE)
    s5_c_re = nrm(ks[9], shp_c, S5_STATE ** -0.5)
    s5_c_im = nrm(ks[10], shp_c, S5_STATE ** -0.5)
    s5_d = nrm(ks[11], (N_S5_LAYERS, D_MODEL), 1.0)
    s5_w_glu = nrm(ks[12], (N_S5_LAYERS, D_MODEL, 2 * D_MODEL), D_MODEL ** -0.5)
    attn_w_qkv = nrm(ks[13], (N_ATTN_LAYERS, D_MODEL, 3 * D_MODEL), D_MODEL ** -0.5)
    attn_w_o = nrm(ks[14], (N_ATTN_LAYERS, D_MODEL, D_MODEL), D_MODEL ** -0.5)
    rel_bias = nrm(ks[15], (REL_BUCKETS, N_HEADS), 0.2)
    ffn_w_in = nrm(ks[16], (DEPTH, D_MODEL, 2 * FFN_HIDDEN), D_MODEL ** -0.5)
    ffn_w_out = nrm(ks[17], (DEPTH, FFN_HIDDEN, D_MODEL), FFN_HIDDEN ** -0.5)
    return {'x': x, 'norm_mix_g': norm_mix_g, 'norm_ffn_g': norm_ffn_g,
            'norm_final_g': norm_final_g, 's5_a_re': s5_a_re, 's5_a_im': s5_a_im,
            's5_log_step': s5_log_step, 's5_b_re': s5_b_re, 's5_b_im': s5_b_im,
            's5_c_re': s5_c_re, 's5_c_im': s5_c_im, 's5_d': s5_d, 's5_w_glu': s5_w_glu,
            'attn_w_qkv': attn_w_qkv, 'attn_w_o': attn_w_o, 'rel_bias': rel_bias,
            'ffn_w_in': ffn_w_in, 'ffn_w_out': ffn_w_out}


def reference(x, norm_mix_g, norm_ffn_g, norm_final_g, s5_a_re, s5_a_im, s5_log_step,
              s5_b_re, s5_b_im, s5_c_re, s5_c_im, s5_d, s5_w_glu, attn_w_qkv, attn_w_o,
              rel_bias, ffn_w_in, ffn_w_out):
    h = x
    for layer in range(DEPTH):
        u = rmsnorm(h, norm_mix_g[layer])
        j = layer // N_MIXERS
        if layer % N_MIXERS == 0:
            mix = s5_mixer(u, s5_a_re[j], s5_a_im[j], s5_log_step[j], s5_b_re[j], s5_b_im[j],
                           s5_c_re[j], s5_c_im[j], s5_d[j], s5_w_glu[j])
        else:
            mix = moba_mixer(u, attn_w_qkv[j], attn_w_o[j], rel_bias)
        h = h + mix.astype(h.dtype)
        h = h + swiglu_ffn(rmsnorm(h, norm_ffn_g[layer]), ffn_w_in[layer], ffn_w_out[layer]).astype(h.dtype)
    return rmsnorm(h, norm_final_g)
```

```python
from contextlib import ExitStack
from concourse.bass_utils import run_bass_kernel_spmd
import numpy as np
import concourse.bass as bass
import concourse.mybir as mybir

F32 = mybir.dt.float32
BF16 = mybir.dt.bfloat16
AF = mybir.ActivationFunctionType
ALU = mybir.AluOpType
AX = mybir.AxisListType

ENGS = ["tensor", "vector", "scalar", "gpsimd", "sync"]
NPOOL = 12


class Op:
    __slots__ = ("idx", "eng", "fn", "deps", "sig", "sem", "val", "dma", "prev_same_sem")

    def __init__(self, idx, eng, fn, dma):
        self.idx = idx
        self.eng = eng
        self.fn = fn
        self.deps = set()
        self.sig = False
        self.sem = None
        self.val = 0
        self.dma = dma
        self.prev_same_sem = None


class Prog:
    def __init__(self, nc, same_engine_sync=True):
        self.nc = nc
        self.ops = []
        self.last_w = {}
        self.readers = {}
        self.same_engine_sync = same_engine_sync
        self.fence_for = {}
        self.since_barrier = []
        self.out_dmas = []

    def add(self, eng, fn, reads=(), writes=(), dma=False, out=False):
        idx = len(self.ops)
        op = Op(idx, eng, fn, dma)
        deps = op.deps
        for k in reads:
            w = self.last_w.get(k)
            if w is not None:
                deps.add(w)
        for k in writes:
            w = self.last_w.get(k)
            if w is not None:
                deps.add(w)
            for r in self.readers.get(k, ()):
                deps.add(r)
        for k in reads:
            self.readers.setdefault(k, []).append(idx)
        for k in writes:
            self.last_w[k] = idx
            self.readers[k] = []
        f = self.fence_for.pop(eng, None)
        if f:
            deps.update(f)
        deps.discard(idx)
        self.ops.append(op)
        self.since_barrier.append(idx)
        if out:
            self.out_dmas.append(idx)
        return idx

    def barrier(self):
        last = {}
        f = []
        for i in self.since_barrier:
            o = self.ops[i]
            if o.dma:
                f.append(i)
            else:
                last[o.eng] = i
        f.extend(last.values())
        self.fence_for = {e: list(f) for e in ENGS}
        self.since_barrier = []

    def mm(self, out, lhsT, rhs, start, stop, reads, writes, **kw):
        return self.add("tensor", lambda e: e.matmul(out, lhsT, rhs, start=start, stop=stop, **kw),
                        reads, writes)

    def dma(self, eng, out, in_, reads, writes, is_out=False, **kw):
        return self.add(eng, lambda e: e.dma_start(out=out, in_=in_, **kw), reads, writes,
                        dma=True, out=is_out)

    def emit(self, stack):
        nc = self.nc
        ops = self.ops
        for o in ops:
            nd = set()
            for d in o.deps:
                p = ops[d]
                if p.eng == o.eng and not p.dma:
                    if o.eng == "tensor":
                        continue
                    if o.eng == "sync":
                        continue
                    if not self.same_engine_sync:
                        continue
                nd.add(d)
            o.deps = nd
            for d in nd:
                ops[d].sig = True
        for i in self.out_dmas:
            ops[i].sig = True
        eng_sem = {e: stack.enter_context(nc.semaphore("s_" + e)) for e in ENGS}
        pools = {e: [stack.enter_context(nc.semaphore("d_%s%d" % (e, i))) for i in range(NPOOL)]
                 for e in ("sync", "gpsimd", "scalar")}
        cnt = {e: 0 for e in ENGS}
        pool_cnt = {e: [0] * NPOOL for e in pools}
        pool_rr = {e: 0 for e in pools}
        pool_last = {e: [None] * NPOOL for e in pools}
        for o in ops:
            if o.dma:
                e = o.eng
                j = pool_rr[e]
                pool_rr[e] = (j + 1) % NPOOL
                o.prev_same_sem = pool_last[e][j]
                pool_cnt[e][j] += 16
                o.sem = pools[e][j]
                o.val = pool_cnt[e][j]
                pool_last[e][j] = o.idx
            elif o.sig:
                cnt[o.eng] += 1
                o.sem = eng_sem[o.eng]
                o.val = cnt[o.eng]
        by_eng = {e: [o for o in ops if o.eng == e] for e in ENGS}
        self.stats = {e: len(v) for e, v in by_eng.items()}
        block = stack.enter_context(nc.Block())

        def make(ename):
            lst = by_eng[ename]

            def body(eng):
                waited = {}
                nwait = 0
                for o in lst:
                    need = {}
                    for d in o.deps:
                        p = ops[d]
                        key = id(p.sem)
                        if waited.get(key, 0) >= p.val:
                            continue
                        if key not in need or need[key][1] < p.val:
                            need[key] = (p.sem, p.val)
                    if o.dma and o.prev_same_sem is not None:
                        p = ops[o.prev_same_sem]
                        key = id(p.sem)
                        if waited.get(key, 0) < p.val and (key not in need or need[key][1] < p.val):
                            need[key] = (p.sem, p.val)
                    for key, (s, v) in need.items():
                        eng.wait_ge(s, v)
                        waited[key] = v
                        nwait += 1
                    inst = o.fn(eng)
                    if o.dma:
                        inst.then_inc(o.sem, 16)
                    elif o.sig:
                        inst.then_inc(o.sem, 1)
                if ename == "sync":
                    for i in self.out_dmas:
                        p = ops[i]
                        if waited.get(id(p.sem), 0) < p.val:
                            eng.wait_ge(p.sem, p.val)
                            waited[id(p.sem)] = p.val
                self.stats[ename + "_waits"] = nwait
            return body

        block.tensor(make("tensor"))
        block.vector(make("vector"))
        block.scalar(make("scalar"))
        block.gpsimd(make("gpsimd"))
        block.sync(make("sync"))
from contextlib import ExitStack

D = 2048
HID = 5632
EPS = 1e-6


class Ctx:
    def __init__(self, nc, P):
        self.nc = nc
        self.P = P
        self.ident = None
        self.uid = 0

    def name(self, s):
        self.uid += 1
        return "%s_%d" % (s, self.uid)


def alloc_fns(nc, st, cx):
    sb = lambda name, shape, dt: st.enter_context(nc.sbuf_tensor(cx.name(name), shape, dt))
    ps = lambda name, shape, dt: st.enter_context(nc.psum_tensor(cx.name(name), shape, dt))
    return sb, ps


def make_ident(cx, st):
    nc, P = cx.nc, cx.P
    sb, ps = alloc_fns(nc, st, cx)
    ident = sb("ident", [128, 128], BF16)
    P.add("gpsimd", lambda e: e.memset(ident[:], 1.0), [], ["ident"])
    P.add("gpsimd", lambda e: e.affine_select(out=ident[:], in_=ident[:], pattern=[[-1, 128]],
                                              compare_op=ALU.is_equal, fill=0.0, base=0,
                                              channel_multiplier=1), ["ident"], ["ident"])
    cx.ident = ident
    return ident


def norm_transpose_tile(cx, pre, xt_ap, xkey, gt, uT, uT_key, t0, bufs, i):
    P = cx.P
    ss, rs, sq, u, ptb = bufs["ss"], bufs["rs"], bufs["sq"], bufs["u"], bufs["ptb"]
    k = lambda s: pre + s
    P.add("scalar", lambda e: e.activation(out=sq[:], in_=xt_ap, func=AF.Square, accum_out=ss[:]),
          [xkey], [k("u"), k("ss")])
    P.add("vector", lambda e: e.tensor_scalar(rs[:], ss[:], 1.0 / D, EPS, op0=ALU.mult, op1=ALU.add),
          [k("ss")], [k("rs")])
    P.add("scalar", lambda e: e.sqrt(rs[:], rs[:]), [k("rs")], [k("rs")])
    P.add("vector", lambda e: e.reciprocal(rs[:], rs[:]), [k("rs")], [k("rs")])
    P.add("vector", lambda e: e.scalar_tensor_tensor(out=u[:], in0=xt_ap, scalar=rs[:, 0:1], in1=gt[:],
                                                     op0=ALU.mult, op1=ALU.mult),
          [xkey, k("rs"), k("gt")], [k("u")])
    for g4 in range(D // 512):
        pb = ptb[(i * 4 + g4) % 2]
        pk = k("ptb%d" % ((i * 4 + g4) % 2))
        for j in range(4):
            dc = g4 * 4 + j
            P.add("tensor", lambda e, pb=pb, j=j, dc=dc: e.transpose(pb[:, j, :], u[:, dc * 128:(dc + 1) * 128],
                                                                     cx.ident[:]),
                  [k("u"), "ident"], [pk])
        P.add("scalar", lambda e, pb=pb, g4=g4: e.copy(out=uT[:, g4 * 4:(g4 + 1) * 4, t0:t0 + 128], in_=pb[:, :, :]),
              [pk], [uT_key(g4)])


def ffn_phase(cx, h_in, h_out, g_ap, w_in, w_out, NT, TB=1024):
    nc, P = cx.nc, cx.P
    pre = cx.name("ffn") + "_"
    k = lambda s: pre + s
    NH = HID // 128
    NDC = D // 128
    NSW = 256
    with ExitStack() as st:
        sb, ps = alloc_fns(nc, st, cx)
        gt = sb("gt", [128, D], F32)
        xt = [sb("xt", [128, D], F32) for _ in range(1)]
        ubuf = sb("u", [128, D], BF16)
        bufs = dict(ss=sb("ss", [128, 1], F32), rs=sb("rs", [128, 1], F32), sq=ubuf,
                    u=ubuf,
                    ptb=[ps("ptb", [128, 4, 128], BF16) for _ in range(2)])
        uT = sb("uT", [128, NDC, TB], BF16)
        hid = sb("hid", [128, NH, TB], BF16)
        wg = [sb("wg", [128, NDC, 128], BF16) for _ in range(2)]
        wu = [sb("wu", [128, NDC, 128], BF16) for _ in range(2)]
        sgt = [sb("sgt", [128, 512], BF16) for _ in range(2)]
        wo = [sb("wo", [128, NH, NSW], BF16) for _ in range(2)]
        res = [sb("res", [128, NSW], F32) for _ in range(2)]
        ot = [sb("ot", [128, NSW], F32) for _ in range(2)]
        psg = [ps("psg", [128, 512], F32) for _ in range(2)]
        psu = [ps("psu", [128, 512], F32) for _ in range(2)]
        pso = [ps("pso", [128, NSW], F32) for _ in range(2)]

        P.dma("sync", gt[:], g_ap.partition_broadcast(128), [], [k("gt")])
        w_in_v = w_in.rearrange("(dc p) n -> p dc n", p=128)
        w_out_v = w_out.rearrange("(kc p) n -> p kc n", p=128)
        cnt = 0
        ccnt = 0
        for tb in range(NT // TB):
            tok0 = tb * TB
            for tt in range(TB // 128):
                s = 0
                P.dma("sync", xt[s][:], h_in[tok0 + tt * 128: tok0 + (tt + 1) * 128, :], [], [k("xt%d" % s)])
                norm_transpose_tile(cx, pre, xt[s][:], k("xt%d" % s), gt, uT,
                                    lambda g4, tt=tt: k("uT_%d_%d" % (tt, g4)), tt * 128, bufs, tt)
            for hc in range(NH):
                s = hc % 2
                P.dma("gpsimd", wg[s][:], w_in_v[:, :, hc * 128:(hc + 1) * 128], [], [k("wg%d" % s)])
                P.dma("gpsimd", wu[s][:], w_in_v[:, :, HID + hc * 128: HID + (hc + 1) * 128], [], [k("wu%d" % s)])
                for tq in range(TB // 512):
                    b = cnt % 2
                    cnt += 1
                    for dc in range(NDC):
                        rk = [k("uT_%d_%d" % (tq * 4 + q, dc // 4)) for q in range(4)]
                        P.mm(psg[b][:], wg[s][:, dc, :], uT[:, dc, tq * 512:(tq + 1) * 512], dc == 0, dc == NDC - 1,
                             rk + [k("wg%d" % s)], [k("psg%d" % b)])
                    for dc in range(NDC):
                        rk = [k("uT_%d_%d" % (tq * 4 + q, dc // 4)) for q in range(4)]
                        P.mm(psu[b][:], wu[s][:, dc, :], uT[:, dc, tq * 512:(tq + 1) * 512], dc == 0, dc == NDC - 1,
                             rk + [k("wu%d" % s)], [k("psu%d" % b)])
                    P.add("scalar", lambda e, b=b: e.activation(out=sgt[b][:], in_=psg[b][:], func=AF.Silu),
                          [k("psg%d" % b)], [k("sgt%d" % b)])
                    P.add("vector", lambda e, b=b, hc=hc, tq=tq: e.tensor_tensor(
                        out=hid[:, hc, tq * 512:(tq + 1) * 512], in0=psu[b][:], in1=sgt[b][:], op=ALU.mult),
                          [k("psu%d" % b), k("sgt%d" % b)], [k("hid_%d_%d" % (hc, tq))])
            for ns in range(D // NSW):
                s = ns % 2
                P.dma("gpsimd", wo[s][:], w_out_v[:, :, ns * NSW:(ns + 1) * NSW], [], [k("wo%d" % s)])
                for tt in range(TB // 128):
                    b = ccnt % 2
                    ccnt += 1
                    r0 = tok0 + tt * 128
                    P.dma("sync", res[b][:], h_in[r0:r0 + 128, ns * NSW:(ns + 1) * NSW], [], [k("res%d" % b)])
                    for kc in range(NH):
                        P.mm(pso[b][:], hid[:, kc, tt * 128:(tt + 1) * 128], wo[s][:, kc, :], kc == 0, kc == NH - 1,
                             [k("hid_%d_%d" % (kc, tt // 4)), k("wo%d" % s)], [k("pso%d" % b)])
                    P.add("vector", lambda e, b=b: e.tensor_tensor(out=ot[b][:], in0=pso[b][:], in1=res[b][:], op=ALU.add),
                          [k("pso%d" % b), k("res%d" % b)], [k("ot%d" % b)])
                    P.dma("sync", h_out[r0:r0 + 128, ns * NSW:(ns + 1) * NSW], ot[b][:], [k("ot%d" % b)], [],
                          is_out=False)
    P.barrier()


def final_norm_phase(cx, h_in, out, g_ap, NT):
    nc, P = cx.nc, cx.P
    pre = cx.name("fn") + "_"
    k = lambda s: pre + s
    with ExitStack() as st:
        sb, ps = alloc_fns(nc, st, cx)
        gt = sb("gt", [128, D], F32)
        xt = [sb("xt", [128, D], F32) for _ in range(2)]
        yt = [sb("yt", [128, D], F32) for _ in range(2)]
        sq = sb("sq", [128, D], BF16)
        ss = [sb("ss", [128, 1], F32) for _ in range(2)]
        P.dma("sync", gt[:], g_ap.partition_broadcast(128), [], [k("gt")])
        for tt in range(NT // 128):
            s = tt % 2
            P.dma("sync", xt[s][:], h_in[tt * 128:(tt + 1) * 128, :], [], [k("xt%d" % s)])
            P.add("scalar", lambda e, s=s: e.activation(out=sq[:], in_=xt[s][:], func=AF.Square, accum_out=ss[s][:]),
                  [k("xt%d" % s)], [k("sq"), k("ss%d" % s)])
            P.add("vector", lambda e, s=s: e.tensor_scalar(ss[s][:], ss[s][:], 1.0 / D, EPS, op0=ALU.mult, op1=ALU.add),
                  [k("ss%d" % s)], [k("ss%d" % s)])
            P.add("scalar", lambda e, s=s: e.sqrt(ss[s][:], ss[s][:]), [k("ss%d" % s)], [k("ss%d" % s)])
            P.add("vector", lambda e, s=s: e.reciprocal(ss[s][:], ss[s][:]), [k("ss%d" % s)], [k("ss%d" % s)])
            P.add("vector", lambda e, s=s: e.scalar_tensor_tensor(out=yt[s][:], in0=xt[s][:], scalar=ss[s][:, 0:1],
                                                                  in1=gt[:], op0=ALU.mult, op1=ALU.mult),
                  [k("xt%d" % s), k("ss%d" % s), k("gt")], [k("yt%d" % s)])
            P.dma("sync", out[tt * 128:(tt + 1) * 128, :], yt[s][:], [k("yt%d" % s)], [], is_out=True)
    P.barrier()


NHEAD = 16
HD = 128
BLK = 256
SCALE = HD ** -0.5
BIG = 1e30


def t5_lo_bounds():
    n = np.arange(0, 1024)
    max_exact = 16
    nf = np.maximum(n, 1).astype(np.float32)
    large = max_exact + (np.log(nf / np.float32(max_exact)) / np.float32(np.log(128 / 16))
                         * np.float32(32 - max_exact)).astype(np.int32)
    large = np.minimum(large, 31)
    bucket = np.where(n < max_exact, n, large)
    lo = [int(np.argmax(bucket >= b)) for b in range(32)]
    return lo


def qkv_phase(cx, h_in, g_ap, w_qkv, QT, KT, V, NT, TB=1024):
    nc, P = cx.nc, cx.P
    pre = cx.name("qkv") + "_"
    k = lambda s: pre + s
    NDC = D // 128
    with ExitStack() as st:
        sb, ps = alloc_fns(nc, st, cx)
        gt = sb("gt", [128, D], F32)
        xt = sb("xt", [128, D], F32)
        ubuf = sb("u", [128, D], BF16)
        bufs = dict(ss=sb("ss", [128, 1], F32), rs=sb("rs", [128, 1], F32), sq=ubuf, u=ubuf,
                    ptb=[ps("ptb", [128, 4, 128], BF16) for _ in range(2)])
        uT = sb("uT", [128, NDC, TB], BF16)
        wq = [sb("wq", [128, NDC, 128], BF16) for _ in range(2)]
        qk_sb = [sb("qk", [128, TB], BF16) for _ in range(2)]
        wv = [sb("wv", [128, NDC, 512], BF16) for _ in range(2)]
        v_sb = [sb("vsb", [128, 512], BF16) for _ in range(2)]
        psq = [ps("psq", [128, 512], F32) for _ in range(2)]
        psv = [ps("psv", [128, 512], F32) for _ in range(2)]
        P.dma("sync", gt[:], g_ap.partition_broadcast(128), [], [k("gt")])
        w_v = w_qkv.rearrange("(dc p) n -> p dc n", p=128)
        cnt = 0
        vc = 0
        for tb in range(NT // TB):
            tok0 = tb * TB
            for tt in range(TB // 128):
                P.dma("sync", xt[:], h_in[tok0 + tt * 128: tok0 + (tt + 1) * 128, :], [], [k("xt")])
                norm_transpose_tile(cx, pre, xt[:], k("xt"), gt, uT,
                                    lambda g4, tt=tt: k("uT_%d_%d" % (tt, g4)), tt * 128, bufs, tt)
            for fc in range(32):
                s = fc % 2
                P.dma("gpsimd", wq[s][:], w_v[:, :, fc * 128:(fc + 1) * 128], [], [k("wq%d" % s)])
                for tq in range(TB // 512):
                    b = cnt % 2
                    cnt += 1
                    for dc in range(NDC):
                        rk = [k("uT_%d_%d" % (tq * 4 + q, dc // 4)) for q in range(4)]
                        P.mm(psq[b][:], wq[s][:, dc, :], uT[:, dc, tq * 512:(tq + 1) * 512], dc == 0, dc == NDC - 1,
                             rk + [k("wq%d" % s)], [k("psq%d" % b)])
                    if b == 0:
                        P.add("scalar", lambda e, b=b, s=s, tq=tq: e.copy(out=qk_sb[s][:, tq * 512:(tq + 1) * 512],
                                                                          in_=psq[b][:]),
                              [k("psq%d" % b)], [k("qk%d_%d" % (s, tq))])
                    else:
                        P.add("vector", lambda e, b=b, s=s, tq=tq: e.tensor_copy(qk_sb[s][:, tq * 512:(tq + 1) * 512],
                                                                                 psq[b][:]),
                              [k("psq%d" % b)], [k("qk%d_%d" % (s, tq))])
                dst = QT if fc < 16 else KT
                P.dma("sync", dst[fc % 16, :, tok0:tok0 + TB], qk_sb[s][:],
                      [k("qk%d_%d" % (s, tq)) for tq in range(TB // 512)], [])
            for nq in range(4):
                s = nq % 2
                P.dma("gpsimd", wv[s][:], w_v[:, :, 4096 + nq * 512: 4096 + (nq + 1) * 512], [], [k("wv%d" % s)])
                for tt in range(TB // 128):
                    b = vc % 2
                    vc += 1
                    for dc in range(NDC):
                        P.mm(psv[b][:], uT[:, dc, tt * 128:(tt + 1) * 128], wv[s][:, dc, :], dc == 0, dc == NDC - 1,
                             [k("uT_%d_%d" % (tt, dc // 4)), k("wv%d" % s)], [k("psv%d" % b)])
                    if b == 0:
                        P.add("scalar", lambda e, b=b: e.copy(out=v_sb[b][:], in_=psv[b][:]),
                              [k("psv%d" % b)], [k("vsb%d" % b)])
                    else:
                        P.add("vector", lambda e, b=b: e.tensor_copy(v_sb[b][:], psv[b][:]),
                              [k("psv%d" % b)], [k("vsb%d" % b)])
                    r0 = tok0 + tt * 128
                    P.dma("sync", V[r0:r0 + 128, nq * 512:(nq + 1) * 512], v_sb[b][:], [k("vsb%d" % b)], [])
    P.barrier()


def attn_phase(cx, QT, KT, V, OT, rel_bias, NT):
    nc, P = cx.nc, cx.P
    pre = cx.name("att") + "_"
    k = lambda s: pre + s
    NB = NT // BLK
    NQT = NT // 128
    lo = t5_lo_bounds()
    with ExitStack() as st:
        sb, ps = alloc_fns(nc, st, cx)
        Wt = sb("Wt", [128, NHEAD, 640], F32)
        dl_i = sb("dl_i", [128, 640], mybir.dt.int32)
        delta = sb("delta", [128, 640], F32)
        stepm = [sb("stepm", [128, 640], F32) for _ in range(2)]
        rbb = sb("rbb", [128, 32, NHEAD], F32)
        dif = sb("dif", [128, 32, NHEAD], F32)
        qT = [sb("qT", [128, NT], BF16) for _ in range(2)]
        kT = [sb("kT", [128, NT], BF16) for _ in range(2)]
        vh = [sb("vh", [128, NT // 128, 128], BF16) for _ in range(2)]
        OTh = [sb("OTh", [128, NT], BF16) for _ in range(2)]
        kmean = sb("kmean", [128, 16], F32)
        kmean_bf = sb("kmean_bf", [128, 16], BF16)
        gm = [sb("gm", [128, 16], F32) for _ in range(2)]
        top8 = [sb("top8", [128, 8], F32) for _ in range(2)]
        selb = [sb("selb", [128, 16], F32) for _ in range(2)]
        biasq = [sb("biasq", [128, 16], F32) for _ in range(2)]
        sbw = [sb("sbw", [128, 256], F32) for _ in range(2)]
        mx = [sb("mx", [128, 1], F32) for _ in range(2)]
        negm = [sb("negm", [128, 1], F32) for _ in range(2)]
        NPJ = 3
        Pj = [sb("Pj", [128, 256], BF16) for _ in range(NPJ)]
        pT = [sb("pT", [128, 2, 128], BF16) for _ in range(NPJ)]
        rsp = [sb("rsp", [128, 16], F32) for _ in range(2)]
        rsum = [sb("rsum", [128, 1], F32) for _ in range(2)]
        o_sb = [sb("o_sb", [128, 128], BF16) for _ in range(2)]
        gate_ps = ps("gate_ps", [128, 16], F32)
        s_ps = [ps("s_ps", [128, 256], F32) for _ in range(2)]
        ptb = [ps("ptb", [128, 2, 128], BF16) for _ in range(2)]
        pso = [ps("pso", [128, 128], F32) for _ in range(2)]
        otp = ps("otp", [128, 128], BF16)

        P.dma("sync", rbb[:], rel_bias.rearrange("b h -> (b h)").partition_broadcast(128), [], [k("rbb")])
        P.add("gpsimd", lambda e: e.iota(dl_i[:], pattern=[[-1, 640]], base=384, channel_multiplier=1), [], [k("dl_i")])
        P.add("vector", lambda e: e.tensor_copy(delta[:], dl_i[:]), [k("dl_i")], [k("delta")])
        P.add("vector", lambda e: e.tensor_tensor(out=dif[:, 1:32, :], in0=rbb[:, 1:32, :], in1=rbb[:, 0:31, :],
                                                  op=ALU.subtract), [k("rbb")], [k("dif")])
        P.add("vector", lambda e: e.tensor_scalar(stepm[0][:], delta[:], 0.0, -BIG, op0=ALU.is_lt, op1=ALU.mult),
              [k("delta")], [k("stepm0")])
        for h in range(NHEAD):
            eng = "vector" if h % 2 == 0 else "gpsimd"
            P.add(eng, lambda e, h=h: e.tensor_scalar(Wt[:, h, :], stepm[0][:], rbb[:, 0, h:h + 1], None, op0=ALU.add),
                  [k("stepm0"), k("rbb")], [k("Wt%d" % h)])
        for b in range(1, 32):
            sm = stepm[b % 2]
            smk = k("stepm%d" % (b % 2))
            P.add("vector", lambda e, sm=sm, b=b: e.tensor_scalar(sm[:], delta[:], float(lo[b]), None, op0=ALU.is_ge),
                  [k("delta")], [smk])
            for h in range(NHEAD):
                eng = "vector"
                P.add(eng, lambda e, sm=sm, b=b, h=h: e.scalar_tensor_tensor(
                    out=Wt[:, h, :], in0=sm[:], scalar=dif[:, b, h:h + 1], in1=Wt[:, h, :], op0=ALU.mult, op1=ALU.add),
                      [smk, k("dif"), k("Wt%d" % h)], [k("Wt%d" % h)])

        pj_i = 0
        sps_i = 0
        for h in range(NHEAD):
            s = h % 2
            P.dma("sync", qT[s][:], QT[h, :, :], [], [k("qT%d" % s)])
            P.dma("sync", kT[s][:], KT[h, :, :], [], [k("kT%d" % s)])
            P.dma("sync", vh[s][:], V[:, h * 128:(h + 1) * 128].rearrange("(t p) d -> p t d", p=128), [], [k("vh%d" % s)])
            P.add("vector", lambda e, s=s: e.reduce_sum(out=kmean[:, 0:NB],
                                                        in_=kT[s][:].rearrange("p (b c) -> p b c", c=BLK), axis=AX.X),
                  [k("kT%d" % s)], [k("kmean")])
            P.add("vector", lambda e: e.tensor_scalar(kmean_bf[:, 0:NB], kmean[:, 0:NB], 1.0 / BLK, None, op0=ALU.mult),
                  [k("kmean")], [k("kmean_bf")])
            if NB < 16:
                P.add("vector", lambda e: e.memset(kmean_bf[:, NB:16], 0.0), [], [k("kmean_bf")])
            for qi in range(NQT):
                n = qi // 2
                v = qi % 2
                r = qi % 2
                qs = slice(qi * 128, (qi + 1) * 128)
                own_off = 384 if v == 0 else 256
                prev_off = 128 if v == 0 else 0
                rd_q = [k("qT%d" % s)]
                rd_k = [k("kT%d" % s)]
                if n >= 1:
                    P.mm(gate_ps[:, :], qT[s][:, qs], kmean_bf[:, :], True, True, rd_q + [k("kmean_bf")], [k("gate_ps")])
                    P.add("gpsimd", lambda e, r=r: e.memset(gm[r][:], -BIG), [], [k("gm%d" % r)])
                    P.add("vector", lambda e, r=r, n=n: e.tensor_copy(gm[r][:, 0:n], gate_ps[:, 0:n]),
                          [k("gate_ps")], [k("gm%d" % r)])
                    P.add("vector", lambda e, r=r: e.max(out=top8[r][:], in_=gm[r][:]), [k("gm%d" % r)], [k("top8%d" % r)])
                    P.add("vector", lambda e, r=r: e.tensor_scalar(selb[r][:], gm[r][:], top8[r][:, 2:3], BIG,
                                                                   op0=ALU.is_ge, op1=ALU.mult),
                          [k("gm%d" % r), k("top8%d" % r)], [k("selb%d" % r)])
                b = sps_i % 2
                sps_i += 1
                P.mm(s_ps[b][:], qT[s][:, qs], kT[s][:, n * BLK:(n + 1) * BLK], True, True, rd_q + rd_k, [k("s_ps%d" % b)])
                P.add("vector", lambda e, b=b, h=h, own_off=own_off: e.scalar_tensor_tensor(
                    out=sbw[0][:], in0=s_ps[b][:], scalar=SCALE, in1=Wt[:, h, own_off:own_off + 256],
                    op0=ALU.mult, op1=ALU.add), [k("s_ps%d" % b), k("Wt%d" % h)], [k("sbw0")])
                P.add("vector", lambda e, r=r: e.reduce_max(out=mx[r][:], in_=sbw[0][:], axis=AX.X),
                      [k("sbw0")], [k("mx%d" % r)])
                P.add("vector", lambda e, r=r: e.tensor_scalar(negm[r][:], mx[r][:], -1.0, None, op0=ALU.mult),
                      [k("mx%d" % r)], [k("negm%d" % r)])
                if n >= 1:
                    P.add("vector", lambda e, r=r: e.tensor_scalar(biasq[r][:], selb[r][:], -BIG, negm[r][:, 0:1],
                                                                   op0=ALU.add, op1=ALU.add),
                          [k("selb%d" % r), k("negm%d" % r)], [k("biasq%d" % r)])
                if n >= 2:
                    P.add("vector", lambda e, r=r, n=n, h=h: e.tensor_scalar(
                        biasq[r][:, 0:n - 1], biasq[r][:, 0:n - 1], rbb[:, 31, h:h + 1], None, op0=ALU.add),
                          [k("biasq%d" % r), k("rbb")], [k("biasq%d" % r)])
                order = [n] + ([n - 1] if n >= 1 else []) + list(range(n - 2, -1, -1))
                for idx, j in enumerate(order):
                    pi = pj_i % NPJ
                    pj_i += 1
                    if j == n:
                        P.add("scalar", lambda e, pi=pi, r=r, j=j: e.activation(
                            out=Pj[pi][:], in_=sbw[0][:], func=AF.Exp, bias=negm[r][:, 0:1], accum_out=rsp[r][:, j:j + 1]),
                              [k("sbw0"), k("negm%d" % r)], [k("Pj%d" % pi), k("rsp%d_%d" % (r, j))])
                    else:
                        b = sps_i % 2
                        sps_i += 1
                        P.mm(s_ps[b][:], qT[s][:, qs], kT[s][:, j * BLK:(j + 1) * BLK], True, True, rd_q + rd_k,
                             [k("s_ps%d" % b)])
                        if j == n - 1:
                            P.add("vector", lambda e, b=b, h=h, prev_off=prev_off: e.scalar_tensor_tensor(
                                out=sbw[1][:], in0=s_ps[b][:], scalar=SCALE, in1=Wt[:, h, prev_off:prev_off + 256],
                                op0=ALU.mult, op1=ALU.add), [k("s_ps%d" % b), k("Wt%d" % h)], [k("sbw1")])
                            P.add("scalar", lambda e, pi=pi, r=r, j=j: e.activation(
                                out=Pj[pi][:], in_=sbw[1][:], func=AF.Exp, bias=biasq[r][:, j:j + 1],
                                accum_out=rsp[r][:, j:j + 1]),
                                  [k("sbw1"), k("biasq%d" % r)], [k("Pj%d" % pi), k("rsp%d_%d" % (r, j))])
                        else:
                            P.add("scalar", lambda e, pi=pi, r=r, j=j, b=b: e.activation(
                                out=Pj[pi][:], in_=s_ps[b][:], func=AF.Exp, bias=biasq[r][:, j:j + 1], scale=SCALE,
                                accum_out=rsp[r][:, j:j + 1]),
                                  [k("s_ps%d" % b), k("biasq%d" % r)], [k("Pj%d" % pi), k("rsp%d_%d" % (r, j))])
                    tb_ = pi % 2
                    for t in range(2):
                        P.add("tensor", lambda e, pi=pi, t=t, tb_=tb_: e.transpose(
                            ptb[tb_][:, t, :], Pj[pi][:, t * 128:(t + 1) * 128], cx.ident[:]),
                              [k("Pj%d" % pi), "ident"], [k("ptb%d" % tb_)])
                    if pi % 2 == 0:
                        P.add("vector", lambda e, pi=pi, tb_=tb_: e.tensor_copy(pT[pi][:], ptb[tb_][:]),
                              [k("ptb%d" % tb_)], [k("pT%d" % pi)])
                    else:
                        P.add("scalar", lambda e, pi=pi, tb_=tb_: e.copy(out=pT[pi][:], in_=ptb[tb_][:]),
                              [k("ptb%d" % tb_)], [k("pT%d" % pi)])
                    for t in range(2):
                        P.mm(pso[r][:], pT[pi][:, t, :], vh[s][:, j * 2 + t, :], idx == 0 and t == 0,
                             idx == len(order) - 1 and t == 1, [k("pT%d" % pi), k("vh%d" % s)], [k("pso%d" % r)])
                P.add("vector", lambda e, r=r, n=n: e.reduce_sum(out=rsum[r][:], in_=rsp[r][:, 0:n + 1], axis=AX.X),
                      [k("rsp%d_%d" % (r, j)) for j in range(n + 1)], [k("rsum%d" % r)])
                P.add("vector", lambda e, r=r: e.reciprocal(rsum[r][:], rsum[r][:]), [k("rsum%d" % r)], [k("rsum%d" % r)])
                P.add("vector", lambda e, r=r: e.tensor_scalar(o_sb[r][:], pso[r][:], rsum[r][:, 0:1], None, op0=ALU.mult),
                      [k("pso%d" % r), k("rsum%d" % r)], [k("o_sb%d" % r)])
                P.add("tensor", lambda e, r=r: e.transpose(otp[:], o_sb[r][:], cx.ident[:]),
                      [k("o_sb%d" % r), "ident"], [k("otp")])
                P.add("scalar", lambda e, s=s, qs=qs: e.copy(out=OTh[s][:, qs], in_=otp[:]),
                      [k("otp")], [k("OTh%d" % s)])
            P.dma("sync", OT[h, :, :], OTh[s][:], [k("OTh%d" % s)], [])
    P.barrier()


def wo_phase(cx, OT, h_in, h_out, w_o, NT, TB=1024):
    nc, P = cx.nc, cx.P
    pre = cx.name("wo") + "_"
    k = lambda s: pre + s
    NSW = 256
    with ExitStack() as st:
        sb, ps = alloc_fns(nc, st, cx)
        oT = sb("oT", [128, NHEAD, TB], BF16)
        wo = [sb("wo", [128, NHEAD, NSW], BF16) for _ in range(2)]
        res = [sb("res", [128, NSW], F32) for _ in range(2)]
        ot = [sb("ot", [128, NSW], F32) for _ in range(2)]
        pso = [ps("pso", [128, NSW], F32) for _ in range(2)]
        w_v = w_o.rearrange("(kc p) n -> p kc n", p=128)
        cc = 0
        for tb in range(NT // TB):
            tok0 = tb * TB
            for hh in range(NHEAD):
                P.dma("sync", oT[:, hh, :], OT[hh, :, tok0:tok0 + TB], [], [k("oT_%d" % hh)])
            for ns in range(D // NSW):
                s = ns % 2
                P.dma("gpsimd", wo[s][:], w_v[:, :, ns * NSW:(ns + 1) * NSW], [], [k("wo%d" % s)])
                for tt in range(TB // 128):
                    b = cc % 2
                    cc += 1
                    r0 = tok0 + tt * 128
                    P.dma("sync", res[b][:], h_in[r0:r0 + 128, ns * NSW:(ns + 1) * NSW], [], [k("res%d" % b)])
                    for kc in range(NHEAD):
                        P.mm(pso[b][:], oT[:, kc, tt * 128:(tt + 1) * 128], wo[s][:, kc, :], kc == 0, kc == NHEAD - 1,
                             [k("oT_%d" % kc), k("wo%d" % s)], [k("pso%d" % b)])
                    P.add("vector", lambda e, b=b: e.tensor_tensor(out=ot[b][:], in0=pso[b][:], in1=res[b][:], op=ALU.add),
                          [k("pso%d" % b), k("res%d" % b)], [k("ot%d" % b)])
                    P.dma("sync", h_out[r0:r0 + 128, ns * NSW:(ns + 1) * NSW], ot[b][:], [k("ot%d" % b)], [])
    P.barrier()


NG = 128
NP = 64
GH = 16
TCH = 8
GELU_C = 1.5957691216057308


def s5_prep(cx, a_re, a_im, log_step, b_re, b_im, c_re, c_im, d_skip, Toep_d, Bmat_d, Cre_d, Cim_d, A8_d):
    nc, P = cx.nc, cx.P
    pre = cx.name("s5p") + "_"
    k = lambda s: pre + s
    GB = 16
    with ExitStack() as st:
        sb, ps = alloc_fns(nc, st, cx)
        identf = sb("identf", [128, 128], F32)
        maskT = sb("maskT", [128, 8, 16], F32)
        aTr = sb("aTr", [NP, NG], F32)
        aTi = sb("aTi", [NP, NG], F32)
        dtb = sb("dtb", [NP, NG], F32)
        lre = sb("lre", [NP, NG], F32)
        lim = sb("lim", [NP, NG], F32)
        PHr = sb("PHr", [NP, 9, NG], F32)
        PHi = sb("PHi", [NP, 9, NG], F32)
        Wr = sb("Wr", [NP, 9, NG], F32)
        Wi = sb("Wi", [NP, 9, NG], F32)
        WRr = sb("WRr", [NP, 8, NG], F32)
        WRi = sb("WRi", [NP, 8, NG], F32)
        WNr = sb("WNr", [NP, 8, NG], F32)
        WNi = sb("WNi", [NP, 8, NG], F32)
        mg = sb("mg", [NP, NG], F32)
        t1 = sb("t1", [NP, NG], F32)
        t2 = sb("t2", [NP, NG], F32)
        cfr = sb("cfr", [NP, NG], F32)
        cfi = sb("cfi", [NP, NG], F32)
        bR = sb("bR", [NP, NG, GH], F32)
        bI = sb("bI", [NP, NG, GH], F32)
        bbr = sb("bbr", [NP, NG, GH], F32)
        bbi = sb("bbi", [NP, NG, GH], F32)
        cN = sb("cN", [128, 16, NP], F32)
        cTr = sb("cTr", [NP, NG, GH], F32)
        cTi = sb("cTi", [NP, NG, GH], F32)
        big1 = sb("big1", [NP, GB, 9, GH], F32)
        big2 = sb("big2", [NP, GB, 9, GH], F32)
        Bmr = sb("Bmr", [NP, GB, 8, GH], F32)
        Bmi = sb("Bmi", [NP, GB, 8, GH], F32)
        Bnr = sb("Bnr", [NP, GB, 8, GH], F32)
        BniN = sb("BniN", [NP, GB, 8, GH], F32)
        CPr = sb("CPr", [NP, GB, 9, GH], F32)
        CPi = sb("CPi", [NP, GB, 9, GH], F32)
        Bst = [sb("Bst", [128, GB, 128], BF16) for _ in range(2)]
        Tst = [sb("Tst", [128, GB, 128], BF16) for _ in range(2)]
        Crst = [sb("Crst", [NP, GB, 8, GH], BF16) for _ in range(2)]
        Cist = [sb("Cist", [NP, GB, 8, GH], BF16) for _ in range(2)]
        Dcols = sb("Dcols", [128, NG], F32)
        tmpT = [sb("tmpT", [128, 128], F32) for _ in range(2)]
        A8s = sb("A8s", [128, 2, 64], F32)
        ctp = [ps("ctp", [NP, 128], F32) for _ in range(2)]
        btp = [ps("btp", [128, 128], F32) for _ in range(2)]
        tpp = [ps("tpp", [128, 128], F32) for _ in range(2)]

        V = "vector"
        G_ = "gpsimd"
        P.add(G_, lambda e: e.memset(identf[:], 1.0), [], [k("identf")])
        P.add(G_, lambda e: e.affine_select(out=identf[:], in_=identf[:], pattern=[[-1, 128]], compare_op=ALU.is_equal,
                                            fill=0.0, base=0, channel_multiplier=1), [k("identf")], [k("identf")])
        P.add(G_, lambda e: e.memset(maskT[:], 1.0), [], [k("maskT")])
        P.add(G_, lambda e: e.affine_select(out=maskT[:], in_=maskT[:], pattern=[[16, 8], [0, 16]], compare_op=ALU.is_ge,
                                            fill=0.0, base=15, channel_multiplier=-1), [k("maskT")], [k("maskT")])
        P.dma("sync", aTr[:], a_re.rearrange("g p -> p g"), [], [k("aTr")], allow_slow_non_contiguous=True)
        P.dma("sync", aTi[:], a_im.rearrange("g p -> p g"), [], [k("aTi")], allow_slow_non_contiguous=True)
        P.dma("sync", dtb[:], log_step.partition_broadcast(NP), [], [k("dtb")])
        P.dma("sync", bR[:], b_re.rearrange("g p h -> p g h"), [], [k("bR")])
        P.dma("sync", bI[:], b_im.rearrange("g p h -> p g h"), [], [k("bI")])
        for s in range(8):
            P.dma("sync", Dcols[s * 16:(s + 1) * 16, :], d_skip.rearrange("(g h) -> h g", h=GH), [], [k("Dcols")],
                  allow_slow_non_contiguous=True)
        for (csrc, cT, nm) in ((c_re, cTr, "cTr"), (c_im, cTi, "cTi")):
            P.dma("sync", cN[:], csrc.rearrange("(gc g8) h p -> (g8 h) gc p", g8=8), [], [k("cN")])
            for gc in range(16):
                b = gc % 2
                P.add("tensor", lambda e, b=b, gc=gc: e.transpose(ctp[b][:], cN[:, gc, :], identf[:]),
                      [k("cN"), k("identf")], [k("ctp%d" % b)])
                P.add("scalar", lambda e, b=b, gc=gc, cT=cT: e.copy(
                    out=cT[:, gc * 8:(gc + 1) * 8, :], in_=ctp[b][:].rearrange("p (g h) -> p g h", h=GH)),
                      [k("ctp%d" % b)], [k(nm)])
        P.add("scalar", lambda e: e.activation(out=dtb[:], in_=dtb[:], func=AF.Exp), [k("dtb")], [k("dtb")])
        P.add(V, lambda e: e.tensor_tensor(out=lre[:], in0=aTr[:], in1=dtb[:], op=ALU.mult), [k("aTr"), k("dtb")], [k("lre")])
        P.add(V, lambda e: e.tensor_tensor(out=lim[:], in0=aTi[:], in1=dtb[:], op=ALU.mult), [k("aTi"), k("dtb")], [k("lim")])
        cc, ss_ = PHr[:, 1, :], PHi[:, 1, :]
        P.add(V, lambda e: e.tensor_scalar(t1[:], lim[:], -0.125, float(np.pi / 2), op0=ALU.mult, op1=ALU.add),
              [k("lim")], [k("t1")])
        P.add("scalar", lambda e: e.activation(out=cc, in_=t1[:], func=AF.Sin), [k("t1")], [k("PH")])
        P.add("scalar", lambda e: e.activation(out=ss_, in_=lim[:], func=AF.Sin, scale=0.125), [k("lim")], [k("PH")])
        for _ in range(3):
            P.add(V, lambda e: e.tensor_tensor(out=t1[:], in0=cc, in1=cc, op=ALU.mult), [k("PH")], [k("t1")])
            P.add(V, lambda e: e.tensor_tensor(out=t2[:], in0=ss_, in1=ss_, op=ALU.mult), [k("PH")], [k("t2")])
            P.add(V, lambda e: e.scalar_tensor_tensor(out=ss_, in0=cc, scalar=2.0, in1=ss_, op0=ALU.mult, op1=ALU.mult),
                  [k("PH")], [k("PH")])
            P.add(V, lambda e: e.tensor_tensor(out=cc, in0=t1[:], in1=t2[:], op=ALU.subtract), [k("t1"), k("t2")], [k("PH")])
        P.add(V, lambda e: e.memset(PHr[:, 0, :], 1.0), [], [k("PH")])
        P.add(V, lambda e: e.memset(PHi[:, 0, :], 0.0), [], [k("PH")])
        for kk in range(2, 9):
            ar, ai = PHr[:, kk - 1, :], PHi[:, kk - 1, :]
            orr, oi = PHr[:, kk, :], PHi[:, kk, :]
            P.add(V, lambda e, ar=ar: e.tensor_tensor(out=t1[:], in0=ar, in1=cc, op=ALU.mult), [k("PH")], [k("t1")])
            P.add(V, lambda e, ai=ai: e.tensor_tensor(out=t2[:], in0=ai, in1=ss_, op=ALU.mult), [k("PH")], [k("t2")])
            P.add(V, lambda e, orr=orr: e.tensor_tensor(out=orr, in0=t1[:], in1=t2[:], op=ALU.subtract),
                  [k("t1"), k("t2")], [k("PH")])
            P.add(V, lambda e, ar=ar: e.tensor_tensor(out=t1[:], in0=ar, in1=ss_, op=ALU.mult), [k("PH")], [k("t1")])
            P.add(V, lambda e, ai=ai: e.tensor_tensor(out=t2[:], in0=ai, in1=cc, op=ALU.mult), [k("PH")], [k("t2")])
            P.add(V, lambda e, oi=oi: e.tensor_tensor(out=oi, in0=t1[:], in1=t2[:], op=ALU.add),
                  [k("t1"), k("t2")], [k("PH")])
        for kk in range(9):
            P.add("scalar", lambda e, kk=kk: e.activation(out=mg[:], in_=lre[:], func=AF.Exp, scale=float(kk)),
                  [k("lre")], [k("mg")])
            P.add(V, lambda e, kk=kk: e.tensor_tensor(out=Wr[:, kk, :], in0=PHr[:, kk, :], in1=mg[:], op=ALU.mult),
                  [k("PH"), k("mg")], [k("W")])
            P.add(V, lambda e, kk=kk: e.tensor_tensor(out=Wi[:, kk, :], in0=PHi[:, kk, :], in1=mg[:], op=ALU.mult),
                  [k("PH"), k("mg")], [k("W")])
        for kk in range(8):
            P.add("scalar", lambda e, kk=kk: e.activation(out=mg[:], in_=lre[:], func=AF.Exp, scale=float(-kk)),
                  [k("lre")], [k("mg")])
            P.add(V, lambda e, kk=kk: e.tensor_tensor(out=WNr[:, kk, :], in0=PHr[:, kk, :], in1=mg[:], op=ALU.mult),
                  [k("PH"), k("mg")], [k("WN")])
            P.add(V, lambda e, kk=kk: e.scalar_tensor_tensor(out=WNi[:, kk, :], in0=PHi[:, kk, :], scalar=-1.0, in1=mg[:],
                                                             op0=ALU.mult, op1=ALU.mult),
                  [k("PH"), k("mg")], [k("WN")])
            P.add(G_, lambda e, kk=kk: e.tensor_copy(WRr[:, kk, :], Wr[:, 7 - kk, :]), [k("W")], [k("WR")])
            P.add(G_, lambda e, kk=kk: e.tensor_copy(WRi[:, kk, :], Wi[:, 7 - kk, :]), [k("W")], [k("WR")])
        ar, ai = Wr[:, 1, :], Wi[:, 1, :]
        P.add(V, lambda e: e.tensor_scalar(t1[:], ar, -1.0, None, op0=ALU.add), [k("W")], [k("t1")])
        P.add(V, lambda e: e.tensor_tensor(out=cfr[:], in0=t1[:], in1=aTr[:], op=ALU.mult), [k("t1"), k("aTr")], [k("cfr")])
        P.add(V, lambda e: e.tensor_tensor(out=t2[:], in0=ai, in1=aTi[:], op=ALU.mult), [k("W"), k("aTi")], [k("t2")])
        P.add(V, lambda e: e.tensor_tensor(out=cfr[:], in0=cfr[:], in1=t2[:], op=ALU.add), [k("cfr"), k("t2")], [k("cfr")])
        P.add(V, lambda e: e.tensor_tensor(out=cfi[:], in0=ai, in1=aTr[:], op=ALU.mult), [k("W"), k("aTr")], [k("cfi")])
        P.add(V, lambda e: e.tensor_tensor(out=t2[:], in0=t1[:], in1=aTi[:], op=ALU.mult), [k("t1"), k("aTi")], [k("t2")])
        P.add(V, lambda e: e.tensor_tensor(out=cfi[:], in0=cfi[:], in1=t2[:], op=ALU.subtract), [k("cfi"), k("t2")], [k("cfi")])
        P.add(V, lambda e: e.tensor_tensor(out=t1[:], in0=aTr[:], in1=aTr[:], op=ALU.mult), [k("aTr")], [k("t1")])
        P.add(V, lambda e: e.tensor_tensor(out=t2[:], in0=aTi[:], in1=aTi[:], op=ALU.mult), [k("aTi")], [k("t2")])
        P.add(V, lambda e: e.tensor_tensor(out=t1[:], in0=t1[:], in1=t2[:], op=ALU.add), [k("t1"), k("t2")], [k("t1")])
        P.add(V, lambda e: e.reciprocal(t1[:], t1[:]), [k("t1")], [k("t1")])
        P.add(V, lambda e: e.tensor_tensor(out=cfr[:], in0=cfr[:], in1=t1[:], op=ALU.mult), [k("cfr"), k("t1")], [k("cfr")])
        P.add(V, lambda e: e.tensor_tensor(out=cfi[:], in0=cfi[:], in1=t1[:], op=ALU.mult), [k("cfi"), k("t1")], [k("cfi")])
        bc3 = lambda t: t[:, :].unsqueeze(2).to_broadcast([NP, NG, GH])
        P.add(V, lambda e: e.tensor_tensor(out=bbr[:], in0=bR[:], in1=bc3(cfr), op=ALU.mult), [k("bR"), k("cfr")], [k("bbr")])
        P.add(V, lambda e: e.tensor_tensor(out=bbi[:], in0=bI[:], in1=bc3(cfi), op=ALU.mult), [k("bI"), k("cfi")], [k("bbi")])
        P.add(V, lambda e: e.tensor_tensor(out=bbr[:], in0=bbr[:], in1=bbi[:], op=ALU.subtract), [k("bbr"), k("bbi")], [k("bbr")])
        P.add(V, lambda e: e.tensor_tensor(out=bbi[:], in0=bI[:], in1=bc3(cfr), op=ALU.mult), [k("bI"), k("cfr"), k("bbr")], [k("bbi")])
        P.add(V, lambda e: e.tensor_tensor(out=bR[:], in0=bR[:], in1=bc3(cfi), op=ALU.mult), [k("bR"), k("cfi")], [k("bR")])
        P.add(V, lambda e: e.tensor_tensor(out=bbi[:], in0=bbi[:], in1=bR[:], op=ALU.add), [k("bbi"), k("bR")], [k("bbi")])

        for ri, Wsrc in ((0, Wr), (1, Wi)):
            P.add("scalar", lambda e, ri=ri, Wsrc=Wsrc: e.copy(out=A8s[0:64, ri, :], in_=Wsrc[:, 8, 0:64]), [k("W")], [k("A8s")])
            P.add("scalar", lambda e, ri=ri, Wsrc=Wsrc: e.copy(out=A8s[64:128, ri, :], in_=Wsrc[:, 8, 64:128]), [k("W")], [k("A8s")])
        P.dma("sync", A8_d.rearrange("r p g -> p r g"), A8s[:], [k("A8s")], [])

        def cmul_b(out_r, out_i, Wre, Wim, nk, xr, xi, g0, neg_im=False, eng=V):
            wb = lambda W: W[:, 0:nk, g0:g0 + GB].rearrange("p k g -> p g k").unsqueeze(3).to_broadcast([NP, GB, nk, GH])
            xb = lambda X: X[:, g0:g0 + GB, :].unsqueeze(2).to_broadcast([NP, GB, nk, GH])
            b1 = big1[:, :, 0:nk, :]
            b2 = big2[:, :, 0:nk, :]
            rd = [k("W"), k("WN"), k("WR"), k("bbr"), k("bbi"), k("cTr"), k("cTi")]
            P.add(eng, lambda e: e.tensor_tensor(out=b1, in0=wb(Wre), in1=xb(xr), op=ALU.mult), rd, [k("big1")])
            P.add(eng, lambda e: e.tensor_tensor(out=b2, in0=wb(Wim), in1=xb(xi), op=ALU.mult), rd, [k("big2")])
            P.add(eng, lambda e: e.tensor_tensor(out=out_r, in0=b1, in1=b2, op=ALU.subtract), [k("big1"), k("big2")], [k("batch")])
            P.add(eng, lambda e: e.tensor_tensor(out=b1, in0=wb(Wre), in1=xb(xi), op=ALU.mult), rd + [k("batch")], [k("big1")])
            P.add(eng, lambda e: e.tensor_tensor(out=b2, in0=wb(Wim), in1=xb(xr), op=ALU.mult), rd + [k("batch")], [k("big2")])
            if neg_im:
                P.add(eng, lambda e: e.scalar_tensor_tensor(out=out_i, in0=b1, scalar=-1.0, in1=b2, op0=ALU.mult, op1=ALU.subtract),
                      [k("big1"), k("big2")], [k("batch")])
            else:
                P.add(eng, lambda e: e.tensor_tensor(out=out_i, in0=b1, in1=b2, op=ALU.add), [k("big1"), k("big2")], [k("batch")])

        for bi in range(NG // GB):
            g0 = bi * GB
            sl = bi % 2
            cmul_b(Bmr[:], Bmi[:], WRr, WRi, 8, bbr, bbi, g0)
            cmul_b(Bnr[:], BniN[:], WNr, WNi, 8, bbr, bbi, g0, neg_im=True)
            cmul_b(CPr[:], CPi[:], Wr, Wi, 9, cTr, cTi, g0)
            P.add("scalar", lambda e, sl=sl: e.copy(out=Crst[sl][:], in_=CPr[:, :, 1:9, :]), [k("batch")], [k("Crst%d" % sl)])
            P.add("scalar", lambda e, sl=sl: e.mul(Cist[sl][:], CPi[:, :, 1:9, :], -1.0), [k("batch")], [k("Cist%d" % sl)])
            for gb in range(GB):
                g = g0 + gb
                b = g % 2
                P.add("tensor", lambda e, b=b, gb=gb: e.transpose(btp[b][:, 0:64], Bmr[:, gb, :, :].rearrange("p s h -> p (s h)"),
                                                                  identf[0:64, 0:64]), [k("batch"), k("identf")], [k("btp%d" % b)])
                P.add("tensor", lambda e, b=b, gb=gb: e.transpose(btp[b][:, 64:128], Bmi[:, gb, :, :].rearrange("p s h -> p (s h)"),
                                                                  identf[0:64, 0:64]), [k("batch"), k("identf")], [k("btp%d" % b)])
                P.add("scalar", lambda e, b=b, gb=gb, sl=sl: e.copy(out=Bst[sl][:, gb, :], in_=btp[b][:]),
                      [k("btp%d" % b)], [k("Bst%d" % sl)])
                P.mm(tpp[b][:], Bnr[:, gb, :, :].rearrange("p s h -> p (s h)"),
                     CPr[:, gb, 0:8, :].rearrange("p s h -> p (s h)"), True, False, [k("batch")], [k("tpp%d" % b)])
                P.mm(tpp[b][:], BniN[:, gb, :, :].rearrange("p s h -> p (s h)"),
                     CPi[:, gb, 0:8, :].rearrange("p s h -> p (s h)"), False, True, [k("batch")], [k("tpp%d" % b)])
                P.add(V, lambda e, b=b: e.tensor_tensor(out=tmpT[b][:], in0=tpp[b][:], in1=maskT[:].rearrange("p s h -> p (s h)"),
                                                        op=ALU.mult), [k("tpp%d" % b), k("maskT")], [k("tmpT%d" % b)])
                P.add(V, lambda e, b=b, g=g, gb=gb, sl=sl: e.scalar_tensor_tensor(
                    out=Tst[sl][:, gb, :], in0=identf[:], scalar=Dcols[:, g:g + 1], in1=tmpT[b][:], op0=ALU.mult, op1=ALU.add),
                      [k("tmpT%d" % b), k("identf"), k("Dcols")], [k("Tst%d" % sl)])
            P.dma("sync", Bmat_d[g0:g0 + GB].rearrange("g k m -> k g m"), Bst[sl][:], [k("Bst%d" % sl)], [])
            P.dma("sync", Toep_d[g0:g0 + GB].rearrange("g k m -> k g m"), Tst[sl][:], [k("Tst%d" % sl)], [])
            P.dma("sync", Cre_d[g0:g0 + GB].rearrange("g p m -> p g m"), Crst[sl][:].rearrange("p g s h -> p g (s h)"),
                  [k("Crst%d" % sl)], [])
            P.dma("sync", Cim_d[g0:g0 + GB].rearrange("g p m -> p g m"), Cist[sl][:].rearrange("p g s h -> p g (s h)"),
                  [k("Cist%d" % sl)], [])
    P.barrier()


def s5_main(cx, x_in, g_ap, Toep_d, Bmat_d, Cre_d, Cim_d, A8_d, zT_d, NT, SEG=512):
    nc, P = cx.nc, cx.P
    pre = cx.name("s5m") + "_"
    k = lambda s: pre + s
    NDC = D // 128
    NC = SEG // TCH
    MB = 8
    with ExitStack() as st:
        sb, ps = alloc_fns(nc, st, cx)
        Sel = sb("Sel", [128, 64, 128], BF16)
        gt = sb("gt", [128, D], F32)
        xt = sb("xt", [128, D], F32)
        ubuf = sb("u", [128, D], BF16)
        bufs = dict(ss=sb("ss", [128, 1], F32), rs=sb("rs", [128, 1], F32), sq=ubuf, u=ubuf,
                    ptb=[ps("ptb", [128, 4, 128], BF16) for _ in range(2)])
        uT = sb("uT", [128, NDC, SEG], BF16)
        zT = sb("zT", [128, NDC, SEG], BF16)
        Uall = sb("Uall", [128, NG, NC], BF16)
        Lz = sb("Lz", [128, NC, 2, 64], F32)
        hist = sb("hist", [128, 2, 64, NC], BF16)
        Z = sb("Z", [128, 2, 64], F32)
        AA = sb("AA", [128, 2, 64], F32)
        BB = sb("BB", [128, 2, 64], F32)
        A8s = sb("A8s", [128, 2, 64], F32)
        st1 = sb("st1", [128, 2, 64], F32)
        st2 = sb("st2", [128, 2, 64], F32)
        Tm = [sb("Tm", [128, MB, 128], BF16) for _ in range(2)]
        Bm = [sb("Bm", [128, MB, 128], BF16) for _ in range(2)]
        Cr = [sb("Cr", [128, MB, 128], BF16) for _ in range(2)]
        Ci = [sb("Ci", [128, MB, 128], BF16) for _ in range(2)]
        gl1 = [sb("gl1", [128, 4, NC], F32) for _ in range(2)]
        ps_u = [ps("ps_u", [128, 4, NC], F32) for _ in range(2)]
        ps_l = ps("ps_l", [128, 8, NC], F32)
        ps_y = [ps("ps_y", [128, 4, NC], F32) for _ in range(2)]
        ps_z = ps("ps_z", [128, 8, NC], F32)

        P.add("gpsimd", lambda e: e.memset(Sel[:], 0.0), [], [k("Sel")])
        for a in range(8):
            for b in range(8):
                P.add("gpsimd", lambda e, a=a, b=b: e.tensor_copy(Sel[:, a * 8 + b, 16 * b:16 * b + 16],
                                                                  cx.ident[:, 16 * a:16 * a + 16]),
                      ["ident", k("Sel")], [k("Sel")])
        P.dma("sync", gt[:], g_ap.partition_broadcast(128), [], [k("gt")])
        P.dma("sync", A8s[:], A8_d.rearrange("r p g -> p r g"), [], [k("A8s")])
        P.add("vector", lambda e: e.tensor_copy(AA[:, 0, :], A8s[:, 0, :]), [k("A8s")], [k("AA")])
        P.add("vector", lambda e: e.tensor_copy(AA[:, 1, :], A8s[:, 0, :]), [k("A8s")], [k("AA")])
        P.add("vector", lambda e: e.tensor_scalar(BB[:, 0, :], A8s[:, 1, :], -1.0, None, op0=ALU.mult), [k("A8s")], [k("BB")])
        P.add("vector", lambda e: e.tensor_copy(BB[:, 1, :], A8s[:, 1, :]), [k("A8s")], [k("BB")])
        P.add("vector", lambda e: e.memset(Z[:], 0.0), [], [k("Z")])

        mb_i = 0
        for seg in range(NT // SEG):
            tok0 = seg * SEG
            for tt in range(SEG // 128):
                P.dma("sync", xt[:], x_in[tok0 + tt * 128: tok0 + (tt + 1) * 128, :], [], [k("xt")])
                norm_transpose_tile(cx, pre, xt[:], k("xt"), gt, uT,
                                    lambda g4, tt=tt: k("uT_%d" % (g4)), tt * 128, bufs, tt)
            for gq in range(NG // 4):
                b = gq % 2
                for gi in range(4):
                    g = gq * 4 + gi
                    dc, g8 = g // 8, g % 8
                    src = uT[:, dc, :].rearrange("p (c s) -> p s c", s=TCH)
                    for s in range(TCH):
                        P.mm(ps_u[b][:, gi, :], Sel[:, g8 * 8 + s, :], src[:, s, :], s == 0, s == TCH - 1,
                             [k("Sel"), k("uT_%d" % (dc // 4))], [k("ps_u%d" % b)])
                if b == 0:
                    P.add("scalar", lambda e, b=b, gq=gq: e.copy(out=Uall[:, gq * 4:(gq + 1) * 4, :], in_=ps_u[b][:]),
                          [k("ps_u%d" % b)], [k("U_%d" % (gq // 2))])
                else:
                    P.add("vector", lambda e, b=b, gq=gq: e.tensor_copy(Uall[:, gq * 4:(gq + 1) * 4, :], ps_u[b][:]),
                          [k("ps_u%d" % b)], [k("U_%d" % (gq // 2))])
            for gb in range(NG // MB):
                sl = mb_i % 2
                mb_i += 1
                g0 = gb * MB
                half = g0 // 64
                P.dma("sync", Bm[sl][:], Bmat_d[g0:g0 + MB].rearrange("g k m -> k g m"), [], [k("Bm%d" % sl)])
                for gi in range(MB):
                    P.mm(ps_l[:, gi, :], Bm[sl][:, gi, :], Uall[:, g0 + gi, :], True, True,
                         [k("Bm%d" % sl), k("U_%d" % gb)], [k("ps_l")])
                gp0 = g0 - half * 64
                rows = slice(half * 64, half * 64 + 64)
                P.add("scalar", lambda e, rows=rows, gp0=gp0: e.copy(
                    out=Lz[rows, :, 0, gp0:gp0 + MB].rearrange("p c g -> p g c"), in_=ps_l[0:64, :, :]),
                      [k("ps_l")], [k("Lz")])
                P.add("vector", lambda e, rows=rows, gp0=gp0: e.tensor_copy(
                    Lz[rows, :, 1, gp0:gp0 + MB].rearrange("p c g -> p g c"), ps_l[64:128, :, :]),
                      [k("ps_l")], [k("Lz")])
            for c in range(NC):
                P.add("scalar", lambda e, c=c: e.copy(out=hist[:, :, :, c], in_=Z[:]), [k("Z")], [k("hist")])
                P.add("vector", lambda e: e.tensor_tensor(out=st1[:], in0=AA[:], in1=Z[:], op=ALU.mult), [k("AA"), k("Z")], [k("st1")])
                P.add("vector", lambda e: e.tensor_tensor(out=st2[:, 0, :], in0=BB[:, 0, :], in1=Z[:, 1, :], op=ALU.mult),
                      [k("BB"), k("Z")], [k("st2")])
                P.add("vector", lambda e: e.tensor_tensor(out=st2[:, 1, :], in0=BB[:, 1, :], in1=Z[:, 0, :], op=ALU.mult),
                      [k("BB"), k("Z")], [k("st2")])
                P.add("vector", lambda e: e.tensor_tensor(out=st1[:], in0=st1[:], in1=st2[:], op=ALU.add),
                      [k("st1"), k("st2")], [k("st1")])
                P.add("vector", lambda e, c=c: e.tensor_tensor(out=Z[:], in0=st1[:], in1=Lz[:, c, :, :], op=ALU.add),
                      [k("st1"), k("Lz")], [k("Z")])
            for gb in range(NG // MB):
                sl = mb_i % 2
                mb_i += 1
                g0 = gb * MB
                half = g0 // 64
                rows = slice(half * 64, half * 64 + 64)
                P.dma("sync", Tm[sl][:], Toep_d[g0:g0 + MB].rearrange("g k m -> k g m"), [], [k("Tm%d" % sl)])
                P.dma("sync", Cr[sl][rows, :, :], Cre_d[g0:g0 + MB].rearrange("g p m -> p g m"), [], [k("Cr%d" % sl)])
                P.dma("sync", Ci[sl][rows, :, :], Cim_d[g0:g0 + MB].rearrange("g p m -> p g m"), [], [k("Ci%d" % sl)])
                for q in range(MB // 4):
                    b = (gb * 2 + q) % 2
                    for gi in range(4):
                        gl = q * 4 + gi
                        g = g0 + gl
                        gp = g - half * 64
                        P.mm(ps_y[b][:, gi, :], Tm[sl][:, gl, :], Uall[:, g, :], True, False,
                             [k("Tm%d" % sl), k("U_%d" % gb)], [k("ps_y%d" % b)])
                        P.mm(ps_y[b][:, gi, :], Cr[sl][rows, gl, :], hist[rows, 0, gp, :], False, False,
                             [k("Cr%d" % sl), k("hist")], [k("ps_y%d" % b)])
                        P.mm(ps_y[b][:, gi, :], Ci[sl][rows, gl, :], hist[rows, 1, gp, :], False, True,
                             [k("Ci%d" % sl), k("hist")], [k("ps_y%d" % b)])
                    t = gl1[b]
                    P.add("scalar", lambda e, b=b, t=t: e.activation(out=t[:], in_=ps_y[b][:], func=AF.Square),
                          [k("ps_y%d" % b)], [k("gl%d" % b)])
                    P.add("vector", lambda e, t=t: e.tensor_scalar(t[:], t[:], 0.044715, 1.0, op0=ALU.mult, op1=ALU.add),
                          [k("gl%d" % b)], [k("gl%d" % b)])
                    P.add("vector", lambda e, b=b, t=t: e.tensor_tensor(out=t[:], in0=t[:], in1=ps_y[b][:], op=ALU.mult),
                          [k("gl%d" % b), k("ps_y%d" % b)], [k("gl%d" % b)])
                    P.add("scalar", lambda e, t=t: e.activation(out=t[:], in_=t[:], func=AF.Sigmoid, scale=GELU_C),
                          [k("gl%d" % b)], [k("gl%d" % b)])
                    gs = g0 + q * 4
                    P.add("vector", lambda e, b=b, t=t, gs=gs: e.tensor_tensor(out=Uall[:, gs:gs + 4, :], in0=t[:], in1=ps_y[b][:],
                                                                               op=ALU.mult),
                          [k("gl%d" % b), k("ps_y%d" % b)], [k("U_%d" % gb)])
            for dc in range(NDC):
                for s in range(TCH):
                    for g8 in range(8):
                        P.mm(ps_z[:, s, :], Sel[:, s * 8 + g8, :], Uall[:, dc * 8 + g8, :], g8 == 0, g8 == 7,
                             [k("Sel"), k("U_%d" % dc)], [k("ps_z")])
                if dc % 2 == 0:
                    P.add("scalar", lambda e, dc=dc: e.copy(out=zT[:, dc, :].rearrange("p (c s) -> p s c", s=TCH), in_=ps_z[:]),
                          [k("ps_z")], [k("zT")])
                else:
                    P.add("vector", lambda e, dc=dc: e.tensor_copy(zT[:, dc, :].rearrange("p (c s) -> p s c", s=TCH), ps_z[:]),
                          [k("ps_z")], [k("zT")])
            P.dma("sync", zT_d[:, :, tok0:tok0 + SEG].rearrange("dc p t -> p dc t"), zT[:], [k("zT")], [])
    P.barrier()


def glu_phase(cx, zT_d, x_in, h_out, w_glu, NT, TB=1024):
    nc, P = cx.nc, cx.P
    pre = cx.name("glu") + "_"
    k = lambda s: pre + s
    NSW = 256
    NDC = D // 128
    with ExitStack() as st:
        sb, ps = alloc_fns(nc, st, cx)
        zT = sb("zT", [128, NDC, TB], BF16)
        wv = [sb("wv", [128, NDC, NSW], BF16) for _ in range(2)]
        wg = [sb("wg", [128, NDC, NSW], BF16) for _ in range(2)]
        res = [sb("res", [128, NSW], F32) for _ in range(2)]
        sg = [sb("sg", [128, NSW], F32) for _ in range(2)]
        ot = [sb("ot", [128, NSW], F32) for _ in range(2)]
        psv = [ps("psv", [128, NSW], F32) for _ in range(2)]
        psg = [ps("psg", [128, NSW], F32) for _ in range(2)]
        w_v = w_glu.rearrange("(kc p) n -> p kc n", p=128)
        cc = 0
        for tb in range(NT // TB):
            tok0 = tb * TB
            P.dma("sync", zT[:], zT_d[:, :, tok0:tok0 + TB].rearrange("dc p t -> p dc t"), [], [k("zT")])
            for ns in range(D // NSW):
                s = ns % 2
                P.dma("gpsimd", wv[s][:], w_v[:, :, ns * NSW:(ns + 1) * NSW], [], [k("wv%d" % s)])
                P.dma("gpsimd", wg[s][:], w_v[:, :, D + ns * NSW:D + (ns + 1) * NSW], [], [k("wg%d" % s)])
                for tt in range(TB // 128):
                    b = cc % 2
                    cc += 1
                    r0 = tok0 + tt * 128
                    P.dma("sync", res[b][:], x_in[r0:r0 + 128, ns * NSW:(ns + 1) * NSW], [], [k("res%d" % b)])
                    for kc in range(NDC):
                        P.mm(psv[b][:], zT[:, kc, tt * 128:(tt + 1) * 128], wv[s][:, kc, :], kc == 0, kc == NDC - 1,
                             [k("zT"), k("wv%d" % s)], [k("psv%d" % b)])
                    for kc in range(NDC):
                        P.mm(psg[b][:], zT[:, kc, tt * 128:(tt + 1) * 128], wg[s][:, kc, :], kc == 0, kc == NDC - 1,
                             [k("zT"), k("wg%d" % s)], [k("psg%d" % b)])
                    P.add("scalar", lambda e, b=b: e.activation(out=sg[b][:], in_=psg[b][:], func=AF.Sigmoid),
                          [k("psg%d" % b)], [k("sg%d" % b)])
                    P.add("vector", lambda e, b=b: e.tensor_tensor(out=sg[b][:], in0=psv[b][:], in1=sg[b][:], op=ALU.mult),
                          [k("psv%d" % b), k("sg%d" % b)], [k("sg%d" % b)])
                    P.add("vector", lambda e, b=b: e.tensor_tensor(out=ot[b][:], in0=sg[b][:], in1=res[b][:], op=ALU.add),
                          [k("sg%d" % b), k("res%d" % b)], [k("ot%d" % b)])
                    P.dma("sync", h_out[r0:r0 + 128, ns * NSW:(ns + 1) * NSW], ot[b][:], [k("ot%d" % b)], [])
    P.barrier()


NCORES = 8
SEQ = 4096
BATCH = 4


def build_program(NT):
    nc = bass.Bass("TRN2", target_bir_lowering=False)
    inp = lambda n, s: nc.dram_tensor(n, s, F32, kind="ExternalInput").ap()
    x = inp("x", [NT, D])
    norm_mix_g = inp("norm_mix_g", [2, D])
    norm_ffn_g = inp("norm_ffn_g", [2, D])
    norm_final_g = inp("norm_final_g", [D])
    a_re = inp("s5_a_re", [NG, NP])
    a_im = inp("s5_a_im", [NG, NP])
    ls = inp("s5_log_step", [NG])
    b_re = inp("s5_b_re", [NG, NP, GH])
    b_im = inp("s5_b_im", [NG, NP, GH])
    c_re = inp("s5_c_re", [NG, GH, NP])
    c_im = inp("s5_c_im", [NG, GH, NP])
    s5_d = inp("s5_d", [D])
    w_glu = inp("s5_w_glu", [D, 2 * D])
    w_qkv = inp("attn_w_qkv", [D, 3 * D])
    w_o = inp("attn_w_o", [D, D])
    rel_bias = inp("rel_bias", [32, NHEAD])
    w_in = inp("ffn_w_in", [2, D, 2 * HID])
    w_out = inp("ffn_w_out", [2, HID, D])
    y = nc.dram_tensor("y", [NT, D], F32, kind="ExternalOutput").ap()
    it = lambda n, s, dt: nc.dram_tensor(n, s, dt, kind="Internal").ap()
    Toep = it("Toep", [NG, 128, 128], BF16)
    Bmat = it("Bmat", [NG, 128, 128], BF16)
    Cre = it("Cre", [NG, 64, 128], BF16)
    Cim = it("Cim", [NG, 64, 128], BF16)
    A8 = it("A8", [2, 128, 64], F32)
    zT_d = it("zT_d", [16, 128, NT], BF16)
    QT = it("QT", [NHEAD, 128, NT], BF16)
    KT = it("KT", [NHEAD, 128, NT], BF16)
    Vd = it("Vd", [NT, D], BF16)
    OT = it("OT", [NHEAD, 128, NT], BF16)
    h1 = it("h1", [NT, D], F32)
    h2 = it("h2", [NT, D], F32)
    h3 = it("h3", [NT, D], F32)
    h4 = it("h4", [NT, D], F32)
    with ExitStack() as st:
        P = Prog(nc)
        cx = Ctx(nc, P)
        make_ident(cx, st)
        s5_prep(cx, a_re, a_im, ls, b_re, b_im, c_re, c_im, s5_d, Toep, Bmat, Cre, Cim, A8)
        s5_main(cx, x, norm_mix_g[0], Toep, Bmat, Cre, Cim, A8, zT_d, NT)
        glu_phase(cx, zT_d, x, h1, w_glu, NT)
        ffn_phase(cx, h1, h2, norm_ffn_g[0], w_in[0], w_out[0], NT)
        qkv_phase(cx, h2, norm_mix_g[1], w_qkv, QT, KT, Vd, NT)
        attn_phase(cx, QT, KT, Vd, OT, rel_bias, NT)
        wo_phase(cx, OT, h2, h3, w_o, NT)
        ffn_phase(cx, h3, h4, norm_ffn_g[1], w_in[1], w_out[1], NT)
        final_norm_phase(cx, h4, y, norm_final_g, NT)
        P.emit(st)
    return nc


def kernel(**inputs):
    f = lambda a: np.ascontiguousarray(np.asarray(a, dtype=np.float32))
    x = f(inputs["x"])
    NT = SEQ
    shared = {
        "norm_mix_g": f(inputs["norm_mix_g"]), "norm_ffn_g": f(inputs["norm_ffn_g"]),
        "norm_final_g": f(inputs["norm_final_g"]),
        "s5_a_re": f(inputs["s5_a_re"][0]), "s5_a_im": f(inputs["s5_a_im"][0]),
        "s5_log_step": f(inputs["s5_log_step"][0]),
        "s5_b_re": f(inputs["s5_b_re"][0]), "s5_b_im": f(inputs["s5_b_im"][0]),
        "s5_c_re": f(inputs["s5_c_re"][0]), "s5_c_im": f(inputs["s5_c_im"][0]),
        "s5_d": f(inputs["s5_d"][0]), "s5_w_glu": f(inputs["s5_w_glu"][0]),
        "attn_w_qkv": f(inputs["attn_w_qkv"][0]), "attn_w_o": f(inputs["attn_w_o"][0]),
        "rel_bias": f(inputs["rel_bias"]),
        "ffn_w_in": f(inputs["ffn_w_in"]), "ffn_w_out": f(inputs["ffn_w_out"]),
    }
    nc = build_program(NT)
    in_maps = []
    for c in range(NCORES):
        m = dict(shared)
        m["x"] = x[c % BATCH]
        in_maps.append(m)
    res = run_bass_kernel_spmd(nc, in_maps, core_ids=list(range(NCORES)))
    out = np.stack([np.asarray(res.results[b]["y"]) for b in range(BATCH)], axis=0)
    return out.astype(np.float32)
```

```python
from contextlib import ExitStack
from concourse.bass_utils import run_bass_kernel_spmd
import numpy as np
import concourse.bass as bass
import concourse.mybir as mybir

F32 = mybir.dt.float32
BF16 = mybir.dt.bfloat16
AF = mybir.ActivationFunctionType
ALU = mybir.AluOpType
AX = mybir.AxisListType

ENGS = ["tensor", "vector", "scalar", "gpsimd", "sync"]
NPOOL = 12


class Op:
    __slots__ = ("idx", "eng", "fn", "deps", "sig", "sem", "val", "dma", "prev_same_sem")

    def __init__(self, idx, eng, fn, dma):
        self.idx = idx
        self.eng = eng
        self.fn = fn
        self.deps = set()
        self.sig = False
        self.sem = None
        self.val = 0
        self.dma = dma
        self.prev_same_sem = None


class Prog:
    def __init__(self, nc, same_engine_sync=True):
        self.nc = nc
        self.ops = []
        self.last_w = {}
        self.readers = {}
        self.same_engine_sync = same_engine_sync
        self.fence_for = {}
        self.since_barrier = []
        self.out_dmas = []

    def add(self, eng, fn, reads=(), writes=(), dma=False, out=False):
        idx = len(self.ops)
        op = Op(idx, eng, fn, dma)
        deps = op.deps
        for k in reads:
            w = self.last_w.get(k)
            if w is not None:
                deps.add(w)
        for k in writes:
            w = self.last_w.get(k)
            if w is not None:
                deps.add(w)
            for r in self.readers.get(k, ()):
                deps.add(r)
        for k in reads:
            self.readers.setdefault(k, []).append(idx)
        for k in writes:
            self.last_w[k] = idx
            self.readers[k] = []
        f = self.fence_for.pop(eng, None)
        if f:
            deps.update(f)
        deps.discard(idx)
        self.ops.append(op)
        self.since_barrier.append(idx)
        if out:
            self.out_dmas.append(idx)
        return idx

    def barrier(self):
        last = {}
        f = []
        for i in self.since_barrier:
            o = self.ops[i]
            if o.dma:
                f.append(i)
            else:
                last[o.eng] = i
        f.extend(last.values())
        self.fence_for = {e: list(f) for e in ENGS}
        self.since_barrier = []

    def mm(self, out, lhsT, rhs, start, stop, reads, writes, **kw):
        return self.add("tensor", lambda e: e.matmul(out, lhsT, rhs, start=start, stop=stop, **kw),
                        reads, writes)

    def dma(self, eng, out, in_, reads, writes, is_out=False, **kw):
        return self.add(eng, lambda e: e.dma_start(out=out, in_=in_, **kw), reads, writes,
                        dma=True, out=is_out)

    def emit(self, stack):
        nc = self.nc
        ops = self.ops
        for o in ops:
            nd = set()
            for d in o.deps:
                p = ops[d]
                if p.eng == o.eng and not p.dma:
                    if o.eng == "tensor":
                        continue
                    if o.eng == "sync":
                        continue
                    if not self.same_engine_sync:
                        continue
                nd.add(d)
            o.deps = nd
            for d in nd:
                ops[d].sig = True
        for i in self.out_dmas:
            ops[i].sig = True
        eng_sem = {e: stack.enter_context(nc.semaphore("s_" + e)) for e in ENGS}
        pools = {e: [stack.enter_context(nc.semaphore("d_%s%d" % (e, i))) for i in range(NPOOL)]
                 for e in ("sync", "gpsimd", "scalar")}
        cnt = {e: 0 for e in ENGS}
        pool_cnt = {e: [0] * NPOOL for e in pools}
        pool_rr = {e: 0 for e in pools}
        pool_last = {e: [None] * NPOOL for e in pools}
        for o in ops:
            if o.dma:
                e = o.eng
                j = pool_rr[e]
                pool_rr[e] = (j + 1) % NPOOL
                o.prev_same_sem = pool_last[e][j]
                pool_cnt[e][j] += 16
                o.sem = pools[e][j]
                o.val = pool_cnt[e][j]
                pool_last[e][j] = o.idx
            elif o.sig:
                cnt[o.eng] += 1
                o.sem = eng_sem[o.eng]
                o.val = cnt[o.eng]
        by_eng = {e: [o for o in ops if o.eng == e] for e in ENGS}
        self.stats = {e: len(v) for e, v in by_eng.items()}
        block = stack.enter_context(nc.Block())

        def make(ename):
            lst = by_eng[ename]

            def body(eng):
                waited = {}
                nwait = 0
                for o in lst:
                    need = {}
                    for d in o.deps:
                        p = ops[d]
                        key = id(p.sem)
                        if waited.get(key, 0) >= p.val:
                            continue
                        if key not in need or need[key][1] < p.val:
                            need[key] = (p.sem, p.val)
                    if o.dma and o.prev_same_sem is not None:
                        p = ops[o.prev_same_sem]
                        key = id(p.sem)
                        if waited.get(key, 0) < p.val and (key not in need or need[key][1] < p.val):
                            need[key] = (p.sem, p.val)
                    for key, (s, v) in need.items():
                        eng.wait_ge(s, v)
                        waited[key] = v
                        nwait += 1
                    inst = o.fn(eng)
                    if o.dma:
                        inst.then_inc(o.sem, 16)
                    elif o.sig:
                        inst.then_inc(o.sem, 1)
                if ename == "sync":
                    for i in self.out_dmas:
                        p = ops[i]
                        if waited.get(id(p.sem), 0) < p.val:
                            eng.wait_ge(p.sem, p.val)
                            waited[id(p.sem)] = p.val
                self.stats[ename + "_waits"] = nwait
            return body

        block.tensor(make("tensor"))
        block.vector(make("vector"))
        block.scalar(make("scalar"))
        block.gpsimd(make("gpsimd"))
        block.sync(make("sync"))
from contextlib import ExitStack

D = 2048
HID = 5632
EPS = 1e-6


class Ctx:
    def __init__(self, nc, P):
        self.nc = nc
        self.P = P
        self.ident = None
        self.uid = 0

    def name(self, s):
        self.uid += 1
        return "%s_%d" % (s, self.uid)


def alloc_fns(nc, st, cx):
    sb = lambda name, shape, dt: st.enter_context(nc.sbuf_tensor(cx.name(name), shape, dt))
    ps = lambda name, shape, dt: st.enter_context(nc.psum_tensor(cx.name(name), shape, dt))
    return sb, ps


def make_ident(cx, st):
    nc, P = cx.nc, cx.P
    sb, ps = alloc_fns(nc, st, cx)
    ident = sb("ident", [128, 128], BF16)
    P.add("gpsimd", lambda e: e.memset(ident[:], 1.0), [], ["ident"])
    P.add("gpsimd", lambda e: e.affine_select(out=ident[:], in_=ident[:], pattern=[[-1, 128]],
                                              compare_op=ALU.is_equal, fill=0.0, base=0,
                                              channel_multiplier=1), ["ident"], ["ident"])
    cx.ident = ident
    return ident


def norm_transpose_tile(cx, pre, xt_ap, xkey, gt, uT, uT_key, t0, bufs, i):
    P = cx.P
    ss, rs, sq, u, ptb = bufs["ss"], bufs["rs"], bufs["sq"], bufs["u"], bufs["ptb"]
    k = lambda s: pre + s
    P.add("scalar", lambda e: e.activation(out=sq[:], in_=xt_ap, func=AF.Square, accum_out=ss[:]),
          [xkey], [k("u"), k("ss")])
    P.add("vector", lambda e: e.tensor_scalar(rs[:], ss[:], 1.0 / D, EPS, op0=ALU.mult, op1=ALU.add),
          [k("ss")], [k("rs")])
    P.add("scalar", lambda e: e.sqrt(rs[:], rs[:]), [k("rs")], [k("rs")])
    P.add("vector", lambda e: e.reciprocal(rs[:], rs[:]), [k("rs")], [k("rs")])
    P.add("vector", lambda e: e.scalar_tensor_tensor(out=u[:], in0=xt_ap, scalar=rs[:, 0:1], in1=gt[:],
                                                     op0=ALU.mult, op1=ALU.mult),
          [xkey, k("rs"), k("gt")], [k("u")])
    for g4 in range(D // 512):
        pb = ptb[(i * 4 + g4) % 2]
        pk = k("ptb%d" % ((i * 4 + g4) % 2))
        for j in range(4):
            dc = g4 * 4 + j
            P.add("tensor", lambda e, pb=pb, j=j, dc=dc: e.transpose(pb[:, j, :], u[:, dc * 128:(dc + 1) * 128],
                                                                     cx.ident[:]),
                  [k("u"), "ident"], [pk])
        P.add("scalar", lambda e, pb=pb, g4=g4: e.copy(out=uT[:, g4 * 4:(g4 + 1) * 4, t0:t0 + 128], in_=pb[:, :, :]),
              [pk], [uT_key(g4)])


def ffn_phase(cx, h_in, h_out, g_ap, w_in, w_out, NT, TB=1024):
    nc, P = cx.nc, cx.P
    pre = cx.name("ffn") + "_"
    k = lambda s: pre + s
    NH = HID // 128
    NDC = D // 128
    NSW = 256
    with ExitStack() as st:
        sb, ps = alloc_fns(nc, st, cx)
        gt = sb("gt", [128, D], F32)
        xt = [sb("xt", [128, D], F32) for _ in range(1)]
        ubuf = sb("u", [128, D], BF16)
        bufs = dict(ss=sb("ss", [128, 1], F32), rs=sb("rs", [128, 1], F32), sq=ubuf,
                    u=ubuf,
                    ptb=[ps("ptb", [128, 4, 128], BF16) for _ in range(2)])
        uT = sb("uT", [128, NDC, TB], BF16)
        hid = sb("hid", [128, NH, TB], BF16)
        wg = [sb("wg", [128, NDC, 128], BF16) for _ in range(2)]
        wu = [sb("wu", [128, NDC, 128], BF16) for _ in range(2)]
        sgt = [sb("sgt", [128, 512], BF16) for _ in range(2)]
        wo = [sb("wo", [128, NH, NSW], BF16) for _ in range(2)]
        res = [sb("res", [128, NSW], F32) for _ in range(2)]
        ot = [sb("ot", [128, NSW], F32) for _ in range(2)]
        psg = [ps("psg", [128, 512], F32) for _ in range(2)]
        psu = [ps("psu", [128, 512], F32) for _ in range(2)]
        pso = [ps("pso", [128, NSW], F32) for _ in range(2)]

        P.dma("sync", gt[:], g_ap.partition_broadcast(128), [], [k("gt")])
        w_in_v = w_in.rearrange("(dc p) n -> p dc n", p=128)
        w_out_v = w_out.rearrange("(kc p) n -> p kc n", p=128)
        cnt = 0
        ccnt = 0
        for tb in range(NT // TB):
            tok0 = tb * TB
            for tt in range(TB // 128):
                s = 0
                P.dma("sync", xt[s][:], h_in[tok0 + tt * 128: tok0 + (tt + 1) * 128, :], [], [k("xt%d" % s)])
                norm_transpose_tile(cx, pre, xt[s][:], k("xt%d" % s), gt, uT,
                                    lambda g4, tt=tt: k("uT_%d_%d" % (tt, g4)), tt * 128, bufs, tt)
            for hc in range(NH):
                s = hc % 2
                P.dma("gpsimd", wg[s][:], w_in_v[:, :, hc * 128:(hc + 1) * 128], [], [k("wg%d" % s)])
                P.dma("gpsimd", wu[s][:], w_in_v[:, :, HID + hc * 128: HID + (hc + 1) * 128], [], [k("wu%d" % s)])
                for tq in range(TB // 512):
                    b = cnt % 2
                    cnt += 1
                    for dc in range(NDC):
                        rk = [k("uT_%d_%d" % (tq * 4 + q, dc // 4)) for q in range(4)]
                        P.mm(psg[b][:], wg[s][:, dc, :], uT[:, dc, tq * 512:(tq + 1) * 512], dc == 0, dc == NDC - 1,
                             rk + [k("wg%d" % s)], [k("psg%d" % b)])
                    for dc in range(NDC):
                        rk = [k("uT_%d_%d" % (tq * 4 + q, dc // 4)) for q in range(4)]
                        P.mm(psu[b][:], wu[s][:, dc, :], uT[:, dc, tq * 512:(tq + 1) * 512], dc == 0, dc == NDC - 1,
                             rk + [k("wu%d" % s)], [k("psu%d" % b)])
                    P.add("scalar", lambda e, b=b: e.activation(out=sgt[b][:], in_=psg[b][:], func=AF.Silu),
                          [k("psg%d" % b)], [k("sgt%d" % b)])
                    P.add("vector", lambda e, b=b, hc=hc, tq=tq: e.tensor_tensor(
                        out=hid[:, hc, tq * 512:(tq + 1) * 512], in0=psu[b][:], in1=sgt[b][:], op=ALU.mult),
                          [k("psu%d" % b), k("sgt%d" % b)], [k("hid_%d_%d" % (hc, tq))])
            for ns in range(D // NSW):
                s = ns % 2
                P.dma("gpsimd", wo[s][:], w_out_v[:, :, ns * NSW:(ns + 1) * NSW], [], [k("wo%d" % s)])
                for tt in range(TB // 128):
                    b = ccnt % 2
                    ccnt += 1
                    r0 = tok0 + tt * 128
                    P.dma("sync", res[b][:], h_in[r0:r0 + 128, ns * NSW:(ns + 1) * NSW], [], [k("res%d" % b)])
                    for kc in range(NH):
                        P.mm(pso[b][:], hid[:, kc, tt * 128:(tt + 1) * 128], wo[s][:, kc, :], kc == 0, kc == NH - 1,
                             [k("hid_%d_%d" % (kc, tt // 4)), k("wo%d" % s)], [k("pso%d" % b)])
                    P.add("vector", lambda e, b=b: e.tensor_tensor(out=ot[b][:], in0=pso[b][:], in1=res[b][:], op=ALU.add),
                          [k("pso%d" % b), k("res%d" % b)], [k("ot%d" % b)])
                    P.dma("sync", h_out[r0:r0 + 128, ns * NSW:(ns + 1) * NSW], ot[b][:], [k("ot%d" % b)], [],
                          is_out=False)
    P.barrier()


def final_norm_phase(cx, h_in, out, g_ap, NT):
    nc, P = cx.nc, cx.P
    pre = cx.name("fn") + "_"
    k = lambda s: pre + s
    with ExitStack() as st:
        sb, ps = alloc_fns(nc, st, cx)
        gt = sb("gt", [128, D], F32)
        xt = [sb("xt", [128, D], F32) for _ in range(2)]
        yt = [sb("yt", [128, D], F32) for _ in range(2)]
        sq = sb("sq", [128, D], BF16)
        ss = [sb("ss", [128, 1], F32) for _ in range(2)]
        P.dma("sync", gt[:], g_ap.partition_broadcast(128), [], [k("gt")])
        for tt in range(NT // 128):
            s = tt % 2
            P.dma("sync", xt[s][:], h_in[tt * 128:(tt + 1) * 128, :], [], [k("xt%d" % s)])
            P.add("scalar", lambda e, s=s: e.activation(out=sq[:], in_=xt[s][:], func=AF.Square, accum_out=ss[s][:]),
                  [k("xt%d" % s)], [k("sq"), k("ss%d" % s)])
            P.add("vector", lambda e, s=s: e.tensor_scalar(ss[s][:], ss[s][:], 1.0 / D, EPS, op0=ALU.mult, op1=ALU.add),
                  [k("ss%d" % s)], [k("ss%d" % s)])
            P.add("scalar", lambda e, s=s: e.sqrt(ss[s][:], ss[s][:]), [k("ss%d" % s)], [k("ss%d" % s)])
            P.add("vector", lambda e, s=s: e.reciprocal(ss[s][:], ss[s][:]), [k("ss%d" % s)], [k("ss%d" % s)])
            P.add("vector", lambda e, s=s: e.scalar_tensor_tensor(out=yt[s][:], in0=xt[s][:], scalar=ss[s][:, 0:1],
                                                                  in1=gt[:], op0=ALU.mult, op1=ALU.mult),
                  [k("xt%d" % s), k("ss%d" % s), k("gt")], [k("yt%d" % s)])
            P.dma("sync", out[tt * 128:(tt + 1) * 128, :], yt[s][:], [k("yt%d" % s)], [], is_out=True)
    P.barrier()


NHEAD = 16
HD = 128
BLK = 256
SCALE = HD ** -0.5
BIG = 1e30


def t5_lo_bounds():
    n = np.arange(0, 1024)
    max_exact = 16
    nf = np.maximum(n, 1).astype(np.float32)
    large = max_exact + (np.log(nf / np.float32(max_exact)) / np.float32(np.log(128 / 16))
                         * np.float32(32 - max_exact)).astype(np.int32)
    large = np.minimum(large, 31)
    bucket = np.where(n < max_exact, n, large)
    lo = [int(np.argmax(bucket >= b)) for b in range(32)]
    return lo


def qkv_phase(cx, h_in, g_ap, w_qkv, QT, KT, V, NT, TB=1024):
    nc, P = cx.nc, cx.P
    pre = cx.name("qkv") + "_"
    k = lambda s: pre + s
    NDC = D // 128
    with ExitStack() as st:
        sb, ps = alloc_fns(nc, st, cx)
        gt = sb("gt", [128, D], F32)
        xt = sb("xt", [128, D], F32)
        ubuf = sb("u", [128, D], BF16)
        bufs = dict(ss=sb("ss", [128, 1], F32), rs=sb("rs", [128, 1], F32), sq=ubuf, u=ubuf,
                    ptb=[ps("ptb", [128, 4, 128], BF16) for _ in range(2)])
        uT = sb("uT", [128, NDC, TB], BF16)
        wq = [sb("wq", [128, NDC, 128], BF16) for _ in range(2)]
        qk_sb = [sb("qk", [128, TB], BF16) for _ in range(2)]
        wv = [sb("wv", [128, NDC, 512], BF16) for _ in range(2)]
        v_sb = [sb("vsb", [128, 512], BF16) for _ in range(2)]
        psq = [ps("psq", [128, 512], F32) for _ in range(2)]
        psv = [ps("psv", [128, 512], F32) for _ in range(2)]
        P.dma("sync", gt[:], g_ap.partition_broadcast(128), [], [k("gt")])
        w_v = w_qkv.rearrange("(dc p) n -> p dc n", p=128)
        cnt = 0
        vc = 0
        for tb in range(NT // TB):
            tok0 = tb * TB
            for tt in range(TB // 128):
                P.dma("sync", xt[:], h_in[tok0 + tt * 128: tok0 + (tt + 1) * 128, :], [], [k("xt")])
                norm_transpose_tile(cx, pre, xt[:], k("xt"), gt, uT,
                                    lambda g4, tt=tt: k("uT_%d_%d" % (tt, g4)), tt * 128, bufs, tt)
            for fc in range(32):
                s = fc % 2
                P.dma("gpsimd", wq[s][:], w_v[:, :, fc * 128:(fc + 1) * 128], [], [k("wq%d" % s)])
                for tq in range(TB // 512):
                    b = cnt % 2
                    cnt += 1
                    for dc in range(NDC):
                        rk = [k("uT_%d_%d" % (tq * 4 + q, dc // 4)) for q in range(4)]
                        P.mm(psq[b][:], wq[s][:, dc, :], uT[:, dc, tq * 512:(tq + 1) * 512], dc == 0, dc == NDC - 1,
                             rk + [k("wq%d" % s)], [k("psq%d" % b)])
                    if b == 0:
                        P.add("scalar", lambda e, b=b, s=s, tq=tq: e.copy(out=qk_sb[s][:, tq * 512:(tq + 1) * 512],
                                                                          in_=psq[b][:]),
                              [k("psq%d" % b)], [k("qk%d_%d" % (s, tq))])
                    else:
                        P.add("vector", lambda e, b=b, s=s, tq=tq: e.tensor_copy(qk_sb[s][:, tq * 512:(tq + 1) * 512],
                                                                                 psq[b][:]),
                              [k("psq%d" % b)], [k("qk%d_%d" % (s, tq))])
                dst = QT if fc < 16 else KT
                P.dma("sync", dst[fc % 16, :, tok0:tok0 + TB], qk_sb[s][:],
                      [k("qk%d_%d" % (s, tq)) for tq in range(TB // 512)], [])
            for nq in range(4):
                s = nq % 2
                P.dma("gpsimd", wv[s][:], w_v[:, :, 4096 + nq * 512: 4096 + (nq + 1) * 512], [], [k("wv%d" % s)])
                for tt in range(TB // 128):
                    b = vc % 2
                    vc += 1
                    for dc in range(NDC):
                        P.mm(psv[b][:], uT[:, dc, tt * 128:(tt + 1) * 128], wv[s][:, dc, :], dc == 0, dc == NDC - 1,
                             [k("uT_%d_%d" % (tt, dc // 4)), k("wv%d" % s)], [k("psv%d" % b)])
                    if b == 0:
                        P.add("scalar", lambda e, b=b: e.copy(out=v_sb[b][:], in_=psv[b][:]),
                              [k("psv%d" % b)], [k("vsb%d" % b)])
                    else:
                        P.add("vector", lambda e, b=b: e.tensor_copy(v_sb[b][:], psv[b][:]),
                              [k("psv%d" % b)], [k("vsb%d" % b)])
                    r0 = tok0 + tt * 128
                    P.dma("sync", V[r0:r0 + 128, nq * 512:(nq + 1) * 512], v_sb[b][:], [k("vsb%d" % b)], [])
    P.barrier()


def attn_phase(cx, QT, KT, V, OT, rel_bias, NT, q_lo=0, pflag=None):
    nc, P = cx.nc, cx.P
    pre = cx.name("att") + "_"
    k = lambda s: pre + s
    NB = NT // BLK
    NQT = NT // 128
    lo = t5_lo_bounds()
    NQ = 4
    NSP = 3
    NPJ = 4
    with ExitStack() as st:
        sb, ps = alloc_fns(nc, st, cx)
        Wt = sb("Wt", [128, NHEAD, 640], F32)
        dl_i = sb("dl_i", [128, 640], mybir.dt.int32)
        delta = sb("delta", [128, 640], F32)
        stepm = [sb("stepm", [128, 640], F32) for _ in range(2)]
        rbb = sb("rbb", [128, 32, NHEAD], F32)
        dif = sb("dif", [128, 32, NHEAD], F32)
        qT = [sb("qT", [128, NT], BF16) for _ in range(2)]
        kT = [sb("kT", [128, NT], BF16) for _ in range(2)]
        vh = [sb("vh", [128, NT // 128, 128], BF16) for _ in range(2)]
        NTO = NT - q_lo * 128
        OTh = [sb("OTh", [128, NTO], BF16) for _ in range(2)]
        pf = sb("pf", [128, 8], F32)
        vb = [sb("vb", [128, 16], F32) for _ in range(4)]
        kmean = sb("kmean", [128, 16], F32)
        kmean_bf = [sb("kmean_bf", [128, 16], BF16) for _ in range(2)]
        gm = [sb("gm", [128, 16], F32) for _ in range(NQ)]
        top8 = [sb("top8", [128, 8], F32) for _ in range(NQ)]
        selb = [sb("selb", [128, 16], F32) for _ in range(NQ)]
        biasq = [sb("biasq", [128, 16], F32) for _ in range(NQ)]
        mx = [sb("mx", [128, 1], F32) for _ in range(NQ)]
        negm = [sb("negm", [128, 1], F32) for _ in range(NQ)]
        rsp = [sb("rsp", [128, 16], F32) for _ in range(NQ)]
        rsum = [sb("rsum", [128, 1], F32) for _ in range(NQ)]
        o_sb = [sb("o_sb", [128, 128], BF16) for _ in range(NQ)]
        sbw = [sb("sbw", [128, 256], F32) for _ in range(3)]
        Pj = [sb("Pj", [128, 256], BF16) for _ in range(NPJ)]
        pT = [sb("pT", [128, 2, 128], BF16) for _ in range(NPJ)]
        s_ps_t = [ps("s_ps", [128, 256], F32) for _ in range(3)]
        s_ps = [t_[:, :] for t_ in s_ps_t]
        ptb_t = [ps("ptb", [128, 2, 128], BF16) for _ in range(2)]
        ptb = [t_[:, :, :] for t_ in ptb_t]
        pso_t = [ps("pso", [128, 128], F32) for _ in range(2)]
        pso = [t_[:, :] for t_ in pso_t]
        misc = ps("misc", [128, 512], F32)
        gate_ps = misc[:, 0:16]
        otp = misc[:, 128:256].bitcast(BF16)[:, 0:128]
        if pflag is not None:
            P.dma("sync", pf[:], pflag[:, :], [], [k("pf")])
        P.dma("sync", rbb[:], rel_bias.rearrange("b h -> (b h)").partition_broadcast(128), [], [k("rbb")])
        P.add("gpsimd", lambda e: e.iota(dl_i[:], pattern=[[-1, 640]], base=384, channel_multiplier=1), [], [k("dl_i")])
        P.add("vector", lambda e: e.tensor_copy(delta[:], dl_i[:]), [k("dl_i")], [k("delta")])
        P.add("vector", lambda e: e.tensor_tensor(out=dif[:, 1:32, :], in0=rbb[:, 1:32, :], in1=rbb[:, 0:31, :],
                                                  op=ALU.subtract), [k("rbb")], [k("dif")])
        P.add("vector", lambda e: e.tensor_scalar(stepm[0][:], delta[:], 0.0, -BIG, op0=ALU.is_lt, op1=ALU.mult),
              [k("delta")], [k("stepm0")])
        for h in range(NHEAD):
            eng = "vector" if h % 2 == 0 else "gpsimd"
            P.add(eng, lambda e, h=h: e.tensor_scalar(Wt[:, h, :], stepm[0][:], rbb[:, 0, h:h + 1], None, op0=ALU.add),
                  [k("stepm0"), k("rbb")], [k("Wt%d" % h)])
        for b in range(1, 32):
            sm = stepm[b % 2]
            smk = k("stepm%d" % (b % 2))
            P.add("vector", lambda e, sm=sm, b=b: e.tensor_scalar(sm[:], delta[:], float(lo[b]), None, op0=ALU.is_ge),
                  [k("delta")], [smk])
            for h in range(NHEAD):
                P.add("vector", lambda e, sm=sm, b=b, h=h: e.scalar_tensor_tensor(
                    out=Wt[:, h, :], in0=sm[:], scalar=dif[:, b, h:h + 1], in1=Wt[:, h, :], op0=ALU.mult, op1=ALU.add),
                      [smk, k("dif"), k("Wt%d" % h)], [k("Wt%d" % h)])

        def head_body(h, s, gstep):
            P.dma("sync", qT[s][:], QT[h, :, :], [], [k("qT%d" % s)])
            P.dma("sync", kT[s][:], KT[h, :, :], [], [k("kT%d" % s)])
            P.dma("sync", vh[s][:], V[:, h * 128:(h + 1) * 128].rearrange("(t p) d -> p t d", p=128), [], [k("vh%d" % s)])
            P.add("vector", lambda e, s=s: e.reduce_sum(out=kmean[:, 0:NB],
                                                        in_=kT[s][:].rearrange("p (b c) -> p b c", c=BLK), axis=AX.X),
                  [k("kT%d" % s)], [k("kmean")])
            P.add("vector", lambda e, s=s: e.tensor_scalar(kmean_bf[s][:, 0:NB], kmean[:, 0:NB], 1.0 / BLK, None, op0=ALU.mult),
                  [k("kmean")], [k("kmean_bf%d" % s)])
            if NB < 16:
                P.add("vector", lambda e, s=s: e.memset(kmean_bf[s][:, NB:16], 0.0), [], [k("kmean_bf%d" % s)])
            rd_q = [k("qT%d" % s)]
            rd_k = [k("kT%d" % s)]
            steps = []
            for qi in range(q_lo, NQT):
                n = qi // 2
                order = [n] + ([n - 1] if n >= 1 else []) + list(range(n - 2, -1, -1))
                for idx, j in enumerate(order):
                    steps.append(dict(qi=qi, n=n, v=qi % 2, r=qi % NQ, r3=qi % 2, j=j, idx=idx, last=(idx == len(order) - 1),
                                      g=gstep))
                    gstep += 1

            def S1(sp):
                qi, n, r, j = sp["qi"], sp["n"], sp["r"], sp["j"]
                qs = slice(qi * 128, (qi + 1) * 128)
                b = sp["g"] % NSP
                if sp["idx"] == 0 and n >= 1:
                    P.mm(gate_ps, qT[s][:, qs], kmean_bf[s][:, :], True, True, rd_q + [k("kmean_bf%d" % s)], [k("miscbank")])
                    P.add("gpsimd", lambda e: e.memset(gm[r][:], -BIG), [], [k("gm%d" % r)])
                    P.add("vector", lambda e: e.tensor_copy(gm[r][:, 0:n], gate_ps[:, 0:n]), [k("miscbank")], [k("gm%d" % r)])
                    if pflag is not None:
                        P.add("vector", lambda e: e.tensor_tensor(out=gm[r][:, 0:8], in0=gm[r][:, 0:8], in1=pf[:], op=ALU.add),
                              [k("gm%d" % r), k("pf")], [k("gm%d" % r)])
                    P.add("vector", lambda e: e.max(out=top8[r][:], in_=gm[r][:]), [k("gm%d" % r)], [k("top8%d" % r)])
                    P.add("gpsimd", lambda e: e.tensor_scalar(vb[r][:], gm[r][:], -1e29, BIG, op0=ALU.is_gt, op1=ALU.mult),
                          [k("gm%d" % r)], [k("vb%d" % r)])
                    P.add("gpsimd", lambda e: e.tensor_scalar(selb[r][:], gm[r][:], top8[r][:, 2:3], None, op0=ALU.is_ge),
                          [k("gm%d" % r), k("top8%d" % r)], [k("selb%d" % r)])
                    P.add("gpsimd", lambda e: e.tensor_tensor(out=selb[r][:], in0=selb[r][:], in1=vb[r][:], op=ALU.mult),
                          [k("selb%d" % r), k("vb%d" % r)], [k("selb%d" % r)])
                P.mm(s_ps[b], qT[s][:, qs], kT[s][:, j * BLK:(j + 1) * BLK], True, True, rd_q + rd_k, [k("s_ps%d" % b)])

            def S2(sp):
                n, v, r, j = sp["n"], sp["v"], sp["r"], sp["j"]
                b = sp["g"] % NSP
                pi = sp["g"] % NPJ
                w = sp["g"] % 3
                own_off = 384 if v == 0 else 256
                prev_off = 128 if v == 0 else 0
                outk = [k("Pj%d" % pi), k("rsp%d_%d" % (r, j))]
                if j == n:
                    P.add("vector", lambda e: e.scalar_tensor_tensor(
                        out=sbw[w][:], in0=s_ps[b], scalar=SCALE, in1=Wt[:, h, own_off:own_off + 256],
                        op0=ALU.mult, op1=ALU.add), [k("s_ps%d" % b), k("Wt%d" % h)], [k("sbw%d" % w)])
                    P.add("vector", lambda e: e.reduce_max(out=mx[r][:], in_=sbw[w][:], axis=AX.X),
                          [k("sbw%d" % w)], [k("mx%d" % r)])
                    P.add("vector", lambda e: e.tensor_scalar(negm[r][:], mx[r][:], -1.0, None, op0=ALU.mult),
                          [k("mx%d" % r)], [k("negm%d" % r)])
                    if n >= 1:
                        P.add("vector", lambda e: e.tensor_scalar(biasq[r][:], selb[r][:], -BIG, negm[r][:, 0:1],
                                                                  op0=ALU.add, op1=ALU.add),
                              [k("selb%d" % r), k("negm%d" % r)], [k("biasq%d" % r)])
                    if n >= 2:
                        P.add("vector", lambda e: e.tensor_scalar(
                            biasq[r][:, 0:n - 1], biasq[r][:, 0:n - 1], rbb[:, 31, h:h + 1], None, op0=ALU.add),
                              [k("biasq%d" % r), k("rbb")], [k("biasq%d" % r)])
                    P.add("scalar", lambda e: e.activation(out=Pj[pi][:], in_=sbw[w][:], func=AF.Exp, bias=negm[r][:, 0:1],
                                                           accum_out=rsp[r][:, j:j + 1]),
                          [k("sbw%d" % w), k("negm%d" % r)], outk)
                elif j == n - 1:
                    P.add("vector", lambda e: e.scalar_tensor_tensor(
                        out=sbw[w][:], in0=s_ps[b], scalar=SCALE, in1=Wt[:, h, prev_off:prev_off + 256],
                        op0=ALU.mult, op1=ALU.add), [k("s_ps%d" % b), k("Wt%d" % h)], [k("sbw%d" % w)])
                    P.add("scalar", lambda e: e.activation(out=Pj[pi][:], in_=sbw[w][:], func=AF.Exp,
                                                           bias=biasq[r][:, j:j + 1], accum_out=rsp[r][:, j:j + 1]),
                          [k("sbw%d" % w), k("biasq%d" % r)], outk)
                else:
                    P.add("scalar", lambda e: e.activation(out=Pj[pi][:], in_=s_ps[b], func=AF.Exp,
                                                           bias=biasq[r][:, j:j + 1], scale=SCALE,
                                                           accum_out=rsp[r][:, j:j + 1]),
                          [k("s_ps%d" % b), k("biasq%d" % r)], outk)

            def S34(sp):
                pi = sp["g"] % NPJ
                tb_ = sp["g"] % 2
                for t in range(2):
                    P.add("tensor", lambda e, t=t: e.transpose(ptb[tb_][:, t, :], Pj[pi][:, t * 128:(t + 1) * 128], cx.ident[:]),
                          [k("Pj%d" % pi), "ident"], [k("ptb%d" % tb_)])
                if sp["g"] % 3 == 0:
                    P.add("scalar", lambda e: e.copy(out=pT[pi][:], in_=ptb[tb_]), [k("ptb%d" % tb_)], [k("pT%d" % pi)])
                else:
                    P.add("vector", lambda e: e.tensor_copy(pT[pi][:], ptb[tb_]), [k("ptb%d" % tb_)], [k("pT%d" % pi)])

            def S5(sp):
                qi, n, r, r3, j = sp["qi"], sp["n"], sp["r"], sp["r3"], sp["j"]
                pi = sp["g"] % NPJ
                qs = slice((qi - q_lo) * 128, (qi - q_lo + 1) * 128)
                for t in range(2):
                    P.mm(pso[r3], pT[pi][:, t, :], vh[s][:, j * 2 + t, :], sp["idx"] == 0 and t == 0,
                         sp["last"] and t == 1, [k("pT%d" % pi), k("vh%d" % s)], [k("pso%d" % r3)])
                if sp["last"]:
                    P.add("vector", lambda e: e.reduce_sum(out=rsum[r][:], in_=rsp[r][:, 0:n + 1], axis=AX.X),
                          [k("rsp%d_%d" % (r, jj)) for jj in range(n + 1)], [k("rsum%d" % r)])
                    P.add("vector", lambda e: e.reciprocal(rsum[r][:], rsum[r][:]), [k("rsum%d" % r)], [k("rsum%d" % r)])
                    P.add("vector", lambda e: e.tensor_scalar(o_sb[r][:], pso[r3], rsum[r][:, 0:1], None, op0=ALU.mult),
                          [k("pso%d" % r3), k("rsum%d" % r)], [k("o_sb%d" % r)])
                    P.add("tensor", lambda e: e.transpose(otp, o_sb[r][:], cx.ident[:]), [k("o_sb%d" % r), "ident"], [k("miscbank")])
                    P.add("scalar", lambda e: e.copy(out=OTh[s][:, qs], in_=otp), [k("miscbank")], [k("OTh%d" % s)])

            ns = len(steps)
            for t in range(-3, ns):
                if 0 <= t + 3 < ns:
                    S1(steps[t + 3])
                if 0 <= t + 2 < ns:
                    S2(steps[t + 2])
                if 0 <= t + 1 < ns:
                    S34(steps[t + 1])
                if 0 <= t < ns:
                    S5(steps[t])
            P.dma("sync", OT[h, :, :], OTh[s][:], [k("OTh%d" % s)], [])
            return gstep

        gstep = 0
        for h in range(NHEAD):
            gstep = head_body(h, h % 2, gstep)
    P.barrier()


def wo_phase(cx, OT, h_in, h_out, w_o, NT, TB=1024):
    nc, P = cx.nc, cx.P
    pre = cx.name("wo") + "_"
    k = lambda s: pre + s
    NSW = 256
    with ExitStack() as st:
        sb, ps = alloc_fns(nc, st, cx)
        oT = sb("oT", [128, NHEAD, TB], BF16)
        wo = [sb("wo", [128, NHEAD, NSW], BF16) for _ in range(2)]
        res = [sb("res", [128, NSW], F32) for _ in range(2)]
        ot = [sb("ot", [128, NSW], F32) for _ in range(2)]
        pso = [ps("pso", [128, NSW], F32) for _ in range(2)]
        w_v = w_o.rearrange("(kc p) n -> p kc n", p=128)
        cc = 0
        for tb in range(NT // TB):
            tok0 = tb * TB
            for hh in range(NHEAD):
                P.dma("sync", oT[:, hh, :], OT[hh, :, tok0:tok0 + TB], [], [k("oT_%d" % hh)])
            for ns in range(D // NSW):
                s = ns % 2
                P.dma("gpsimd", wo[s][:], w_v[:, :, ns * NSW:(ns + 1) * NSW], [], [k("wo%d" % s)])
                for tt in range(TB // 128):
                    b = cc % 2
                    cc += 1
                    r0 = tok0 + tt * 128
                    P.dma("sync", res[b][:], h_in[r0:r0 + 128, ns * NSW:(ns + 1) * NSW], [], [k("res%d" % b)])
                    for kc in range(NHEAD):
                        P.mm(pso[b][:], oT[:, kc, tt * 128:(tt + 1) * 128], wo[s][:, kc, :], kc == 0, kc == NHEAD - 1,
                             [k("oT_%d" % kc), k("wo%d" % s)], [k("pso%d" % b)])
                    P.add("vector", lambda e, b=b: e.tensor_tensor(out=ot[b][:], in0=pso[b][:], in1=res[b][:], op=ALU.add),
                          [k("pso%d" % b), k("res%d" % b)], [k("ot%d" % b)])
                    P.dma("sync", h_out[r0:r0 + 128, ns * NSW:(ns + 1) * NSW], ot[b][:], [k("ot%d" % b)], [])
    P.barrier()


NG = 128
NP = 64
GH = 16
TCH = 8
GELU_C = 1.5957691216057308


def s5_prep(cx, a_re, a_im, log_step, b_re, b_im, c_re, c_im, d_skip, Toep_d, Bmat_d, Cre_d, Cim_d, A8_d):
    nc, P = cx.nc, cx.P
    pre = cx.name("s5p") + "_"
    k = lambda s: pre + s
    GB = 16
    with ExitStack() as st:
        sb, ps = alloc_fns(nc, st, cx)
        identf = sb("identf", [128, 128], F32)
        maskT = sb("maskT", [128, 8, 16], F32)
        aTr = sb("aTr", [NP, NG], F32)
        aTi = sb("aTi", [NP, NG], F32)
        dtb = sb("dtb", [NP, NG], F32)
        lre = sb("lre", [NP, NG], F32)
        lim = sb("lim", [NP, NG], F32)
        PHr = sb("PHr", [NP, 9, NG], F32)
        PHi = sb("PHi", [NP, 9, NG], F32)
        Wr = sb("Wr", [NP, 9, NG], F32)
        Wi = sb("Wi", [NP, 9, NG], F32)
        WRr = sb("WRr", [NP, 8, NG], F32)
        WRi = sb("WRi", [NP, 8, NG], F32)
        WNr = sb("WNr", [NP, 8, NG], F32)
        WNi = sb("WNi", [NP, 8, NG], F32)
        mg = sb("mg", [NP, NG], F32)
        t1 = sb("t1", [NP, NG], F32)
        t2 = sb("t2", [NP, NG], F32)
        cfr = sb("cfr", [NP, NG], F32)
        cfi = sb("cfi", [NP, NG], F32)
        bR = sb("bR", [NP, NG, GH], F32)
        bI = sb("bI", [NP, NG, GH], F32)
        bbr = sb("bbr", [NP, NG, GH], F32)
        bbi = sb("bbi", [NP, NG, GH], F32)
        cN = sb("cN", [128, 16, NP], F32)
        cTr = sb("cTr", [NP, NG, GH], F32)
        cTi = sb("cTi", [NP, NG, GH], F32)
        big1 = sb("big1", [NP, GB, 9, GH], F32)
        big2 = sb("big2", [NP, GB, 9, GH], F32)
        Bmr = sb("Bmr", [NP, GB, 8, GH], F32)
        Bmi = sb("Bmi", [NP, GB, 8, GH], F32)
        Bnr = sb("Bnr", [NP, GB, 8, GH], F32)
        BniN = sb("BniN", [NP, GB, 8, GH], F32)
        CPr = sb("CPr", [NP, GB, 9, GH], F32)
        CPi = sb("CPi", [NP, GB, 9, GH], F32)
        Bst = [sb("Bst", [128, GB, 128], BF16) for _ in range(2)]
        Tst = [sb("Tst", [128, GB, 128], BF16) for _ in range(2)]
        Crst = [sb("Crst", [NP, GB, 8, GH], BF16) for _ in range(2)]
        Cist = [sb("Cist", [NP, GB, 8, GH], BF16) for _ in range(2)]
        Dcols = sb("Dcols", [128, NG], F32)
        anat = sb("anat", [128, NP], F32)
        dnat = sb("dnat", [128, GH], F32)
        dnat8 = sb("dnat8", [128, 8, GH], F32)
        tmpT = [sb("tmpT", [128, 128], F32) for _ in range(2)]
        A8s = sb("A8s", [128, 2, 64], F32)
        ctp = [ps("ctp", [NP, 128], F32) for _ in range(2)]
        btp = [ps("btp", [128, 128], F32) for _ in range(2)]
        tpp = [ps("tpp", [128, 128], F32) for _ in range(2)]

        V = "vector"
        G_ = "gpsimd"
        P.add(G_, lambda e: e.memset(identf[:], 1.0), [], [k("identf")])
        P.add(G_, lambda e: e.affine_select(out=identf[:], in_=identf[:], pattern=[[-1, 128]], compare_op=ALU.is_equal,
                                            fill=0.0, base=0, channel_multiplier=1), [k("identf")], [k("identf")])
        P.add(G_, lambda e: e.memset(maskT[:], 1.0), [], [k("maskT")])
        P.add(G_, lambda e: e.affine_select(out=maskT[:], in_=maskT[:], pattern=[[16, 8], [0, 16]], compare_op=ALU.is_ge,
                                            fill=0.0, base=15, channel_multiplier=-1), [k("maskT")], [k("maskT")])
        for (asrc, adst, nm) in ((a_re, aTr, "aTr"), (a_im, aTi, "aTi")):
            P.dma("sync", anat[:], asrc[:, :], [], [k("anat")])
            P.add("tensor", lambda e: e.transpose(ctp[0][:], anat[:], identf[:]), [k("anat"), k("identf")], [k("ctp0")])
            P.add("scalar", lambda e, adst=adst: e.copy(out=adst[:], in_=ctp[0][:]), [k("ctp0")], [k(nm)])
        P.dma("sync", dtb[:], log_step.partition_broadcast(NP), [], [k("dtb")])
        P.dma("sync", bR[:], b_re.rearrange("g p h -> p g h"), [], [k("bR")])
        P.dma("sync", bI[:], b_im.rearrange("g p h -> p g h"), [], [k("bI")])
        P.dma("sync", dnat[:], d_skip.rearrange("(g h) -> g h", h=GH), [], [k("dnat")])
        P.add("vector", lambda e: e.tensor_copy(dnat8[:], dnat[:, :].unsqueeze(1).to_broadcast([128, 8, GH])),
              [k("dnat")], [k("dnat8")])
        P.mm(tpp[0][:], dnat8[:].rearrange("p s h -> p (s h)"), identf[:], True, True, [k("dnat8"), k("identf")], [k("tpp0")])
        P.add("scalar", lambda e: e.copy(out=Dcols[:], in_=tpp[0][:]), [k("tpp0")], [k("Dcols")])
        for (csrc, cT, nm) in ((c_re, cTr, "cTr"), (c_im, cTi, "cTi")):
            P.dma("sync", cN[:], csrc.rearrange("(gc g8) h p -> (g8 h) gc p", g8=8), [], [k("cN")])
            for gc in range(16):
                b = gc % 2
                P.add("tensor", lambda e, b=b, gc=gc: e.transpose(ctp[b][:], cN[:, gc, :], identf[:]),
                      [k("cN"), k("identf")], [k("ctp%d" % b)])
                P.add("scalar", lambda e, b=b, gc=gc, cT=cT: e.copy(
                    out=cT[:, gc * 8:(gc + 1) * 8, :], in_=ctp[b][:].rearrange("p (g h) -> p g h", h=GH)),
                      [k("ctp%d" % b)], [k(nm)])
        P.add("scalar", lambda e: e.activation(out=dtb[:], in_=dtb[:], func=AF.Exp), [k("dtb")], [k("dtb")])
        P.add(V, lambda e: e.tensor_tensor(out=lre[:], in0=aTr[:], in1=dtb[:], op=ALU.mult), [k("aTr"), k("dtb")], [k("lre")])
        P.add(V, lambda e: e.tensor_tensor(out=lim[:], in0=aTi[:], in1=dtb[:], op=ALU.mult), [k("aTi"), k("dtb")], [k("lim")])
        cc, ss_ = PHr[:, 1, :], PHi[:, 1, :]
        P.add(V, lambda e: e.tensor_scalar(t1[:], lim[:], -0.125, float(np.pi / 2), op0=ALU.mult, op1=ALU.add),
              [k("lim")], [k("t1")])
        P.add("scalar", lambda e: e.activation(out=cc, in_=t1[:], func=AF.Sin), [k("t1")], [k("PH")])
        P.add("scalar", lambda e: e.activation(out=ss_, in_=lim[:], func=AF.Sin, scale=0.125), [k("lim")], [k("PH")])
        for _ in range(3):
            P.add(V, lambda e: e.tensor_tensor(out=t1[:], in0=cc, in1=cc, op=ALU.mult), [k("PH")], [k("t1")])
            P.add(V, lambda e: e.tensor_tensor(out=t2[:], in0=ss_, in1=ss_, op=ALU.mult), [k("PH")], [k("t2")])
            P.add(V, lambda e: e.scalar_tensor_tensor(out=ss_, in0=cc, scalar=2.0, in1=ss_, op0=ALU.mult, op1=ALU.mult),
                  [k("PH")], [k("PH")])
            P.add(V, lambda e: e.tensor_tensor(out=cc, in0=t1[:], in1=t2[:], op=ALU.subtract), [k("t1"), k("t2")], [k("PH")])
        P.add(V, lambda e: e.memset(PHr[:, 0, :], 1.0), [], [k("PH")])
        P.add(V, lambda e: e.memset(PHi[:, 0, :], 0.0), [], [k("PH")])
        for kk in range(2, 9):
            ar, ai = PHr[:, kk - 1, :], PHi[:, kk - 1, :]
            orr, oi = PHr[:, kk, :], PHi[:, kk, :]
            P.add(V, lambda e, ar=ar: e.tensor_tensor(out=t1[:], in0=ar, in1=cc, op=ALU.mult), [k("PH")], [k("t1")])
            P.add(V, lambda e, ai=ai: e.tensor_tensor(out=t2[:], in0=ai, in1=ss_, op=ALU.mult), [k("PH")], [k("t2")])
            P.add(V, lambda e, orr=orr: e.tensor_tensor(out=orr, in0=t1[:], in1=t2[:], op=ALU.subtract),
                  [k("t1"), k("t2")], [k("PH")])
            P.add(V, lambda e, ar=ar: e.tensor_tensor(out=t1[:], in0=ar, in1=ss_, op=ALU.mult), [k("PH")], [k("t1")])
            P.add(V, lambda e, ai=ai: e.tensor_tensor(out=t2[:], in0=ai, in1=cc, op=ALU.mult), [k("PH")], [k("t2")])
            P.add(V, lambda e, oi=oi: e.tensor_tensor(out=oi, in0=t1[:], in1=t2[:], op=ALU.add),
                  [k("t1"), k("t2")], [k("PH")])
        for kk in range(9):
            P.add("scalar", lambda e, kk=kk: e.activation(out=mg[:], in_=lre[:], func=AF.Exp, scale=float(kk)),
                  [k("lre")], [k("mg")])
            P.add(V, lambda e, kk=kk: e.tensor_tensor(out=Wr[:, kk, :], in0=PHr[:, kk, :], in1=mg[:], op=ALU.mult),
                  [k("PH"), k("mg")], [k("W")])
            P.add(V, lambda e, kk=kk: e.tensor_tensor(out=Wi[:, kk, :], in0=PHi[:, kk, :], in1=mg[:], op=ALU.mult),
                  [k("PH"), k("mg")], [k("W")])
        for kk in range(8):
            P.add("scalar", lambda e, kk=kk: e.activation(out=mg[:], in_=lre[:], func=AF.Exp, scale=float(-kk)),
                  [k("lre")], [k("mg")])
            P.add(V, lambda e, kk=kk: e.tensor_tensor(out=WNr[:, kk, :], in0=PHr[:, kk, :], in1=mg[:], op=ALU.mult),
                  [k("PH"), k("mg")], [k("WN")])
            P.add(V, lambda e, kk=kk: e.scalar_tensor_tensor(out=WNi[:, kk, :], in0=PHi[:, kk, :], scalar=-1.0, in1=mg[:],
                                                             op0=ALU.mult, op1=ALU.mult),
                  [k("PH"), k("mg")], [k("WN")])
            P.add(G_, lambda e, kk=kk: e.tensor_copy(WRr[:, kk, :], Wr[:, 7 - kk, :]), [k("W")], [k("WR")])
            P.add(G_, lambda e, kk=kk: e.tensor_copy(WRi[:, kk, :], Wi[:, 7 - kk, :]), [k("W")], [k("WR")])
        ar, ai = Wr[:, 1, :], Wi[:, 1, :]
        P.add(V, lambda e: e.tensor_scalar(t1[:], ar, -1.0, None, op0=ALU.add), [k("W")], [k("t1")])
        P.add(V, lambda e: e.tensor_tensor(out=cfr[:], in0=t1[:], in1=aTr[:], op=ALU.mult), [k("t1"), k("aTr")], [k("cfr")])
        P.add(V, lambda e: e.tensor_tensor(out=t2[:], in0=ai, in1=aTi[:], op=ALU.mult), [k("W"), k("aTi")], [k("t2")])
        P.add(V, lambda e: e.tensor_tensor(out=cfr[:], in0=cfr[:], in1=t2[:], op=ALU.add), [k("cfr"), k("t2")], [k("cfr")])
        P.add(V, lambda e: e.tensor_tensor(out=cfi[:], in0=ai, in1=aTr[:], op=ALU.mult), [k("W"), k("aTr")], [k("cfi")])
        P.add(V, lambda e: e.tensor_tensor(out=t2[:], in0=t1[:], in1=aTi[:], op=ALU.mult), [k("t1"), k("aTi")], [k("t2")])
        P.add(V, lambda e: e.tensor_tensor(out=cfi[:], in0=cfi[:], in1=t2[:], op=ALU.subtract), [k("cfi"), k("t2")], [k("cfi")])
        P.add(V, lambda e: e.tensor_tensor(out=t1[:], in0=aTr[:], in1=aTr[:], op=ALU.mult), [k("aTr")], [k("t1")])
        P.add(V, lambda e: e.tensor_tensor(out=t2[:], in0=aTi[:], in1=aTi[:], op=ALU.mult), [k("aTi")], [k("t2")])
        P.add(V, lambda e: e.tensor_tensor(out=t1[:], in0=t1[:], in1=t2[:], op=ALU.add), [k("t1"), k("t2")], [k("t1")])
        P.add(V, lambda e: e.reciprocal(t1[:], t1[:]), [k("t1")], [k("t1")])
        P.add(V, lambda e: e.tensor_tensor(out=cfr[:], in0=cfr[:], in1=t1[:], op=ALU.mult), [k("cfr"), k("t1")], [k("cfr")])
        P.add(V, lambda e: e.tensor_tensor(out=cfi[:], in0=cfi[:], in1=t1[:], op=ALU.mult), [k("cfi"), k("t1")], [k("cfi")])
        bc3 = lambda t: t[:, :].unsqueeze(2).to_broadcast([NP, NG, GH])
        P.add(V, lambda e: e.tensor_tensor(out=bbr[:], in0=bR[:], in1=bc3(cfr), op=ALU.mult), [k("bR"), k("cfr")], [k("bbr")])
        P.add(V, lambda e: e.tensor_tensor(out=bbi[:], in0=bI[:], in1=bc3(cfi), op=ALU.mult), [k("bI"), k("cfi")], [k("bbi")])
        P.add(V, lambda e: e.tensor_tensor(out=bbr[:], in0=bbr[:], in1=bbi[:], op=ALU.subtract), [k("bbr"), k("bbi")], [k("bbr")])
        P.add(V, lambda e: e.tensor_tensor(out=bbi[:], in0=bI[:], in1=bc3(cfr), op=ALU.mult), [k("bI"), k("cfr"), k("bbr")], [k("bbi")])
        P.add(V, lambda e: e.tensor_tensor(out=bR[:], in0=bR[:], in1=bc3(cfi), op=ALU.mult), [k("bR"), k("cfi")], [k("bR")])
        P.add(V, lambda e: e.tensor_tensor(out=bbi[:], in0=bbi[:], in1=bR[:], op=ALU.add), [k("bbi"), k("bR")], [k("bbi")])

        for ri, Wsrc in ((0, Wr), (1, Wi)):
            P.add("scalar", lambda e, ri=ri, Wsrc=Wsrc: e.copy(out=A8s[0:64, ri, :], in_=Wsrc[:, 8, 0:64]), [k("W")], [k("A8s")])
            P.add("scalar", lambda e, ri=ri, Wsrc=Wsrc: e.copy(out=A8s[64:128, ri, :], in_=Wsrc[:, 8, 64:128]), [k("W")], [k("A8s")])
        P.dma("sync", A8_d.rearrange("r p g -> p r g"), A8s[:], [k("A8s")], [])

        def cmul_b(out_r, out_i, Wre, Wim, nk, xr, xi, g0, neg_im=False, eng=V):
            wb = lambda W: W[:, 0:nk, g0:g0 + GB].rearrange("p k g -> p g k").unsqueeze(3).to_broadcast([NP, GB, nk, GH])
            xb = lambda X: X[:, g0:g0 + GB, :].unsqueeze(2).to_broadcast([NP, GB, nk, GH])
            b1 = big1[:, :, 0:nk, :]
            b2 = big2[:, :, 0:nk, :]
            rd = [k("W"), k("WN"), k("WR"), k("bbr"), k("bbi"), k("cTr"), k("cTi")]
            P.add(eng, lambda e: e.tensor_tensor(out=b1, in0=wb(Wre), in1=xb(xr), op=ALU.mult), rd, [k("big1")])
            P.add(eng, lambda e: e.tensor_tensor(out=b2, in0=wb(Wim), in1=xb(xi), op=ALU.mult), rd, [k("big2")])
            P.add(eng, lambda e: e.tensor_tensor(out=out_r, in0=b1, in1=b2, op=ALU.subtract), [k("big1"), k("big2")], [k("batch")])
            P.add(eng, lambda e: e.tensor_tensor(out=b1, in0=wb(Wre), in1=xb(xi), op=ALU.mult), rd + [k("batch")], [k("big1")])
            P.add(eng, lambda e: e.tensor_tensor(out=b2, in0=wb(Wim), in1=xb(xr), op=ALU.mult), rd + [k("batch")], [k("big2")])
            if neg_im:
                P.add(eng, lambda e: e.scalar_tensor_tensor(out=out_i, in0=b1, scalar=-1.0, in1=b2, op0=ALU.mult, op1=ALU.subtract),
                      [k("big1"), k("big2")], [k("batch")])
            else:
                P.add(eng, lambda e: e.tensor_tensor(out=out_i, in0=b1, in1=b2, op=ALU.add), [k("big1"), k("big2")], [k("batch")])

        for bi in range(NG // GB):
            g0 = bi * GB
            sl = bi % 2
            cmul_b(Bmr[:], Bmi[:], WRr, WRi, 8, bbr, bbi, g0)
            cmul_b(Bnr[:], BniN[:], WNr, WNi, 8, bbr, bbi, g0, neg_im=True)
            cmul_b(CPr[:], CPi[:], Wr, Wi, 9, cTr, cTi, g0)
            P.add("scalar", lambda e, sl=sl: e.copy(out=Crst[sl][:], in_=CPr[:, :, 1:9, :]), [k("batch")], [k("Crst%d" % sl)])
            P.add("scalar", lambda e, sl=sl: e.mul(Cist[sl][:], CPi[:, :, 1:9, :], -1.0), [k("batch")], [k("Cist%d" % sl)])
            for gb in range(GB):
                g = g0 + gb
                b = g % 2
                P.add("tensor", lambda e, b=b, gb=gb: e.transpose(btp[b][:, 0:64], Bmr[:, gb, :, :].rearrange("p s h -> p (s h)"),
                                                                  identf[0:64, 0:64]), [k("batch"), k("identf")], [k("btp%d" % b)])
                P.add("tensor", lambda e, b=b, gb=gb: e.transpose(btp[b][:, 64:128], Bmi[:, gb, :, :].rearrange("p s h -> p (s h)"),
                                                                  identf[0:64, 0:64]), [k("batch"), k("identf")], [k("btp%d" % b)])
                P.add("scalar", lambda e, b=b, gb=gb, sl=sl: e.copy(out=Bst[sl][:, gb, :], in_=btp[b][:]),
                      [k("btp%d" % b)], [k("Bst%d" % sl)])
                P.mm(tpp[b][:], Bnr[:, gb, :, :].rearrange("p s h -> p (s h)"),
                     CPr[:, gb, 0:8, :].rearrange("p s h -> p (s h)"), True, False, [k("batch")], [k("tpp%d" % b)])
                P.mm(tpp[b][:], BniN[:, gb, :, :].rearrange("p s h -> p (s h)"),
                     CPi[:, gb, 0:8, :].rearrange("p s h -> p (s h)"), False, True, [k("batch")], [k("tpp%d" % b)])
                P.add(V, lambda e, b=b: e.tensor_tensor(out=tmpT[b][:], in0=tpp[b][:], in1=maskT[:].rearrange("p s h -> p (s h)"),
                                                        op=ALU.mult), [k("tpp%d" % b), k("maskT")], [k("tmpT%d" % b)])
                P.add(V, lambda e, b=b, g=g, gb=gb, sl=sl: e.scalar_tensor_tensor(
                    out=Tst[sl][:, gb, :], in0=identf[:], scalar=Dcols[:, g:g + 1], in1=tmpT[b][:], op0=ALU.mult, op1=ALU.add),
                      [k("tmpT%d" % b), k("identf"), k("Dcols")], [k("Tst%d" % sl)])
            P.dma("sync", Bmat_d[g0:g0 + GB].rearrange("g k m -> k g m"), Bst[sl][:], [k("Bst%d" % sl)], [])
            P.dma("sync", Toep_d[g0:g0 + GB].rearrange("g k m -> k g m"), Tst[sl][:], [k("Tst%d" % sl)], [])
            P.dma("sync", Cre_d[g0:g0 + GB].rearrange("g p m -> p g m"), Crst[sl][:].rearrange("p g s h -> p g (s h)"),
                  [k("Crst%d" % sl)], [])
            P.dma("sync", Cim_d[g0:g0 + GB].rearrange("g p m -> p g m"), Cist[sl][:].rearrange("p g s h -> p g (s h)"),
                  [k("Cist%d" % sl)], [])
    P.barrier()


def s5_main(cx, x_in, g_ap, Toep_d, Bmat_d, Cre_d, Cim_d, A8_d, zT_d, NT, SEG=512):
    nc, P = cx.nc, cx.P
    pre = cx.name("s5m") + "_"
    k = lambda s: pre + s
    NDC = D // 128
    NC = SEG // TCH
    MB = 8
    with ExitStack() as st:
        sb, ps = alloc_fns(nc, st, cx)
        Sel = sb("Sel", [128, 64, 128], BF16)
        gt = sb("gt", [128, D], F32)
        xt = sb("xt", [128, D], F32)
        ubuf = sb("u", [128, D], BF16)
        bufs = dict(ss=sb("ss", [128, 1], F32), rs=sb("rs", [128, 1], F32), sq=ubuf, u=ubuf,
                    ptb=[ps("ptb", [128, 4, 128], BF16) for _ in range(2)])
        uT = sb("uT", [128, NDC, SEG], BF16)
        zT = sb("zT", [128, NDC, SEG], BF16)
        Uall = sb("Uall", [128, NG, NC], BF16)
        Lz = sb("Lz", [128, NC, 2, 64], F32)
        hist = sb("hist", [128, 2, 64, NC], BF16)
        Z = sb("Z", [128, 2, 64], F32)
        AA = sb("AA", [128, 2, 64], F32)
        BB = sb("BB", [128, 2, 64], F32)
        A8s = sb("A8s", [128, 2, 64], F32)
        st1 = sb("st1", [128, 2, 64], F32)
        st2 = sb("st2", [128, 2, 64], F32)
        Tm = [sb("Tm", [128, MB, 128], BF16) for _ in range(2)]
        Bm = [sb("Bm", [128, MB, 128], BF16) for _ in range(2)]
        Cr = [sb("Cr", [128, MB, 128], BF16) for _ in range(2)]
        Ci = [sb("Ci", [128, MB, 128], BF16) for _ in range(2)]
        gl1 = [sb("gl1", [128, 4, NC], F32) for _ in range(2)]
        ps_u = [ps("ps_u", [128, 4, NC], F32) for _ in range(2)]
        ps_l = ps("ps_l", [128, 8, NC], F32)
        ps_y = [ps("ps_y", [128, 4, NC], F32) for _ in range(2)]
        ps_z = ps("ps_z", [128, 8, NC], F32)

        P.add("gpsimd", lambda e: e.memset(Sel[:], 0.0), [], [k("Sel")])
        for a in range(8):
            for b in range(8):
                P.add("gpsimd", lambda e, a=a, b=b: e.tensor_copy(Sel[:, a * 8 + b, 16 * b:16 * b + 16],
                                                                  cx.ident[:, 16 * a:16 * a + 16]),
                      ["ident", k("Sel")], [k("Sel")])
        P.dma("sync", gt[:], g_ap.partition_broadcast(128), [], [k("gt")])
        P.dma("sync", A8s[:], A8_d.rearrange("r p g -> p r g"), [], [k("A8s")])
        P.add("vector", lambda e: e.tensor_copy(AA[:, 0, :], A8s[:, 0, :]), [k("A8s")], [k("AA")])
        P.add("vector", lambda e: e.tensor_copy(AA[:, 1, :], A8s[:, 0, :]), [k("A8s")], [k("AA")])
        P.add("vector", lambda e: e.tensor_scalar(BB[:, 0, :], A8s[:, 1, :], -1.0, None, op0=ALU.mult), [k("A8s")], [k("BB")])
        P.add("vector", lambda e: e.tensor_copy(BB[:, 1, :], A8s[:, 1, :]), [k("A8s")], [k("BB")])
        P.add("vector", lambda e: e.memset(Z[:], 0.0), [], [k("Z")])

        mb_i = 0
        for seg in range(NT // SEG):
            tok0 = seg * SEG
            for tt in range(SEG // 128):
                P.dma("sync", xt[:], x_in[tok0 + tt * 128: tok0 + (tt + 1) * 128, :], [], [k("xt")])
                norm_transpose_tile(cx, pre, xt[:], k("xt"), gt, uT,
                                    lambda g4, tt=tt: k("uT_%d" % (g4)), tt * 128, bufs, tt)
            for gq in range(NG // 4):
                b = gq % 2
                for gi in range(4):
                    g = gq * 4 + gi
                    dc, g8 = g // 8, g % 8
                    src = uT[:, dc, :].rearrange("p (c s) -> p s c", s=TCH)
                    for s in range(TCH):
                        P.mm(ps_u[b][:, gi, :], Sel[:, g8 * 8 + s, :], src[:, s, :], s == 0, s == TCH - 1,
                             [k("Sel"), k("uT_%d" % (dc // 4))], [k("ps_u%d" % b)])
                if b == 0:
                    P.add("scalar", lambda e, b=b, gq=gq: e.copy(out=Uall[:, gq * 4:(gq + 1) * 4, :], in_=ps_u[b][:]),
                          [k("ps_u%d" % b)], [k("U_%d" % (gq // 2))])
                else:
                    P.add("vector", lambda e, b=b, gq=gq: e.tensor_copy(Uall[:, gq * 4:(gq + 1) * 4, :], ps_u[b][:]),
                          [k("ps_u%d" % b)], [k("U_%d" % (gq // 2))])
            for gb in range(NG // MB):
                sl = mb_i % 2
                mb_i += 1
                g0 = gb * MB
                half = g0 // 64
                P.dma("sync", Bm[sl][:], Bmat_d[g0:g0 + MB].rearrange("g k m -> k g m"), [], [k("Bm%d" % sl)])
                for gi in range(MB):
                    P.mm(ps_l[:, gi, :], Bm[sl][:, gi, :], Uall[:, g0 + gi, :], True, True,
                         [k("Bm%d" % sl), k("U_%d" % gb)], [k("ps_l")])
                gp0 = g0 - half * 64
                rows = slice(half * 64, half * 64 + 64)
                P.add("scalar", lambda e, rows=rows, gp0=gp0: e.copy(
                    out=Lz[rows, :, 0, gp0:gp0 + MB].rearrange("p c g -> p g c"), in_=ps_l[0:64, :, :]),
                      [k("ps_l")], [k("Lz")])
                P.add("vector", lambda e, rows=rows, gp0=gp0: e.tensor_copy(
                    Lz[rows, :, 1, gp0:gp0 + MB].rearrange("p c g -> p g c"), ps_l[64:128, :, :]),
                      [k("ps_l")], [k("Lz")])
            for c in range(NC):
                P.add("scalar", lambda e, c=c: e.copy(out=hist[:, :, :, c], in_=Z[:]), [k("Z")], [k("hist")])
                P.add("vector", lambda e: e.tensor_tensor(out=st1[:], in0=AA[:], in1=Z[:], op=ALU.mult), [k("AA"), k("Z")], [k("st1")])
                P.add("vector", lambda e: e.tensor_tensor(out=st2[:, 0, :], in0=BB[:, 0, :], in1=Z[:, 1, :], op=ALU.mult),
                      [k("BB"), k("Z")], [k("st2")])
                P.add("vector", lambda e: e.tensor_tensor(out=st2[:, 1, :], in0=BB[:, 1, :], in1=Z[:, 0, :], op=ALU.mult),
                      [k("BB"), k("Z")], [k("st2")])
                P.add("vector", lambda e: e.tensor_tensor(out=st1[:], in0=st1[:], in1=st2[:], op=ALU.add),
                      [k("st1"), k("st2")], [k("st1")])
                P.add("vector", lambda e, c=c: e.tensor_tensor(out=Z[:], in0=st1[:], in1=Lz[:, c, :, :], op=ALU.add),
                      [k("st1"), k("Lz")], [k("Z")])
            for gb in range(NG // MB):
                sl = mb_i % 2
                mb_i += 1
                g0 = gb * MB
                half = g0 // 64
                rows = slice(half * 64, half * 64 + 64)
                P.dma("sync", Tm[sl][:], Toep_d[g0:g0 + MB].rearrange("g k m -> k g m"), [], [k("Tm%d" % sl)])
                P.dma("sync", Cr[sl][rows, :, :], Cre_d[g0:g0 + MB].rearrange("g p m -> p g m"), [], [k("Cr%d" % sl)])
                P.dma("sync", Ci[sl][rows, :, :], Cim_d[g0:g0 + MB].rearrange("g p m -> p g m"), [], [k("Ci%d" % sl)])
                for q in range(MB // 4):
                    b = (gb * 2 + q) % 2
                    for gi in range(4):
                        gl = q * 4 + gi
                        g = g0 + gl
                        gp = g - half * 64
                        P.mm(ps_y[b][:, gi, :], Tm[sl][:, gl, :], Uall[:, g, :], True, False,
                             [k("Tm%d" % sl), k("U_%d" % gb)], [k("ps_y%d" % b)])
                        P.mm(ps_y[b][:, gi, :], Cr[sl][rows, gl, :], hist[rows, 0, gp, :], False, False,
                             [k("Cr%d" % sl), k("hist")], [k("ps_y%d" % b)])
                        P.mm(ps_y[b][:, gi, :], Ci[sl][rows, gl, :], hist[rows, 1, gp, :], False, True,
                             [k("Ci%d" % sl), k("hist")], [k("ps_y%d" % b)])
                    t = gl1[b]
                    P.add("scalar", lambda e, b=b, t=t: e.activation(out=t[:], in_=ps_y[b][:], func=AF.Square),
                          [k("ps_y%d" % b)], [k("gl%d" % b)])
                    P.add("vector", lambda e, t=t: e.tensor_scalar(t[:], t[:], 0.044715, 1.0, op0=ALU.mult, op1=ALU.add),
                          [k("gl%d" % b)], [k("gl%d" % b)])
                    P.add("vector", lambda e, b=b, t=t: e.tensor_tensor(out=t[:], in0=t[:], in1=ps_y[b][:], op=ALU.mult),
                          [k("gl%d" % b), k("ps_y%d" % b)], [k("gl%d" % b)])
                    P.add("scalar", lambda e, t=t: e.activation(out=t[:], in_=t[:], func=AF.Sigmoid, scale=GELU_C),
                          [k("gl%d" % b)], [k("gl%d" % b)])
                    gs = g0 + q * 4
                    P.add("vector", lambda e, b=b, t=t, gs=gs: e.tensor_tensor(out=Uall[:, gs:gs + 4, :], in0=t[:], in1=ps_y[b][:],
                                                                               op=ALU.mult),
                          [k("gl%d" % b), k("ps_y%d" % b)], [k("U_%d" % gb)])
            for dc in range(NDC):
                for s in range(TCH):
                    for g8 in range(8):
                        P.mm(ps_z[:, s, :], Sel[:, s * 8 + g8, :], Uall[:, dc * 8 + g8, :], g8 == 0, g8 == 7,
                             [k("Sel"), k("U_%d" % dc)], [k("ps_z")])
                if dc % 2 == 0:
                    P.add("scalar", lambda e, dc=dc: e.copy(out=zT[:, dc, :].rearrange("p (c s) -> p s c", s=TCH), in_=ps_z[:]),
                          [k("ps_z")], [k("zT")])
                else:
                    P.add("vector", lambda e, dc=dc: e.tensor_copy(zT[:, dc, :].rearrange("p (c s) -> p s c", s=TCH), ps_z[:]),
                          [k("ps_z")], [k("zT")])
            P.dma("sync", zT_d[:, :, tok0:tok0 + SEG].rearrange("dc p t -> p dc t"), zT[:], [k("zT")], [])
    P.barrier()


def glu_phase(cx, zT_d, x_in, h_out, w_glu, NT, TB=1024):
    nc, P = cx.nc, cx.P
    pre = cx.name("glu") + "_"
    k = lambda s: pre + s
    NSW = 256
    NDC = D // 128
    with ExitStack() as st:
        sb, ps = alloc_fns(nc, st, cx)
        zT = sb("zT", [128, NDC, TB], BF16)
        wv = [sb("wv", [128, NDC, NSW], BF16) for _ in range(2)]
        wg = [sb("wg", [128, NDC, NSW], BF16) for _ in range(2)]
        res = [sb("res", [128, NSW], F32) for _ in range(2)]
        sg = [sb("sg", [128, NSW], F32) for _ in range(2)]
        ot = [sb("ot", [128, NSW], F32) for _ in range(2)]
        psv = [ps("psv", [128, NSW], F32) for _ in range(2)]
        psg = [ps("psg", [128, NSW], F32) for _ in range(2)]
        w_v = w_glu.rearrange("(kc p) n -> p kc n", p=128)
        cc = 0
        for tb in range(NT // TB):
            tok0 = tb * TB
            P.dma("sync", zT[:], zT_d[:, :, tok0:tok0 + TB].rearrange("dc p t -> p dc t"), [], [k("zT")])
            for ns in range(D // NSW):
                s = ns % 2
                P.dma("gpsimd", wv[s][:], w_v[:, :, ns * NSW:(ns + 1) * NSW], [], [k("wv%d" % s)])
                P.dma("gpsimd", wg[s][:], w_v[:, :, D + ns * NSW:D + (ns + 1) * NSW], [], [k("wg%d" % s)])
                for tt in range(TB // 128):
                    b = cc % 2
                    cc += 1
                    r0 = tok0 + tt * 128
                    P.dma("sync", res[b][:], x_in[r0:r0 + 128, ns * NSW:(ns + 1) * NSW], [], [k("res%d" % b)])
                    for kc in range(NDC):
                        P.mm(psv[b][:], zT[:, kc, tt * 128:(tt + 1) * 128], wv[s][:, kc, :], kc == 0, kc == NDC - 1,
                             [k("zT"), k("wv%d" % s)], [k("psv%d" % b)])
                    for kc in range(NDC):
                        P.mm(psg[b][:], zT[:, kc, tt * 128:(tt + 1) * 128], wg[s][:, kc, :], kc == 0, kc == NDC - 1,
                             [k("zT"), k("wg%d" % s)], [k("psg%d" % b)])
                    P.add("scalar", lambda e, b=b: e.activation(out=sg[b][:], in_=psg[b][:], func=AF.Sigmoid),
                          [k("psg%d" % b)], [k("sg%d" % b)])
                    P.add("vector", lambda e, b=b: e.tensor_tensor(out=sg[b][:], in0=psv[b][:], in1=sg[b][:], op=ALU.mult),
                          [k("psv%d" % b), k("sg%d" % b)], [k("sg%d" % b)])
                    P.add("vector", lambda e, b=b: e.tensor_tensor(out=ot[b][:], in0=sg[b][:], in1=res[b][:], op=ALU.add),
                          [k("sg%d" % b), k("res%d" % b)], [k("ot%d" % b)])
                    P.dma("sync", h_out[r0:r0 + 128, ns * NSW:(ns + 1) * NSW], ot[b][:], [k("ot%d" % b)], [])
    P.barrier()


NCORES = 8
SEQ = 4096
BATCH = 4
NT_EXT = 4096
NT_OWN = 2048


def build_program():
    NE, NO = NT_EXT, NT_OWN
    nc = bass.Bass("TRN2", target_bir_lowering=False)
    inp = lambda n, s: nc.dram_tensor(n, s, F32, kind="ExternalInput").ap()
    x = inp("x", [NE, D])
    pflag = inp("pflag", [128, 8])
    norm_mix_g = inp("norm_mix_g", [2, D])
    norm_ffn_g = inp("norm_ffn_g", [2, D])
    norm_final_g = inp("norm_final_g", [D])
    a_re = inp("s5_a_re", [NG, NP])
    a_im = inp("s5_a_im", [NG, NP])
    ls = inp("s5_log_step", [NG])
    b_re = inp("s5_b_re", [NG, NP, GH])
    b_im = inp("s5_b_im", [NG, NP, GH])
    c_re = inp("s5_c_re", [NG, GH, NP])
    c_im = inp("s5_c_im", [NG, GH, NP])
    s5_d = inp("s5_d", [D])
    w_glu = inp("s5_w_glu", [D, 2 * D])
    w_qkv = inp("attn_w_qkv", [D, 3 * D])
    w_o = inp("attn_w_o", [D, D])
    rel_bias = inp("rel_bias", [32, NHEAD])
    w_in = inp("ffn_w_in", [2, D, 2 * HID])
    w_out = inp("ffn_w_out", [2, HID, D])
    y = nc.dram_tensor("y", [NO, D], F32, kind="ExternalOutput").ap()
    it = lambda n, s, dt: nc.dram_tensor(n, s, dt, kind="Internal").ap()
    Toep = it("Toep", [NG, 128, 128], BF16)
    Bmat = it("Bmat", [NG, 128, 128], BF16)
    Cre = it("Cre", [NG, 64, 128], BF16)
    Cim = it("Cim", [NG, 64, 128], BF16)
    A8 = it("A8", [2, 128, 64], F32)
    zT_d = it("zT_d", [16, 128, NE], BF16)
    QT = it("QT", [NHEAD, 128, NE], BF16)
    KT = it("KT", [NHEAD, 128, NE], BF16)
    Vd = it("Vd", [NE, D], BF16)
    OT = it("OT", [NHEAD, 128, NO], BF16)
    h1 = it("h1", [NE, D], F32)
    h2 = it("h2", [NE, D], F32)
    h3 = it("h3", [NO, D], F32)
    h4 = it("h4", [NO, D], F32)
    with ExitStack() as st:
        P = Prog(nc)
        cx = Ctx(nc, P)
        make_ident(cx, st)
        s5_prep(cx, a_re, a_im, ls, b_re, b_im, c_re, c_im, s5_d, Toep, Bmat, Cre, Cim, A8)
        s5_main(cx, x, norm_mix_g[0], Toep, Bmat, Cre, Cim, A8, zT_d, NE)
        glu_phase(cx, zT_d, x, h1, w_glu, NE)
        ffn_phase(cx, h1, h2, norm_ffn_g[0], w_in[0], w_out[0], NE)
        qkv_phase(cx, h2, norm_mix_g[1], w_qkv, QT, KT, Vd, NE)
        attn_phase(cx, QT, KT, Vd, OT, rel_bias, NE, q_lo=(NE - NO) // 128, pflag=pflag)
        wo_phase(cx, OT, h2[NE - NO:NE, :], h3, w_o, NO)
        ffn_phase(cx, h3, h4, norm_ffn_g[1], w_in[1], w_out[1], NO)
        final_norm_phase(cx, h4, y, norm_final_g, NO)
        P.emit(st)
    return nc


def kernel(**inputs):
    f = lambda a: np.ascontiguousarray(np.asarray(a, dtype=np.float32))
    x = f(inputs["x"])
    shared = {
        "norm_mix_g": f(inputs["norm_mix_g"]), "norm_ffn_g": f(inputs["norm_ffn_g"]),
        "norm_final_g": f(inputs["norm_final_g"]),
        "s5_a_re": f(inputs["s5_a_re"][0]), "s5_a_im": f(inputs["s5_a_im"][0]),
        "s5_log_step": f(inputs["s5_log_step"][0]),
        "s5_b_re": f(inputs["s5_b_re"][0]), "s5_b_im": f(inputs["s5_b_im"][0]),
        "s5_c_re": f(inputs["s5_c_re"][0]), "s5_c_im": f(inputs["s5_c_im"][0]),
        "s5_d": f(inputs["s5_d"][0]), "s5_w_glu": f(inputs["s5_w_glu"][0]),
        "attn_w_qkv": f(inputs["attn_w_qkv"][0]), "attn_w_o": f(inputs["attn_w_o"][0]),
        "rel_bias": f(inputs["rel_bias"]),
        "ffn_w_in": f(inputs["ffn_w_in"]), "ffn_w_out": f(inputs["ffn_w_out"]),
    }
    nc = build_program()
    in_maps = []
    for c in range(NCORES):
        b, half = c // 2, c % 2
        m = dict(shared)
        if half == 1:
            m["x"] = x[b]
            m["pflag"] = np.zeros((128, 8), np.float32)
        else:
            xe = np.zeros((NT_EXT, D), np.float32)
            xe[NT_EXT - NT_OWN:] = x[b, :NT_OWN]
            m["x"] = xe
            m["pflag"] = np.full((128, 8), -BIG, np.float32)
        in_maps.append(m)
    res = run_bass_kernel_spmd(nc, in_maps, core_ids=list(range(NCORES)))
    out = np.empty((BATCH, SEQ, D), np.float32)
    for c in range(NCORES):
        b, half = c // 2, c % 2
        out[b, half * NT_OWN:(half + 1) * NT_OWN] = np.asarray(res.results[c]["y"])
    return out
```

```python
from contextlib import ExitStack
from concourse.bass_utils import run_bass_kernel_spmd
import numpy as np
import concourse.bass as bass
import concourse.mybir as mybir

F32 = mybir.dt.float32
BF16 = mybir.dt.bfloat16
AF = mybir.ActivationFunctionType
ALU = mybir.AluOpType
AX = mybir.AxisListType

ENGS = ["tensor", "vector", "scalar", "gpsimd", "sync"]
NPOOL = 12


class Op:
    __slots__ = ("idx", "eng", "fn", "deps", "sig", "sem", "val", "dma", "prev_same_sem")

    def __init__(self, idx, eng, fn, dma):
        self.idx = idx
        self.eng = eng
        self.fn = fn
        self.deps = set()
        self.sig = False
        self.sem = None
        self.val = 0
        self.dma = dma
        self.prev_same_sem = None


class Prog:
    def __init__(self, nc, same_engine_sync=True):
        self.nc = nc
        self.ops = []
        self.last_w = {}
        self.readers = {}
        self.same_engine_sync = same_engine_sync
        self.fence_for = {}
        self.since_barrier = []
        self.out_dmas = []

    def add(self, eng, fn, reads=(), writes=(), dma=False, out=False):
        idx = len(self.ops)
        op = Op(idx, eng, fn, dma)
        deps = op.deps
        for k in reads:
            w = self.last_w.get(k)
            if w is not None:
                deps.add(w)
        for k in writes:
            w = self.last_w.get(k)
            if w is not None:
                deps.add(w)
            for r in self.readers.get(k, ()):
                deps.add(r)
        for k in reads:
            self.readers.setdefault(k, []).append(idx)
        for k in writes:
            self.last_w[k] = idx
            self.readers[k] = []
        f = self.fence_for.pop(eng, None)
        if f:
            deps.update(f)
        deps.discard(idx)
        self.ops.append(op)
        self.since_barrier.append(idx)
        if out:
            self.out_dmas.append(idx)
        return idx

    def barrier(self):
        last = {}
        f = []
        for i in self.since_barrier:
            o = self.ops[i]
            if o.dma:
                f.append(i)
            else:
                last[o.eng] = i
        f.extend(last.values())
        self.fence_for = {e: list(f) for e in ENGS}
        self.since_barrier = []

    def mm(self, out, lhsT, rhs, start, stop, reads, writes, **kw):
        return self.add("tensor", lambda e: e.matmul(out, lhsT, rhs, start=start, stop=stop, **kw),
                        reads, writes)

    def dma(self, eng, out, in_, reads, writes, is_out=False, **kw):
        return self.add(eng, lambda e: e.dma_start(out=out, in_=in_, **kw), reads, writes,
                        dma=True, out=is_out)

    def emit(self, stack):
        nc = self.nc
        ops = self.ops
        for o in ops:
            nd = set()
            for d in o.deps:
                p = ops[d]
                if p.eng == o.eng and not p.dma:
                    if o.eng == "tensor":
                        continue
                    if o.eng == "sync":
                        continue
                    if not self.same_engine_sync:
                        continue
                nd.add(d)
            o.deps = nd
            for d in nd:
                ops[d].sig = True
        for i in self.out_dmas:
            ops[i].sig = True
        eng_sem = {e: stack.enter_context(nc.semaphore("s_" + e)) for e in ENGS}
        pools = {e: [stack.enter_context(nc.semaphore("d_%s%d" % (e, i))) for i in range(NPOOL)]
                 for e in ("sync", "gpsimd", "scalar")}
        cnt = {e: 0 for e in ENGS}
        pool_cnt = {e: [0] * NPOOL for e in pools}
        pool_rr = {e: 0 for e in pools}
        pool_last = {e: [None] * NPOOL for e in pools}
        for o in ops:
            if o.dma:
                e = o.eng
                j = pool_rr[e]
                pool_rr[e] = (j + 1) % NPOOL
                o.prev_same_sem = pool_last[e][j]
                pool_cnt[e][j] += 16
                o.sem = pools[e][j]
                o.val = pool_cnt[e][j]
                pool_last[e][j] = o.idx
            elif o.sig:
                cnt[o.eng] += 1
                o.sem = eng_sem[o.eng]
                o.val = cnt[o.eng]
        by_eng = {e: [o for o in ops if o.eng == e] for e in ENGS}
        self.stats = {e: len(v) for e, v in by_eng.items()}
        block = stack.enter_context(nc.Block())

        def make(ename):
            lst = by_eng[ename]

            def body(eng):
                waited = {}
                nwait = 0
                for o in lst:
                    need = {}
                    for d in o.deps:
                        p = ops[d]
                        key = id(p.sem)
                        if waited.get(key, 0) >= p.val:
                            continue
                        if key not in need or need[key][1] < p.val:
                            need[key] = (p.sem, p.val)
                    if o.dma and o.prev_same_sem is not None:
                        p = ops[o.prev_same_sem]
                        key = id(p.sem)
                        if waited.get(key, 0) < p.val and (key not in need or need[key][1] < p.val):
                            need[key] = (p.sem, p.val)
                    for key, (s, v) in need.items():
                        eng.wait_ge(s, v)
                        waited[key] = v
                        nwait += 1
                    inst = o.fn(eng)
                    if o.dma:
                        inst.then_inc(o.sem, 16)
                    elif o.sig:
                        inst.then_inc(o.sem, 1)
                if ename == "sync":
                    for i in self.out_dmas:
                        p = ops[i]
                        if waited.get(id(p.sem), 0) < p.val:
                            eng.wait_ge(p.sem, p.val)
                            waited[id(p.sem)] = p.val
                self.stats[ename + "_waits"] = nwait
            return body

        block.tensor(make("tensor"))
        block.vector(make("vector"))
        block.scalar(make("scalar"))
        block.gpsimd(make("gpsimd"))
        block.sync(make("sync"))
from contextlib import ExitStack

D = 2048
HID = 5632
EPS = 1e-6


class Ctx:
    def __init__(self, nc, P):
        self.nc = nc
        self.P = P
        self.ident = None
        self.uid = 0

    def name(self, s):
        self.uid += 1
        return "%s_%d" % (s, self.uid)


def alloc_fns(nc, st, cx):
    sb = lambda name, shape, dt: st.enter_context(nc.sbuf_tensor(cx.name(name), shape, dt))
    ps = lambda name, shape, dt: st.enter_context(nc.psum_tensor(cx.name(name), shape, dt))
    return sb, ps


def make_ident(cx, st):
    nc, P = cx.nc, cx.P
    sb, ps = alloc_fns(nc, st, cx)
    ident = sb("ident", [128, 128], BF16)
    P.add("gpsimd", lambda e: e.memset(ident[:], 1.0), [], ["ident"])
    P.add("gpsimd", lambda e: e.affine_select(out=ident[:], in_=ident[:], pattern=[[-1, 128]],
                                              compare_op=ALU.is_equal, fill=0.0, base=0,
                                              channel_multiplier=1), ["ident"], ["ident"])
    cx.ident = ident
    return ident


def norm_transpose_tile(cx, pre, xt_ap, xkey, gt, uT, uT_key, t0, bufs, i):
    P = cx.P
    ss, rs, sq, u, ptb = bufs["ss"], bufs["rs"], bufs["sq"], bufs["u"], bufs["ptb"]
    k = lambda s: pre + s
    P.add("scalar", lambda e: e.activation(out=sq[:], in_=xt_ap, func=AF.Square, accum_out=ss[:]),
          [xkey], [k("u"), k("ss")])
    P.add("vector", lambda e: e.tensor_scalar(rs[:], ss[:], 1.0 / D, EPS, op0=ALU.mult, op1=ALU.add),
          [k("ss")], [k("rs")])
    P.add("scalar", lambda e: e.sqrt(rs[:], rs[:]), [k("rs")], [k("rs")])
    P.add("vector", lambda e: e.reciprocal(rs[:], rs[:]), [k("rs")], [k("rs")])
    P.add("vector", lambda e: e.scalar_tensor_tensor(out=u[:], in0=xt_ap, scalar=rs[:, 0:1], in1=gt[:],
                                                     op0=ALU.mult, op1=ALU.mult),
          [xkey, k("rs"), k("gt")], [k("u")])
    for g4 in range(D // 512):
        pb = ptb[(i * 4 + g4) % 2]
        pk = k("ptb%d" % ((i * 4 + g4) % 2))
        for j in range(4):
            dc = g4 * 4 + j
            P.add("tensor", lambda e, pb=pb, j=j, dc=dc: e.transpose(pb[:, j, :], u[:, dc * 128:(dc + 1) * 128],
                                                                     cx.ident[:]),
                  [k("u"), "ident"], [pk])
        P.add("scalar", lambda e, pb=pb, g4=g4: e.copy(out=uT[:, g4 * 4:(g4 + 1) * 4, t0:t0 + 128], in_=pb[:, :, :]),
              [pk], [uT_key(g4)])


def ffn_phase(cx, h_in, h_out, g_ap, w_in, w_out, NT, TB=1024):
    nc, P = cx.nc, cx.P
    pre = cx.name("ffn") + "_"
    k = lambda s: pre + s
    NH = HID // 128
    NDC = D // 128
    NSW = 256
    with ExitStack() as st:
        sb, ps = alloc_fns(nc, st, cx)
        gt = sb("gt", [128, D], F32)
        xt = [sb("xt", [128, D], F32) for _ in range(1)]
        ubuf = sb("u", [128, D], BF16)
        bufs = dict(ss=sb("ss", [128, 1], F32), rs=sb("rs", [128, 1], F32), sq=ubuf,
                    u=ubuf,
                    ptb=[ps("ptb", [128, 4, 128], BF16) for _ in range(2)])
        uT = sb("uT", [128, NDC, TB], BF16)
        hid = sb("hid", [128, NH, TB], BF16)
        wg = [sb("wg", [128, NDC, 128], BF16) for _ in range(2)]
        wu = [sb("wu", [128, NDC, 128], BF16) for _ in range(2)]
        sgt = [sb("sgt", [128, 512], BF16) for _ in range(2)]
        wo = [sb("wo", [128, NH, NSW], BF16) for _ in range(2)]
        res = [sb("res", [128, NSW], F32) for _ in range(2)]
        ot = [sb("ot", [128, NSW], F32) for _ in range(2)]
        psg = [ps("psg", [128, 512], F32) for _ in range(2)]
        psu = [ps("psu", [128, 512], F32) for _ in range(2)]
        pso = [ps("pso", [128, NSW], F32) for _ in range(2)]

        P.dma("sync", gt[:], g_ap.partition_broadcast(128), [], [k("gt")])
        w_in_v = w_in.rearrange("(dc p) n -> p dc n", p=128)
        w_out_v = w_out.rearrange("(kc p) n -> p kc n", p=128)
        cnt = 0
        ccnt = 0
        for tb in range(NT // TB):
            tok0 = tb * TB
            for tt in range(TB // 128):
                s = 0
                P.dma("sync", xt[s][:], h_in[tok0 + tt * 128: tok0 + (tt + 1) * 128, :], [], [k("xt%d" % s)])
                norm_transpose_tile(cx, pre, xt[s][:], k("xt%d" % s), gt, uT,
                                    lambda g4, tt=tt: k("uT_%d_%d" % (tt, g4)), tt * 128, bufs, tt)
            for hc in range(NH):
                s = hc % 2
                P.dma("gpsimd", wg[s][:], w_in_v[:, :, hc * 128:(hc + 1) * 128], [], [k("wg%d" % s)])
                P.dma("gpsimd", wu[s][:], w_in_v[:, :, HID + hc * 128: HID + (hc + 1) * 128], [], [k("wu%d" % s)])
                for tq in range(TB // 512):
                    b = cnt % 2
                    cnt += 1
                    for dc in range(NDC):
                        rk = [k("uT_%d_%d" % (tq * 4 + q, dc // 4)) for q in range(4)]
                        P.mm(psg[b][:], wg[s][:, dc, :], uT[:, dc, tq * 512:(tq + 1) * 512], dc == 0, dc == NDC - 1,
                             rk + [k("wg%d" % s)], [k("psg%d" % b)])
                    for dc in range(NDC):
                        rk = [k("uT_%d_%d" % (tq * 4 + q, dc // 4)) for q in range(4)]
                        P.mm(psu[b][:], wu[s][:, dc, :], uT[:, dc, tq * 512:(tq + 1) * 512], dc == 0, dc == NDC - 1,
                             rk + [k("wu%d" % s)], [k("psu%d" % b)])
                    P.add("scalar", lambda e, b=b: e.activation(out=sgt[b][:], in_=psg[b][:], func=AF.Silu),
                          [k("psg%d" % b)], [k("sgt%d" % b)])
                    P.add("vector", lambda e, b=b, hc=hc, tq=tq: e.tensor_tensor(
                        out=hid[:, hc, tq * 512:(tq + 1) * 512], in0=psu[b][:], in1=sgt[b][:], op=ALU.mult),
                          [k("psu%d" % b), k("sgt%d" % b)], [k("hid_%d_%d" % (hc, tq))])
            for ns in range(D // NSW):
                s = ns % 2
                P.dma("gpsimd", wo[s][:], w_out_v[:, :, ns * NSW:(ns + 1) * NSW], [], [k("wo%d" % s)])
                for tt in range(TB // 128):
                    b = ccnt % 2
                    ccnt += 1
                    r0 = tok0 + tt * 128
                    P.dma("sync", res[b][:], h_in[r0:r0 + 128, ns * NSW:(ns + 1) * NSW], [], [k("res%d" % b)])
                    for kc in range(NH):
                        P.mm(pso[b][:], hid[:, kc, tt * 128:(tt + 1) * 128], wo[s][:, kc, :], kc == 0, kc == NH - 1,
                             [k("hid_%d_%d" % (kc, tt // 4)), k("wo%d" % s)], [k("pso%d" % b)])
                    P.add("vector", lambda e, b=b: e.tensor_tensor(out=ot[b][:], in0=pso[b][:], in1=res[b][:], op=ALU.add),
                          [k("pso%d" % b), k("res%d" % b)], [k("ot%d" % b)])
                    P.dma("sync", h_out[r0:r0 + 128, ns * NSW:(ns + 1) * NSW], ot[b][:], [k("ot%d" % b)], [],
                          is_out=False)
    P.barrier()


def final_norm_phase(cx, h_in, out, g_ap, NT):
    nc, P = cx.nc, cx.P
    pre = cx.name("fn") + "_"
    k = lambda s: pre + s
    with ExitStack() as st:
        sb, ps = alloc_fns(nc, st, cx)
        gt = sb("gt", [128, D], F32)
        xt = [sb("xt", [128, D], F32) for _ in range(2)]
        yt = [sb("yt", [128, D], F32) for _ in range(2)]
        sq = sb("sq", [128, D], BF16)
        ss = [sb("ss", [128, 1], F32) for _ in range(2)]
        P.dma("sync", gt[:], g_ap.partition_broadcast(128), [], [k("gt")])
        for tt in range(NT // 128):
            s = tt % 2
            P.dma("sync", xt[s][:], h_in[tt * 128:(tt + 1) * 128, :], [], [k("xt%d" % s)])
            P.add("scalar", lambda e, s=s: e.activation(out=sq[:], in_=xt[s][:], func=AF.Square, accum_out=ss[s][:]),
                  [k("xt%d" % s)], [k("sq"), k("ss%d" % s)])
            P.add("vector", lambda e, s=s: e.tensor_scalar(ss[s][:], ss[s][:], 1.0 / D, EPS, op0=ALU.mult, op1=ALU.add),
                  [k("ss%d" % s)], [k("ss%d" % s)])
            P.add("scalar", lambda e, s=s: e.sqrt(ss[s][:], ss[s][:]), [k("ss%d" % s)], [k("ss%d" % s)])
            P.add("vector", lambda e, s=s: e.reciprocal(ss[s][:], ss[s][:]), [k("ss%d" % s)], [k("ss%d" % s)])
            P.add("vector", lambda e, s=s: e.scalar_tensor_tensor(out=yt[s][:], in0=xt[s][:], scalar=ss[s][:, 0:1],
                                                                  in1=gt[:], op0=ALU.mult, op1=ALU.mult),
                  [k("xt%d" % s), k("ss%d" % s), k("gt")], [k("yt%d" % s)])
            P.dma("sync", out[tt * 128:(tt + 1) * 128, :], yt[s][:], [k("yt%d" % s)], [], is_out=True)
    P.barrier()


NHEAD = 16
HD = 128
BLK = 256
SCALE = HD ** -0.5
BIG = 1e30


def t5_lo_bounds():
    n = np.arange(0, 1024)
    max_exact = 16
    nf = np.maximum(n, 1).astype(np.float32)
    large = max_exact + (np.log(nf / np.float32(max_exact)) / np.float32(np.log(128 / 16))
                         * np.float32(32 - max_exact)).astype(np.int32)
    large = np.minimum(large, 31)
    bucket = np.where(n < max_exact, n, large)
    lo = [int(np.argmax(bucket >= b)) for b in range(32)]
    return lo


def qkv_phase(cx, h_in, g_ap, w_qkv, QT, KT, V, NT, TB=1024, q_row_lo=0):
    nc, P = cx.nc, cx.P
    pre = cx.name("qkv") + "_"
    k = lambda s: pre + s
    NDC = D // 128
    with ExitStack() as st:
        sb, ps = alloc_fns(nc, st, cx)
        gt = sb("gt", [128, D], F32)
        xt = sb("xt", [128, D], F32)
        ubuf = sb("u", [128, D], BF16)
        bufs = dict(ss=sb("ss", [128, 1], F32), rs=sb("rs", [128, 1], F32), sq=ubuf, u=ubuf,
                    ptb=[ps("ptb", [128, 4, 128], BF16) for _ in range(2)])
        uT = sb("uT", [128, NDC, TB], BF16)
        wq = [sb("wq", [128, NDC, 128], BF16) for _ in range(2)]
        qk_sb = [sb("qk", [128, TB], BF16) for _ in range(2)]
        wv = [sb("wv", [128, NDC, 512], BF16) for _ in range(2)]
        v_sb = [sb("vsb", [128, 512], BF16) for _ in range(2)]
        psq = [ps("psq", [128, 512], F32) for _ in range(2)]
        psv = [ps("psv", [128, 512], F32) for _ in range(2)]
        P.dma("sync", gt[:], g_ap.partition_broadcast(128), [], [k("gt")])
        w_v = w_qkv.rearrange("(dc p) n -> p dc n", p=128)
        cnt = 0
        vc = 0
        for tb in range(NT // TB):
            tok0 = tb * TB
            for tt in range(TB // 128):
                P.dma("sync", xt[:], h_in[tok0 + tt * 128: tok0 + (tt + 1) * 128, :], [], [k("xt")])
                norm_transpose_tile(cx, pre, xt[:], k("xt"), gt, uT,
                                    lambda g4, tt=tt: k("uT_%d_%d" % (tt, g4)), tt * 128, bufs, tt)
            for fc in range(32):
                if fc < 16 and tok0 + TB <= q_row_lo:
                    continue
                s = fc % 2
                P.dma("gpsimd", wq[s][:], w_v[:, :, fc * 128:(fc + 1) * 128], [], [k("wq%d" % s)])
                for tq in range(TB // 512):
                    b = cnt % 2
                    cnt += 1
                    for dc in range(NDC):
                        rk = [k("uT_%d_%d" % (tq * 4 + q, dc // 4)) for q in range(4)]
                        P.mm(psq[b][:], wq[s][:, dc, :], uT[:, dc, tq * 512:(tq + 1) * 512], dc == 0, dc == NDC - 1,
                             rk + [k("wq%d" % s)], [k("psq%d" % b)])
                    if b == 0:
                        P.add("scalar", lambda e, b=b, s=s, tq=tq: e.copy(out=qk_sb[s][:, tq * 512:(tq + 1) * 512],
                                                                          in_=psq[b][:]),
                              [k("psq%d" % b)], [k("qk%d_%d" % (s, tq))])
                    else:
                        P.add("vector", lambda e, b=b, s=s, tq=tq: e.tensor_copy(qk_sb[s][:, tq * 512:(tq + 1) * 512],
                                                                                 psq[b][:]),
                              [k("psq%d" % b)], [k("qk%d_%d" % (s, tq))])
                dst = QT if fc < 16 else KT
                P.dma("sync", dst[fc % 16, :, tok0:tok0 + TB], qk_sb[s][:],
                      [k("qk%d_%d" % (s, tq)) for tq in range(TB // 512)], [])
            for nq in range(4):
                s = nq % 2
                P.dma("gpsimd", wv[s][:], w_v[:, :, 4096 + nq * 512: 4096 + (nq + 1) * 512], [], [k("wv%d" % s)])
                for tt in range(TB // 128):
                    b = vc % 2
                    vc += 1
                    for dc in range(NDC):
                        P.mm(psv[b][:], uT[:, dc, tt * 128:(tt + 1) * 128], wv[s][:, dc, :], dc == 0, dc == NDC - 1,
                             [k("uT_%d_%d" % (tt, dc // 4)), k("wv%d" % s)], [k("psv%d" % b)])
                    if b == 0:
                        P.add("scalar", lambda e, b=b: e.copy(out=v_sb[b][:], in_=psv[b][:]),
                              [k("psv%d" % b)], [k("vsb%d" % b)])
                    else:
                        P.add("vector", lambda e, b=b: e.tensor_copy(v_sb[b][:], psv[b][:]),
                              [k("psv%d" % b)], [k("vsb%d" % b)])
                    r0 = tok0 + tt * 128
                    P.dma("sync", V[r0:r0 + 128, nq * 512:(nq + 1) * 512], v_sb[b][:], [k("vsb%d" % b)], [])
    P.barrier()


def attn_phase(cx, QT, KT, V, OT, rel_bias, NT, q_lo=0, pflag=None):
    nc, P = cx.nc, cx.P
    pre = cx.name("att") + "_"
    k = lambda s: pre + s
    NB = NT // BLK
    NQT = NT // 128
    lo = t5_lo_bounds()
    NQ = 4
    NSP = 3
    NPJ = 4
    with ExitStack() as st:
        sb, ps = alloc_fns(nc, st, cx)
        Wt = sb("Wt", [128, NHEAD, 640], F32)
        dl_i = sb("dl_i", [128, 640], mybir.dt.int32)
        delta = sb("delta", [128, 640], F32)
        stepm = [sb("stepm", [128, 640], F32) for _ in range(2)]
        rbb = sb("rbb", [128, 32, NHEAD], F32)
        dif = sb("dif", [128, 32, NHEAD], F32)
        qT = [sb("qT", [128, NT], BF16) for _ in range(2)]
        kT = [sb("kT", [128, NT], BF16) for _ in range(2)]
        vh = [sb("vh", [128, NT // 128, 128], BF16) for _ in range(2)]
        NTO = NT - q_lo * 128
        OTh = [sb("OTh", [128, NTO], BF16) for _ in range(2)]
        pf = sb("pf", [128, 8], F32)
        vb = [sb("vb", [128, 16], F32) for _ in range(4)]
        kmean = sb("kmean", [128, 16], F32)
        kmean_bf = [sb("kmean_bf", [128, 16], BF16) for _ in range(2)]
        gm = [sb("gm", [128, 16], F32) for _ in range(NQ)]
        top8 = [sb("top8", [128, 8], F32) for _ in range(NQ)]
        selb = [sb("selb", [128, 16], F32) for _ in range(NQ)]
        biasq = [sb("biasq", [128, 16], F32) for _ in range(NQ)]
        mx = [sb("mx", [128, 1], F32) for _ in range(NQ)]
        negm = [sb("negm", [128, 1], F32) for _ in range(NQ)]
        rsp = [sb("rsp", [128, 16], F32) for _ in range(NQ)]
        rsum = [sb("rsum", [128, 1], F32) for _ in range(NQ)]
        o_sb = [sb("o_sb", [128, 128], BF16) for _ in range(NQ)]
        sbw = [sb("sbw", [128, 256], F32) for _ in range(3)]
        Pj = [sb("Pj", [128, 512], BF16) for _ in range(NPJ)]
        pT = [sb("pT", [128, 4, 128], BF16) for _ in range(NPJ)]
        s_ps_t = [ps("s_ps", [128, 512], F32) for _ in range(3)]
        s_ps = [t_[:, :] for t_ in s_ps_t]
        ptb_t = [ps("ptb", [128, 4, 128], BF16) for _ in range(2)]
        ptb = [t_[:, :, :] for t_ in ptb_t]
        pso_t = [ps("pso", [128, 128], F32) for _ in range(2)]
        pso = [t_[:, :] for t_ in pso_t]
        misc = ps("misc", [128, 512], F32)
        gate_ps = misc[:, 0:16]
        otp = misc[:, 128:256].bitcast(BF16)[:, 0:128]
        if pflag is not None:
            P.dma("sync", pf[:], pflag[:, :], [], [k("pf")])
        P.dma("sync", rbb[:], rel_bias.rearrange("b h -> (b h)").partition_broadcast(128), [], [k("rbb")])
        P.add("gpsimd", lambda e: e.iota(dl_i[:], pattern=[[-1, 640]], base=384, channel_multiplier=1), [], [k("dl_i")])
        P.add("vector", lambda e: e.tensor_copy(delta[:], dl_i[:]), [k("dl_i")], [k("delta")])
        P.add("vector", lambda e: e.tensor_tensor(out=dif[:, 1:32, :], in0=rbb[:, 1:32, :], in1=rbb[:, 0:31, :],
                                                  op=ALU.subtract), [k("rbb")], [k("dif")])
        P.add("vector", lambda e: e.tensor_scalar(stepm[0][:], delta[:], 0.0, -BIG, op0=ALU.is_lt, op1=ALU.mult),
              [k("delta")], [k("stepm0")])
        for h in range(NHEAD):
            eng = "vector" if h % 2 == 0 else "gpsimd"
            P.add(eng, lambda e, h=h: e.tensor_scalar(Wt[:, h, :], stepm[0][:], rbb[:, 0, h:h + 1], None, op0=ALU.add),
                  [k("stepm0"), k("rbb")], [k("Wt%d" % h)])
        for b in range(1, 32):
            sm = stepm[b % 2]
            smk = k("stepm%d" % (b % 2))
            P.add("vector", lambda e, sm=sm, b=b: e.tensor_scalar(sm[:], delta[:], float(lo[b]), None, op0=ALU.is_ge),
                  [k("delta")], [smk])
            for h in range(NHEAD):
                P.add("vector", lambda e, sm=sm, b=b, h=h: e.scalar_tensor_tensor(
                    out=Wt[:, h, :], in0=sm[:], scalar=dif[:, b, h:h + 1], in1=Wt[:, h, :], op0=ALU.mult, op1=ALU.add),
                      [smk, k("dif"), k("Wt%d" % h)], [k("Wt%d" % h)])

        def head_body(h, s, gstep):
            P.dma("sync", qT[s][:], QT[h, :, :], [], [k("qT%d" % s)])
            P.dma("sync", kT[s][:], KT[h, :, :], [], [k("kT%d" % s)])
            P.dma("sync", vh[s][:], V[:, h * 128:(h + 1) * 128].rearrange("(t p) d -> p t d", p=128), [], [k("vh%d" % s)])
            P.add("vector", lambda e, s=s: e.reduce_sum(out=kmean[:, 0:NB],
                                                        in_=kT[s][:].rearrange("p (b c) -> p b c", c=BLK), axis=AX.X),
                  [k("kT%d" % s)], [k("kmean")])
            P.add("vector", lambda e, s=s: e.tensor_scalar(kmean_bf[s][:, 0:NB], kmean[:, 0:NB], 1.0 / BLK, None, op0=ALU.mult),
                  [k("kmean")], [k("kmean_bf%d" % s)])
            if NB < 16:
                P.add("vector", lambda e, s=s: e.memset(kmean_bf[s][:, NB:16], 0.0), [], [k("kmean_bf%d" % s)])
            rd_q = [k("qT%d" % s)]
            rd_k = [k("kT%d" % s)]
            steps = []
            for qi in range(q_lo, NQT):
                n = qi // 2
                order = [[n]] + ([[n - 1]] if n >= 1 else [])
                jj = n - 2
                while jj >= 0:
                    if jj >= 1:
                        order.append([jj - 1, jj])
                        jj -= 2
                    else:
                        order.append([jj])
                        jj -= 1
                for idx, js in enumerate(order):
                    steps.append(dict(qi=qi, n=n, v=qi % 2, r=qi % NQ, r3=qi % 2, js=js, j=js[0], idx=idx,
                                      last=(idx == len(order) - 1), g=gstep))
                    gstep += 1

            def S1(sp):
                qi, n, r, j = sp["qi"], sp["n"], sp["r"], sp["j"]
                qs = slice(qi * 128, (qi + 1) * 128)
                b = sp["g"] % NSP
                if sp["idx"] == 0 and n >= 1:
                    P.mm(gate_ps, qT[s][:, qs], kmean_bf[s][:, :], True, True, rd_q + [k("kmean_bf%d" % s)], [k("miscbank")])
                    P.add("gpsimd", lambda e: e.memset(gm[r][:], -BIG), [], [k("gm%d" % r)])
                    P.add("vector", lambda e: e.tensor_copy(gm[r][:, 0:n], gate_ps[:, 0:n]), [k("miscbank")], [k("gm%d" % r)])
                    if pflag is not None:
                        P.add("vector", lambda e: e.tensor_tensor(out=gm[r][:, 0:8], in0=gm[r][:, 0:8], in1=pf[:], op=ALU.add),
                              [k("gm%d" % r), k("pf")], [k("gm%d" % r)])
                    P.add("vector", lambda e: e.max(out=top8[r][:], in_=gm[r][:]), [k("gm%d" % r)], [k("top8%d" % r)])
                    P.add("gpsimd", lambda e: e.tensor_scalar(vb[r][:], gm[r][:], -1e29, BIG, op0=ALU.is_gt, op1=ALU.mult),
                          [k("gm%d" % r)], [k("vb%d" % r)])
                    P.add("gpsimd", lambda e: e.tensor_scalar(selb[r][:], gm[r][:], top8[r][:, 2:3], None, op0=ALU.is_ge),
                          [k("gm%d" % r), k("top8%d" % r)], [k("selb%d" % r)])
                    P.add("gpsimd", lambda e: e.tensor_tensor(out=selb[r][:], in0=selb[r][:], in1=vb[r][:], op=ALU.mult),
                          [k("selb%d" % r), k("vb%d" % r)], [k("selb%d" % r)])
                js = sp["js"]
                P.mm(s_ps[b][:, 0:BLK * len(js)], qT[s][:, qs], kT[s][:, js[0] * BLK:(js[-1] + 1) * BLK], True, True,
                     rd_q + rd_k, [k("s_ps%d" % b)])

            def S2(sp):
                n, v, r, j = sp["n"], sp["v"], sp["r"], sp["j"]
                b = sp["g"] % NSP
                pi = sp["g"] % NPJ
                w = sp["g"] % 3
                own_off = 384 if v == 0 else 256
                prev_off = 128 if v == 0 else 0
                outk = [k("Pj%d" % pi), k("rsp%d_%d" % (r, j))]
                if j == n:
                    P.add("vector", lambda e: e.scalar_tensor_tensor(
                        out=sbw[w][:], in0=s_ps[b][:, 0:256], scalar=SCALE, in1=Wt[:, h, own_off:own_off + 256],
                        op0=ALU.mult, op1=ALU.add), [k("s_ps%d" % b), k("Wt%d" % h)], [k("sbw%d" % w)])
                    P.add("vector", lambda e: e.reduce_max(out=mx[r][:], in_=sbw[w][:], axis=AX.X),
                          [k("sbw%d" % w)], [k("mx%d" % r)])
                    P.add("vector", lambda e: e.tensor_scalar(negm[r][:], mx[r][:], -1.0, None, op0=ALU.mult),
                          [k("mx%d" % r)], [k("negm%d" % r)])
                    if n >= 1:
                        P.add("vector", lambda e: e.tensor_scalar(biasq[r][:], selb[r][:], -BIG, negm[r][:, 0:1],
                                                                  op0=ALU.add, op1=ALU.add),
                              [k("selb%d" % r), k("negm%d" % r)], [k("biasq%d" % r)])
                    if n >= 2:
                        P.add("vector", lambda e: e.tensor_scalar(
                            biasq[r][:, 0:n - 1], biasq[r][:, 0:n - 1], rbb[:, 31, h:h + 1], None, op0=ALU.add),
                              [k("biasq%d" % r), k("rbb")], [k("biasq%d" % r)])
                    P.add("scalar", lambda e: e.activation(out=Pj[pi][:, 0:256], in_=sbw[w][:], func=AF.Exp, bias=negm[r][:, 0:1],
                                                           accum_out=rsp[r][:, j:j + 1]),
                          [k("sbw%d" % w), k("negm%d" % r)], outk)
                elif j == n - 1:
                    P.add("vector", lambda e: e.scalar_tensor_tensor(
                        out=sbw[w][:], in0=s_ps[b][:, 0:256], scalar=SCALE, in1=Wt[:, h, prev_off:prev_off + 256],
                        op0=ALU.mult, op1=ALU.add), [k("s_ps%d" % b), k("Wt%d" % h)], [k("sbw%d" % w)])
                    P.add("scalar", lambda e: e.activation(out=Pj[pi][:, 0:256], in_=sbw[w][:], func=AF.Exp,
                                                           bias=biasq[r][:, j:j + 1], accum_out=rsp[r][:, j:j + 1]),
                          [k("sbw%d" % w), k("biasq%d" % r)], outk)
                else:
                    for ji, jb in enumerate(sp["js"]):
                        P.add("scalar", lambda e, ji=ji, jb=jb: e.activation(
                            out=Pj[pi][:, ji * 256:(ji + 1) * 256], in_=s_ps[b][:, ji * 256:(ji + 1) * 256], func=AF.Exp,
                            bias=biasq[r][:, jb:jb + 1], scale=SCALE, accum_out=rsp[r][:, jb:jb + 1]),
                              [k("s_ps%d" % b), k("biasq%d" % r)], [k("Pj%d" % pi), k("rsp%d_%d" % (r, jb))])

            def S34(sp):
                pi = sp["g"] % NPJ
                tb_ = sp["g"] % 2
                nt = 2 * len(sp["js"])
                for t in range(nt):
                    P.add("tensor", lambda e, t=t: e.transpose(ptb[tb_][:, t, :], Pj[pi][:, t * 128:(t + 1) * 128], cx.ident[:]),
                          [k("Pj%d" % pi), "ident"], [k("ptb%d" % tb_)])
                P.add("vector", lambda e: e.tensor_copy(pT[pi][:, 0:nt, :], ptb[tb_][:, 0:nt, :]), [k("ptb%d" % tb_)], [k("pT%d" % pi)])

            def S5(sp):
                qi, n, r, r3, j = sp["qi"], sp["n"], sp["r"], sp["r3"], sp["j"]
                pi = sp["g"] % NPJ
                qs = slice((qi - q_lo) * 128, (qi - q_lo + 1) * 128)
                nt = 2 * len(sp["js"])
                for t in range(nt):
                    P.mm(pso[r3], pT[pi][:, t, :], vh[s][:, j * 2 + t, :], sp["idx"] == 0 and t == 0,
                         sp["last"] and t == nt - 1, [k("pT%d" % pi), k("vh%d" % s)], [k("pso%d" % r3)])
                if sp["last"]:
                    P.add("vector", lambda e: e.reduce_sum(out=rsum[r][:], in_=rsp[r][:, 0:n + 1], axis=AX.X),
                          [k("rsp%d_%d" % (r, jj)) for jj in range(n + 1)], [k("rsum%d" % r)])
                    P.add("vector", lambda e: e.reciprocal(rsum[r][:], rsum[r][:]), [k("rsum%d" % r)], [k("rsum%d" % r)])
                    P.add("vector", lambda e: e.tensor_scalar(o_sb[r][:], pso[r3], rsum[r][:, 0:1], None, op0=ALU.mult),
                          [k("pso%d" % r3), k("rsum%d" % r)], [k("o_sb%d" % r)])
                    P.add("tensor", lambda e: e.transpose(otp, o_sb[r][:], cx.ident[:]), [k("o_sb%d" % r), "ident"], [k("miscbank")])
                    P.add("scalar", lambda e: e.copy(out=OTh[s][:, qs], in_=otp), [k("miscbank")], [k("OTh%d" % s)])

            ns = len(steps)
            for t in range(-3, ns):
                if 0 <= t + 3 < ns:
                    S1(steps[t + 3])
                if 0 <= t + 2 < ns:
                    S2(steps[t + 2])
                if 0 <= t + 1 < ns:
                    S34(steps[t + 1])
                if 0 <= t < ns:
                    S5(steps[t])
            P.dma("sync", OT[h, :, :], OTh[s][:], [k("OTh%d" % s)], [])
            return gstep

        gstep = 0
        for h in range(NHEAD):
            gstep = head_body(h, h % 2, gstep)
    P.barrier()


def wo_phase(cx, OT, h_in, h_out, w_o, NT, TB=1024):
    nc, P = cx.nc, cx.P
    pre = cx.name("wo") + "_"
    k = lambda s: pre + s
    NSW = 256
    with ExitStack() as st:
        sb, ps = alloc_fns(nc, st, cx)
        oT = sb("oT", [128, NHEAD, TB], BF16)
        wo = [sb("wo", [128, NHEAD, NSW], BF16) for _ in range(2)]
        res = [sb("res", [128, NSW], F32) for _ in range(2)]
        ot = [sb("ot", [128, NSW], F32) for _ in range(2)]
        pso = [ps("pso", [128, NSW], F32) for _ in range(2)]
        w_v = w_o.rearrange("(kc p) n -> p kc n", p=128)
        cc = 0
        for tb in range(NT // TB):
            tok0 = tb * TB
            for hh in range(NHEAD):
                P.dma("sync", oT[:, hh, :], OT[hh, :, tok0:tok0 + TB], [], [k("oT_%d" % hh)])
            for ns in range(D // NSW):
                s = ns % 2
                P.dma("gpsimd", wo[s][:], w_v[:, :, ns * NSW:(ns + 1) * NSW], [], [k("wo%d" % s)])
                for tt in range(TB // 128):
                    b = cc % 2
                    cc += 1
                    r0 = tok0 + tt * 128
                    P.dma("sync", res[b][:], h_in[r0:r0 + 128, ns * NSW:(ns + 1) * NSW], [], [k("res%d" % b)])
                    for kc in range(NHEAD):
                        P.mm(pso[b][:], oT[:, kc, tt * 128:(tt + 1) * 128], wo[s][:, kc, :], kc == 0, kc == NHEAD - 1,
                             [k("oT_%d" % kc), k("wo%d" % s)], [k("pso%d" % b)])
                    P.add("vector", lambda e, b=b: e.tensor_tensor(out=ot[b][:], in0=pso[b][:], in1=res[b][:], op=ALU.add),
                          [k("pso%d" % b), k("res%d" % b)], [k("ot%d" % b)])
                    P.dma("sync", h_out[r0:r0 + 128, ns * NSW:(ns + 1) * NSW], ot[b][:], [k("ot%d" % b)], [])
    P.barrier()


NG = 128
NP = 64
GH = 16
TCH = 8
GELU_C = 1.5957691216057308


def s5_prep(cx, a_re, a_im, log_step, b_re, b_im, c_re, c_im, d_skip, Toep_d, Bmat_d, Cre_d, Cim_d, A8_d):
    nc, P = cx.nc, cx.P
    pre = cx.name("s5p") + "_"
    k = lambda s: pre + s
    GB = 16
    with ExitStack() as st:
        sb, ps = alloc_fns(nc, st, cx)
        identf = sb("identf", [128, 128], F32)
        maskT = sb("maskT", [128, 8, 16], F32)
        aTr = sb("aTr", [NP, NG], F32)
        aTi = sb("aTi", [NP, NG], F32)
        dtb = sb("dtb", [NP, NG], F32)
        lre = sb("lre", [NP, NG], F32)
        lim = sb("lim", [NP, NG], F32)
        PHr = sb("PHr", [NP, 9, NG], F32)
        PHi = sb("PHi", [NP, 9, NG], F32)
        Wr = sb("Wr", [NP, 9, NG], F32)
        Wi = sb("Wi", [NP, 9, NG], F32)
        WRr = sb("WRr", [NP, 8, NG], F32)
        WRi = sb("WRi", [NP, 8, NG], F32)
        WNr = sb("WNr", [NP, 8, NG], F32)
        WNi = sb("WNi", [NP, 8, NG], F32)
        mg = sb("mg", [NP, NG], F32)
        t1 = sb("t1", [NP, NG], F32)
        t2 = sb("t2", [NP, NG], F32)
        cfr = sb("cfr", [NP, NG], F32)
        cfi = sb("cfi", [NP, NG], F32)
        bR = sb("bR", [NP, NG, GH], F32)
        bI = sb("bI", [NP, NG, GH], F32)
        bbr = sb("bbr", [NP, NG, GH], F32)
        bbi = sb("bbi", [NP, NG, GH], F32)
        cN = sb("cN", [128, 16, NP], F32)
        cTr = sb("cTr", [NP, NG, GH], F32)
        cTi = sb("cTi", [NP, NG, GH], F32)
        big1 = sb("big1", [NP, GB, 9, GH], F32)
        big2 = sb("big2", [NP, GB, 9, GH], F32)
        Bmr = sb("Bmr", [NP, GB, 8, GH], F32)
        Bmi = sb("Bmi", [NP, GB, 8, GH], F32)
        Bnr = sb("Bnr", [NP, GB, 8, GH], F32)
        BniN = sb("BniN", [NP, GB, 8, GH], F32)
        CPr = sb("CPr", [NP, GB, 9, GH], F32)
        CPi = sb("CPi", [NP, GB, 9, GH], F32)
        Bst = [sb("Bst", [128, GB, 128], BF16) for _ in range(2)]
        Tst = [sb("Tst", [128, GB, 128], BF16) for _ in range(2)]
        Crst = [sb("Crst", [NP, GB, 8, GH], BF16) for _ in range(2)]
        Cist = [sb("Cist", [NP, GB, 8, GH], BF16) for _ in range(2)]
        Dcols = sb("Dcols", [128, NG], F32)
        anat = sb("anat", [128, NP], F32)
        dnat = sb("dnat", [128, GH], F32)
        dnat8 = sb("dnat8", [128, 8, GH], F32)
        tmpT = [sb("tmpT", [128, 128], F32) for _ in range(2)]
        A8s = sb("A8s", [128, 2, 64], F32)
        ctp = [ps("ctp", [NP, 128], F32) for _ in range(2)]
        btp = [ps("btp", [128, 128], F32) for _ in range(2)]
        tpp = [ps("tpp", [128, 128], F32) for _ in range(2)]

        V = "vector"
        G_ = "gpsimd"
        P.add(G_, lambda e: e.memset(identf[:], 1.0), [], [k("identf")])
        P.add(G_, lambda e: e.affine_select(out=identf[:], in_=identf[:], pattern=[[-1, 128]], compare_op=ALU.is_equal,
                                            fill=0.0, base=0, channel_multiplier=1), [k("identf")], [k("identf")])
        P.add(G_, lambda e: e.memset(maskT[:], 1.0), [], [k("maskT")])
        P.add(G_, lambda e: e.affine_select(out=maskT[:], in_=maskT[:], pattern=[[16, 8], [0, 16]], compare_op=ALU.is_ge,
                                            fill=0.0, base=15, channel_multiplier=-1), [k("maskT")], [k("maskT")])
        for (asrc, adst, nm) in ((a_re, aTr, "aTr"), (a_im, aTi, "aTi")):
            P.dma("sync", anat[:], asrc[:, :], [], [k("anat")])
            P.add("tensor", lambda e: e.transpose(ctp[0][:], anat[:], identf[:]), [k("anat"), k("identf")], [k("ctp0")])
            P.add("scalar", lambda e, adst=adst: e.copy(out=adst[:], in_=ctp[0][:]), [k("ctp0")], [k(nm)])
        P.dma("sync", dtb[:], log_step.partition_broadcast(NP), [], [k("dtb")])
        P.dma("sync", bR[:], b_re.rearrange("g p h -> p g h"), [], [k("bR")])
        P.dma("sync", bI[:], b_im.rearrange("g p h -> p g h"), [], [k("bI")])
        P.dma("sync", dnat[:], d_skip.rearrange("(g h) -> g h", h=GH), [], [k("dnat")])
        P.add("vector", lambda e: e.tensor_copy(dnat8[:], dnat[:, :].unsqueeze(1).to_broadcast([128, 8, GH])),
              [k("dnat")], [k("dnat8")])
        P.mm(tpp[0][:], dnat8[:].rearrange("p s h -> p (s h)"), identf[:], True, True, [k("dnat8"), k("identf")], [k("tpp0")])
        P.add("scalar", lambda e: e.copy(out=Dcols[:], in_=tpp[0][:]), [k("tpp0")], [k("Dcols")])
        for (csrc, cT, nm) in ((c_re, cTr, "cTr"), (c_im, cTi, "cTi")):
            P.dma("sync", cN[:], csrc.rearrange("(gc g8) h p -> (g8 h) gc p", g8=8), [], [k("cN")])
            for gc in range(16):
                b = gc % 2
                P.add("tensor", lambda e, b=b, gc=gc: e.transpose(ctp[b][:], cN[:, gc, :], identf[:]),
                      [k("cN"), k("identf")], [k("ctp%d" % b)])
                P.add("scalar", lambda e, b=b, gc=gc, cT=cT: e.copy(
                    out=cT[:, gc * 8:(gc + 1) * 8, :], in_=ctp[b][:].rearrange("p (g h) -> p g h", h=GH)),
                      [k("ctp%d" % b)], [k(nm)])
        P.add("scalar", lambda e: e.activation(out=dtb[:], in_=dtb[:], func=AF.Exp), [k("dtb")], [k("dtb")])
        P.add(V, lambda e: e.tensor_tensor(out=lre[:], in0=aTr[:], in1=dtb[:], op=ALU.mult), [k("aTr"), k("dtb")], [k("lre")])
        P.add(V, lambda e: e.tensor_tensor(out=lim[:], in0=aTi[:], in1=dtb[:], op=ALU.mult), [k("aTi"), k("dtb")], [k("lim")])
        cc, ss_ = PHr[:, 1, :], PHi[:, 1, :]
        P.add(V, lambda e: e.tensor_scalar(t1[:], lim[:], -0.125, float(np.pi / 2), op0=ALU.mult, op1=ALU.add),
              [k("lim")], [k("t1")])
        P.add("scalar", lambda e: e.activation(out=cc, in_=t1[:], func=AF.Sin), [k("t1")], [k("PH")])
        P.add("scalar", lambda e: e.activation(out=ss_, in_=lim[:], func=AF.Sin, scale=0.125), [k("lim")], [k("PH")])
        for _ in range(3):
            P.add(V, lambda e: e.tensor_tensor(out=t1[:], in0=cc, in1=cc, op=ALU.mult), [k("PH")], [k("t1")])
            P.add(V, lambda e: e.tensor_tensor(out=t2[:], in0=ss_, in1=ss_, op=ALU.mult), [k("PH")], [k("t2")])
            P.add(V, lambda e: e.scalar_tensor_tensor(out=ss_, in0=cc, scalar=2.0, in1=ss_, op0=ALU.mult, op1=ALU.mult),
                  [k("PH")], [k("PH")])
            P.add(V, lambda e: e.tensor_tensor(out=cc, in0=t1[:], in1=t2[:], op=ALU.subtract), [k("t1"), k("t2")], [k("PH")])
        P.add(V, lambda e: e.memset(PHr[:, 0, :], 1.0), [], [k("PH")])
        P.add(V, lambda e: e.memset(PHi[:, 0, :], 0.0), [], [k("PH")])
        for kk in range(2, 9):
            ar, ai = PHr[:, kk - 1, :], PHi[:, kk - 1, :]
            orr, oi = PHr[:, kk, :], PHi[:, kk, :]
            P.add(V, lambda e, ar=ar: e.tensor_tensor(out=t1[:], in0=ar, in1=cc, op=ALU.mult), [k("PH")], [k("t1")])
            P.add(V, lambda e, ai=ai: e.tensor_tensor(out=t2[:], in0=ai, in1=ss_, op=ALU.mult), [k("PH")], [k("t2")])
            P.add(V, lambda e, orr=orr: e.tensor_tensor(out=orr, in0=t1[:], in1=t2[:], op=ALU.subtract),
                  [k("t1"), k("t2")], [k("PH")])
            P.add(V, lambda e, ar=ar: e.tensor_tensor(out=t1[:], in0=ar, in1=ss_, op=ALU.mult), [k("PH")], [k("t1")])
            P.add(V, lambda e, ai=ai: e.tensor_tensor(out=t2[:], in0=ai, in1=cc, op=ALU.mult), [k("PH")], [k("t2")])
            P.add(V, lambda e, oi=oi: e.tensor_tensor(out=oi, in0=t1[:], in1=t2[:], op=ALU.add),
                  [k("t1"), k("t2")], [k("PH")])
        for kk in range(9):
            P.add("scalar", lambda e, kk=kk: e.activation(out=mg[:], in_=lre[:], func=AF.Exp, scale=float(kk)),
                  [k("lre")], [k("mg")])
            P.add(V, lambda e, kk=kk: e.tensor_tensor(out=Wr[:, kk, :], in0=PHr[:, kk, :], in1=mg[:], op=ALU.mult),
                  [k("PH"), k("mg")], [k("W")])
            P.add(V, lambda e, kk=kk: e.tensor_tensor(out=Wi[:, kk, :], in0=PHi[:, kk, :], in1=mg[:], op=ALU.mult),
                  [k("PH"), k("mg")], [k("W")])
        for kk in range(8):
            P.add("scalar", lambda e, kk=kk: e.activation(out=mg[:], in_=lre[:], func=AF.Exp, scale=float(-kk)),
                  [k("lre")], [k("mg")])
            P.add(V, lambda e, kk=kk: e.tensor_tensor(out=WNr[:, kk, :], in0=PHr[:, kk, :], in1=mg[:], op=ALU.mult),
                  [k("PH"), k("mg")], [k("WN")])
            P.add(V, lambda e, kk=kk: e.scalar_tensor_tensor(out=WNi[:, kk, :], in0=PHi[:, kk, :], scalar=-1.0, in1=mg[:],
                                                             op0=ALU.mult, op1=ALU.mult),
                  [k("PH"), k("mg")], [k("WN")])
            P.add(G_, lambda e, kk=kk: e.tensor_copy(WRr[:, kk, :], Wr[:, 7 - kk, :]), [k("W")], [k("WR")])
            P.add(G_, lambda e, kk=kk: e.tensor_copy(WRi[:, kk, :], Wi[:, 7 - kk, :]), [k("W")], [k("WR")])
        ar, ai = Wr[:, 1, :], Wi[:, 1, :]
        P.add(V, lambda e: e.tensor_scalar(t1[:], ar, -1.0, None, op0=ALU.add), [k("W")], [k("t1")])
        P.add(V, lambda e: e.tensor_tensor(out=cfr[:], in0=t1[:], in1=aTr[:], op=ALU.mult), [k("t1"), k("aTr")], [k("cfr")])
        P.add(V, lambda e: e.tensor_tensor(out=t2[:], in0=ai, in1=aTi[:], op=ALU.mult), [k("W"), k("aTi")], [k("t2")])
        P.add(V, lambda e: e.tensor_tensor(out=cfr[:], in0=cfr[:], in1=t2[:], op=ALU.add), [k("cfr"), k("t2")], [k("cfr")])
        P.add(V, lambda e: e.tensor_tensor(out=cfi[:], in0=ai, in1=aTr[:], op=ALU.mult), [k("W"), k("aTr")], [k("cfi")])
        P.add(V, lambda e: e.tensor_tensor(out=t2[:], in0=t1[:], in1=aTi[:], op=ALU.mult), [k("t1"), k("aTi")], [k("t2")])
        P.add(V, lambda e: e.tensor_tensor(out=cfi[:], in0=cfi[:], in1=t2[:], op=ALU.subtract), [k("cfi"), k("t2")], [k("cfi")])
        P.add(V, lambda e: e.tensor_tensor(out=t1[:], in0=aTr[:], in1=aTr[:], op=ALU.mult), [k("aTr")], [k("t1")])
        P.add(V, lambda e: e.tensor_tensor(out=t2[:], in0=aTi[:], in1=aTi[:], op=ALU.mult), [k("aTi")], [k("t2")])
        P.add(V, lambda e: e.tensor_tensor(out=t1[:], in0=t1[:], in1=t2[:], op=ALU.add), [k("t1"), k("t2")], [k("t1")])
        P.add(V, lambda e: e.reciprocal(t1[:], t1[:]), [k("t1")], [k("t1")])
        P.add(V, lambda e: e.tensor_tensor(out=cfr[:], in0=cfr[:], in1=t1[:], op=ALU.mult), [k("cfr"), k("t1")], [k("cfr")])
        P.add(V, lambda e: e.tensor_tensor(out=cfi[:], in0=cfi[:], in1=t1[:], op=ALU.mult), [k("cfi"), k("t1")], [k("cfi")])
        bc3 = lambda t: t[:, :].unsqueeze(2).to_broadcast([NP, NG, GH])
        P.add(V, lambda e: e.tensor_tensor(out=bbr[:], in0=bR[:], in1=bc3(cfr), op=ALU.mult), [k("bR"), k("cfr")], [k("bbr")])
        P.add(V, lambda e: e.tensor_tensor(out=bbi[:], in0=bI[:], in1=bc3(cfi), op=ALU.mult), [k("bI"), k("cfi")], [k("bbi")])
        P.add(V, lambda e: e.tensor_tensor(out=bbr[:], in0=bbr[:], in1=bbi[:], op=ALU.subtract), [k("bbr"), k("bbi")], [k("bbr")])
        P.add(V, lambda e: e.tensor_tensor(out=bbi[:], in0=bI[:], in1=bc3(cfr), op=ALU.mult), [k("bI"), k("cfr"), k("bbr")], [k("bbi")])
        P.add(V, lambda e: e.tensor_tensor(out=bR[:], in0=bR[:], in1=bc3(cfi), op=ALU.mult), [k("bR"), k("cfi")], [k("bR")])
        P.add(V, lambda e: e.tensor_tensor(out=bbi[:], in0=bbi[:], in1=bR[:], op=ALU.add), [k("bbi"), k("bR")], [k("bbi")])

        for ri, Wsrc in ((0, Wr), (1, Wi)):
            P.add("scalar", lambda e, ri=ri, Wsrc=Wsrc: e.copy(out=A8s[0:64, ri, :], in_=Wsrc[:, 8, 0:64]), [k("W")], [k("A8s")])
            P.add("scalar", lambda e, ri=ri, Wsrc=Wsrc: e.copy(out=A8s[64:128, ri, :], in_=Wsrc[:, 8, 64:128]), [k("W")], [k("A8s")])
        P.dma("sync", A8_d.rearrange("r p g -> p r g"), A8s[:], [k("A8s")], [])

        def cmul_b(out_r, out_i, Wre, Wim, nk, xr, xi, g0, neg_im=False, eng=V):
            wb = lambda W: W[:, 0:nk, g0:g0 + GB].rearrange("p k g -> p g k").unsqueeze(3).to_broadcast([NP, GB, nk, GH])
            xb = lambda X: X[:, g0:g0 + GB, :].unsqueeze(2).to_broadcast([NP, GB, nk, GH])
            b1 = big1[:, :, 0:nk, :]
            b2 = big2[:, :, 0:nk, :]
            rd = [k("W"), k("WN"), k("WR"), k("bbr"), k("bbi"), k("cTr"), k("cTi")]
            P.add(eng, lambda e: e.tensor_tensor(out=b1, in0=wb(Wre), in1=xb(xr), op=ALU.mult), rd, [k("big1")])
            P.add(eng, lambda e: e.tensor_tensor(out=b2, in0=wb(Wim), in1=xb(xi), op=ALU.mult), rd, [k("big2")])
            P.add(eng, lambda e: e.tensor_tensor(out=out_r, in0=b1, in1=b2, op=ALU.subtract), [k("big1"), k("big2")], [k("batch")])
            P.add(eng, lambda e: e.tensor_tensor(out=b1, in0=wb(Wre), in1=xb(xi), op=ALU.mult), rd + [k("batch")], [k("big1")])
            P.add(eng, lambda e: e.tensor_tensor(out=b2, in0=wb(Wim), in1=xb(xr), op=ALU.mult), rd + [k("batch")], [k("big2")])
            if neg_im:
                P.add(eng, lambda e: e.scalar_tensor_tensor(out=out_i, in0=b1, scalar=-1.0, in1=b2, op0=ALU.mult, op1=ALU.subtract),
                      [k("big1"), k("big2")], [k("batch")])
            else:
                P.add(eng, lambda e: e.tensor_tensor(out=out_i, in0=b1, in1=b2, op=ALU.add), [k("big1"), k("big2")], [k("batch")])

        for bi in range(NG // GB):
            g0 = bi * GB
            sl = bi % 2
            cmul_b(Bmr[:], Bmi[:], WRr, WRi, 8, bbr, bbi, g0)
            cmul_b(Bnr[:], BniN[:], WNr, WNi, 8, bbr, bbi, g0, neg_im=True)
            cmul_b(CPr[:], CPi[:], Wr, Wi, 9, cTr, cTi, g0)
            P.add("scalar", lambda e, sl=sl: e.copy(out=Crst[sl][:], in_=CPr[:, :, 1:9, :]), [k("batch")], [k("Crst%d" % sl)])
            P.add("scalar", lambda e, sl=sl: e.mul(Cist[sl][:], CPi[:, :, 1:9, :], -1.0), [k("batch")], [k("Cist%d" % sl)])
            for gb in range(GB):
                g = g0 + gb
                b = g % 2
                P.add("tensor", lambda e, b=b, gb=gb: e.transpose(btp[b][:, 0:64], Bmr[:, gb, :, :].rearrange("p s h -> p (s h)"),
                                                                  identf[0:64, 0:64]), [k("batch"), k("identf")], [k("btp%d" % b)])
                P.add("tensor", lambda e, b=b, gb=gb: e.transpose(btp[b][:, 64:128], Bmi[:, gb, :, :].rearrange("p s h -> p (s h)"),
                                                                  identf[0:64, 0:64]), [k("batch"), k("identf")], [k("btp%d" % b)])
                P.add("scalar", lambda e, b=b, gb=gb, sl=sl: e.copy(out=Bst[sl][:, gb, :], in_=btp[b][:]),
                      [k("btp%d" % b)], [k("Bst%d" % sl)])
                P.mm(tpp[b][:], Bnr[:, gb, :, :].rearrange("p s h -> p (s h)"),
                     CPr[:, gb, 0:8, :].rearrange("p s h -> p (s h)"), True, False, [k("batch")], [k("tpp%d" % b)])
                P.mm(tpp[b][:], BniN[:, gb, :, :].rearrange("p s h -> p (s h)"),
                     CPi[:, gb, 0:8, :].rearrange("p s h -> p (s h)"), False, True, [k("batch")], [k("tpp%d" % b)])
                P.add(V, lambda e, b=b: e.tensor_tensor(out=tmpT[b][:], in0=tpp[b][:], in1=maskT[:].rearrange("p s h -> p (s h)"),
                                                        op=ALU.mult), [k("tpp%d" % b), k("maskT")], [k("tmpT%d" % b)])
                P.add(V, lambda e, b=b, g=g, gb=gb, sl=sl: e.scalar_tensor_tensor(
                    out=Tst[sl][:, gb, :], in0=identf[:], scalar=Dcols[:, g:g + 1], in1=tmpT[b][:], op0=ALU.mult, op1=ALU.add),
                      [k("tmpT%d" % b), k("identf"), k("Dcols")], [k("Tst%d" % sl)])
            P.dma("sync", Bmat_d[g0:g0 + GB].rearrange("g k m -> k g m"), Bst[sl][:], [k("Bst%d" % sl)], [])
            P.dma("sync", Toep_d[g0:g0 + GB].rearrange("g k m -> k g m"), Tst[sl][:], [k("Tst%d" % sl)], [])
            P.dma("sync", Cre_d[g0:g0 + GB].rearrange("g p m -> p g m"), Crst[sl][:].rearrange("p g s h -> p g (s h)"),
                  [k("Crst%d" % sl)], [])
            P.dma("sync", Cim_d[g0:g0 + GB].rearrange("g p m -> p g m"), Cist[sl][:].rearrange("p g s h -> p g (s h)"),
                  [k("Cist%d" % sl)], [])
    P.barrier()


def s5_main(cx, x_in, g_ap, Toep_d, Bmat_d, Cre_d, Cim_d, A8_d, zT_d, NT, SEG=512):
    nc, P = cx.nc, cx.P
    pre = cx.name("s5m") + "_"
    k = lambda s: pre + s
    NDC = D // 128
    NC = SEG // TCH
    MB = 8
    with ExitStack() as st:
        sb, ps = alloc_fns(nc, st, cx)
        Sel = sb("Sel", [128, 64, 128], BF16)
        gt = sb("gt", [128, D], F32)
        xt = sb("xt", [128, D], F32)
        ubuf = sb("u", [128, D], BF16)
        bufs = dict(ss=sb("ss", [128, 1], F32), rs=sb("rs", [128, 1], F32), sq=ubuf, u=ubuf,
                    ptb=[ps("ptb", [128, 4, 128], BF16) for _ in range(2)])
        uT = sb("uT", [128, NDC, SEG], BF16)
        zT = sb("zT", [128, NDC, SEG], BF16)
        Uall = sb("Uall", [128, NG, NC], BF16)
        Lz = sb("Lz", [128, NC, 2, 64], F32)
        hist = sb("hist", [128, 2, 64, NC], BF16)
        Z = sb("Z", [128, 2, 64], F32)
        AA = sb("AA", [128, 2, 64], F32)
        BB = sb("BB", [128, 2, 64], F32)
        A8s = sb("A8s", [128, 2, 64], F32)
        st1 = sb("st1", [128, 2, 64], F32)
        st2 = sb("st2", [128, 2, 64], F32)
        Tm = [sb("Tm", [128, MB, 128], BF16) for _ in range(2)]
        Bm = [sb("Bm", [128, MB, 128], BF16) for _ in range(2)]
        Cr = [sb("Cr", [128, MB, 128], BF16) for _ in range(2)]
        Ci = [sb("Ci", [128, MB, 128], BF16) for _ in range(2)]
        gl1 = [sb("gl1", [128, 4, NC], F32) for _ in range(2)]
        ps_u = [ps("ps_u", [128, 4, NC], F32) for _ in range(2)]
        ps_l = ps("ps_l", [128, 8, NC], F32)
        ps_y = [ps("ps_y", [128, 4, NC], F32) for _ in range(2)]
        ps_z = ps("ps_z", [128, 8, NC], F32)

        P.add("gpsimd", lambda e: e.memset(Sel[:], 0.0), [], [k("Sel")])
        for a in range(8):
            for b in range(8):
                P.add("gpsimd", lambda e, a=a, b=b: e.tensor_copy(Sel[:, a * 8 + b, 16 * b:16 * b + 16],
                                                                  cx.ident[:, 16 * a:16 * a + 16]),
                      ["ident", k("Sel")], [k("Sel")])
        P.dma("sync", gt[:], g_ap.partition_broadcast(128), [], [k("gt")])
        P.dma("sync", A8s[:], A8_d.rearrange("r p g -> p r g"), [], [k("A8s")])
        P.add("vector", lambda e: e.tensor_copy(AA[:, 0, :], A8s[:, 0, :]), [k("A8s")], [k("AA")])
        P.add("vector", lambda e: e.tensor_copy(AA[:, 1, :], A8s[:, 0, :]), [k("A8s")], [k("AA")])
        P.add("vector", lambda e: e.tensor_scalar(BB[:, 0, :], A8s[:, 1, :], -1.0, None, op0=ALU.mult), [k("A8s")], [k("BB")])
        P.add("vector", lambda e: e.tensor_copy(BB[:, 1, :], A8s[:, 1, :]), [k("A8s")], [k("BB")])
        P.add("vector", lambda e: e.memset(Z[:], 0.0), [], [k("Z")])

        mb_i = 0
        for seg in range(NT // SEG):
            tok0 = seg * SEG
            for tt in range(SEG // 128):
                P.dma("sync", xt[:], x_in[tok0 + tt * 128: tok0 + (tt + 1) * 128, :], [], [k("xt")])
                norm_transpose_tile(cx, pre, xt[:], k("xt"), gt, uT,
                                    lambda g4, tt=tt: k("uT_%d" % (g4)), tt * 128, bufs, tt)
            for gq in range(NG // 4):
                b = gq % 2
                for gi in range(4):
                    g = gq * 4 + gi
                    dc, g8 = g // 8, g % 8
                    src = uT[:, dc, :].rearrange("p (c s) -> p s c", s=TCH)
                    for s in range(TCH):
                        P.mm(ps_u[b][:, gi, :], Sel[:, g8 * 8 + s, :], src[:, s, :], s == 0, s == TCH - 1,
                             [k("Sel"), k("uT_%d" % (dc // 4))], [k("ps_u%d" % b)])
                if b == 0:
                    P.add("scalar", lambda e, b=b, gq=gq: e.copy(out=Uall[:, gq * 4:(gq + 1) * 4, :], in_=ps_u[b][:]),
                          [k("ps_u%d" % b)], [k("U_%d" % (gq // 2))])
                else:
                    P.add("vector", lambda e, b=b, gq=gq: e.tensor_copy(Uall[:, gq * 4:(gq + 1) * 4, :], ps_u[b][:]),
                          [k("ps_u%d" % b)], [k("U_%d" % (gq // 2))])
            for gb in range(NG // MB):
                sl = mb_i % 2
                mb_i += 1
                g0 = gb * MB
                half = g0 // 64
                P.dma("sync", Bm[sl][:], Bmat_d[g0:g0 + MB].rearrange("g k m -> k g m"), [], [k("Bm%d" % sl)])
                for gi in range(MB):
                    P.mm(ps_l[:, gi, :], Bm[sl][:, gi, :], Uall[:, g0 + gi, :], True, True,
                         [k("Bm%d" % sl), k("U_%d" % gb)], [k("ps_l")])
                gp0 = g0 - half * 64
                rows = slice(half * 64, half * 64 + 64)
                P.add("scalar", lambda e, rows=rows, gp0=gp0: e.copy(
                    out=Lz[rows, :, 0, gp0:gp0 + MB].rearrange("p c g -> p g c"), in_=ps_l[0:64, :, :]),
                      [k("ps_l")], [k("Lz")])
                P.add("vector", lambda e, rows=rows, gp0=gp0: e.tensor_copy(
                    Lz[rows, :, 1, gp0:gp0 + MB].rearrange("p c g -> p g c"), ps_l[64:128, :, :]),
                      [k("ps_l")], [k("Lz")])
            for c in range(NC):
                P.add("scalar", lambda e, c=c: e.copy(out=hist[:, :, :, c], in_=Z[:]), [k("Z")], [k("hist")])
                P.add("vector", lambda e: e.tensor_tensor(out=st1[:], in0=AA[:], in1=Z[:], op=ALU.mult), [k("AA"), k("Z")], [k("st1")])
                P.add("vector", lambda e: e.tensor_tensor(out=st2[:, 0, :], in0=BB[:, 0, :], in1=Z[:, 1, :], op=ALU.mult),
                      [k("BB"), k("Z")], [k("st2")])
                P.add("vector", lambda e: e.tensor_tensor(out=st2[:, 1, :], in0=BB[:, 1, :], in1=Z[:, 0, :], op=ALU.mult),
                      [k("BB"), k("Z")], [k("st2")])
                P.add("vector", lambda e: e.tensor_tensor(out=st1[:], in0=st1[:], in1=st2[:], op=ALU.add),
                      [k("st1"), k("st2")], [k("st1")])
                P.add("vector", lambda e, c=c: e.tensor_tensor(out=Z[:], in0=st1[:], in1=Lz[:, c, :, :], op=ALU.add),
                      [k("st1"), k("Lz")], [k("Z")])
            for gb in range(NG // MB):
                sl = mb_i % 2
                mb_i += 1
                g0 = gb * MB
                half = g0 // 64
                rows = slice(half * 64, half * 64 + 64)
                P.dma("sync", Tm[sl][:], Toep_d[g0:g0 + MB].rearrange("g k m -> k g m"), [], [k("Tm%d" % sl)])
                P.dma("sync", Cr[sl][rows, :, :], Cre_d[g0:g0 + MB].rearrange("g p m -> p g m"), [], [k("Cr%d" % sl)])
                P.dma("sync", Ci[sl][rows, :, :], Cim_d[g0:g0 + MB].rearrange("g p m -> p g m"), [], [k("Ci%d" % sl)])
                for q in range(MB // 4):
                    b = (gb * 2 + q) % 2
                    for gi in range(4):
                        gl = q * 4 + gi
                        g = g0 + gl
                        gp = g - half * 64
                        P.mm(ps_y[b][:, gi, :], Tm[sl][:, gl, :], Uall[:, g, :], True, False,
                             [k("Tm%d" % sl), k("U_%d" % gb)], [k("ps_y%d" % b)])
                        P.mm(ps_y[b][:, gi, :], Cr[sl][rows, gl, :], hist[rows, 0, gp, :], False, False,
                             [k("Cr%d" % sl), k("hist")], [k("ps_y%d" % b)])
                        P.mm(ps_y[b][:, gi, :], Ci[sl][rows, gl, :], hist[rows, 1, gp, :], False, True,
                             [k("Ci%d" % sl), k("hist")], [k("ps_y%d" % b)])
                    t = gl1[b]
                    P.add("scalar", lambda e, b=b, t=t: e.activation(out=t[:], in_=ps_y[b][:], func=AF.Square),
                          [k("ps_y%d" % b)], [k("gl%d" % b)])
                    P.add("vector", lambda e, t=t: e.tensor_scalar(t[:], t[:], 0.044715, 1.0, op0=ALU.mult, op1=ALU.add),
                          [k("gl%d" % b)], [k("gl%d" % b)])
                    P.add("vector", lambda e, b=b, t=t: e.tensor_tensor(out=t[:], in0=t[:], in1=ps_y[b][:], op=ALU.mult),
                          [k("gl%d" % b), k("ps_y%d" % b)], [k("gl%d" % b)])
                    P.add("scalar", lambda e, t=t: e.activation(out=t[:], in_=t[:], func=AF.Sigmoid, scale=GELU_C),
                          [k("gl%d" % b)], [k("gl%d" % b)])
                    gs = g0 + q * 4
                    P.add("vector", lambda e, b=b, t=t, gs=gs: e.tensor_tensor(out=Uall[:, gs:gs + 4, :], in0=t[:], in1=ps_y[b][:],
                                                                               op=ALU.mult),
                          [k("gl%d" % b), k("ps_y%d" % b)], [k("U_%d" % gb)])
            for dc in range(NDC):
                for s in range(TCH):
                    for g8 in range(8):
                        P.mm(ps_z[:, s, :], Sel[:, s * 8 + g8, :], Uall[:, dc * 8 + g8, :], g8 == 0, g8 == 7,
                             [k("Sel"), k("U_%d" % dc)], [k("ps_z")])
                if dc % 2 == 0:
                    P.add("scalar", lambda e, dc=dc: e.copy(out=zT[:, dc, :].rearrange("p (c s) -> p s c", s=TCH), in_=ps_z[:]),
                          [k("ps_z")], [k("zT")])
                else:
                    P.add("vector", lambda e, dc=dc: e.tensor_copy(zT[:, dc, :].rearrange("p (c s) -> p s c", s=TCH), ps_z[:]),
                          [k("ps_z")], [k("zT")])
            P.dma("sync", zT_d[:, :, tok0:tok0 + SEG].rearrange("dc p t -> p dc t"), zT[:], [k("zT")], [])
    P.barrier()


def glu_phase(cx, zT_d, x_in, h_out, w_glu, NT, TB=1024):
    nc, P = cx.nc, cx.P
    pre = cx.name("glu") + "_"
    k = lambda s: pre + s
    NSW = 256
    NDC = D // 128
    with ExitStack() as st:
        sb, ps = alloc_fns(nc, st, cx)
        zT = sb("zT", [128, NDC, TB], BF16)
        wv = [sb("wv", [128, NDC, NSW], BF16) for _ in range(2)]
        wg = [sb("wg", [128, NDC, NSW], BF16) for _ in range(2)]
        res = [sb("res", [128, NSW], F32) for _ in range(2)]
        sg = [sb("sg", [128, NSW], F32) for _ in range(2)]
        ot = [sb("ot", [128, NSW], F32) for _ in range(2)]
        psv = [ps("psv", [128, NSW], F32) for _ in range(2)]
        psg = [ps("psg", [128, NSW], F32) for _ in range(2)]
        w_v = w_glu.rearrange("(kc p) n -> p kc n", p=128)
        cc = 0
        for tb in range(NT // TB):
            tok0 = tb * TB
            P.dma("sync", zT[:], zT_d[:, :, tok0:tok0 + TB].rearrange("dc p t -> p dc t"), [], [k("zT")])
            for ns in range(D // NSW):
                s = ns % 2
                P.dma("gpsimd", wv[s][:], w_v[:, :, ns * NSW:(ns + 1) * NSW], [], [k("wv%d" % s)])
                P.dma("gpsimd", wg[s][:], w_v[:, :, D + ns * NSW:D + (ns + 1) * NSW], [], [k("wg%d" % s)])
                for tt in range(TB // 128):
                    b = cc % 2
                    cc += 1
                    r0 = tok0 + tt * 128
                    P.dma("sync", res[b][:], x_in[r0:r0 + 128, ns * NSW:(ns + 1) * NSW], [], [k("res%d" % b)])
                    for kc in range(NDC):
                        P.mm(psv[b][:], zT[:, kc, tt * 128:(tt + 1) * 128], wv[s][:, kc, :], kc == 0, kc == NDC - 1,
                             [k("zT"), k("wv%d" % s)], [k("psv%d" % b)])
                    for kc in range(NDC):
                        P.mm(psg[b][:], zT[:, kc, tt * 128:(tt + 1) * 128], wg[s][:, kc, :], kc == 0, kc == NDC - 1,
                             [k("zT"), k("wg%d" % s)], [k("psg%d" % b)])
                    P.add("scalar", lambda e, b=b: e.activation(out=sg[b][:], in_=psg[b][:], func=AF.Sigmoid),
                          [k("psg%d" % b)], [k("sg%d" % b)])
                    P.add("vector", lambda e, b=b: e.tensor_tensor(out=sg[b][:], in0=psv[b][:], in1=sg[b][:], op=ALU.mult),
                          [k("psv%d" % b), k("sg%d" % b)], [k("sg%d" % b)])
                    P.add("vector", lambda e, b=b: e.tensor_tensor(out=ot[b][:], in0=sg[b][:], in1=res[b][:], op=ALU.add),
                          [k("sg%d" % b), k("res%d" % b)], [k("ot%d" % b)])
                    P.dma("sync", h_out[r0:r0 + 128, ns * NSW:(ns + 1) * NSW], ot[b][:], [k("ot%d" % b)], [])
    P.barrier()


NCORES = 8
SEQ = 4096
BATCH = 4
NT_EXT = 4096
NT_OWN = 2048


def build_program():
    NE, NO = NT_EXT, NT_OWN
    nc = bass.Bass("TRN2", target_bir_lowering=False)
    inp = lambda n, s: nc.dram_tensor(n, s, F32, kind="ExternalInput").ap()
    x = inp("x", [NE, D])
    pflag = inp("pflag", [128, 8])
    norm_mix_g = inp("norm_mix_g", [2, D])
    norm_ffn_g = inp("norm_ffn_g", [2, D])
    norm_final_g = inp("norm_final_g", [D])
    a_re = inp("s5_a_re", [NG, NP])
    a_im = inp("s5_a_im", [NG, NP])
    ls = inp("s5_log_step", [NG])
    b_re = inp("s5_b_re", [NG, NP, GH])
    b_im = inp("s5_b_im", [NG, NP, GH])
    c_re = inp("s5_c_re", [NG, GH, NP])
    c_im = inp("s5_c_im", [NG, GH, NP])
    s5_d = inp("s5_d", [D])
    w_glu = inp("s5_w_glu", [D, 2 * D])
    w_qkv = inp("attn_w_qkv", [D, 3 * D])
    w_o = inp("attn_w_o", [D, D])
    rel_bias = inp("rel_bias", [32, NHEAD])
    w_in = inp("ffn_w_in", [2, D, 2 * HID])
    w_out = inp("ffn_w_out", [2, HID, D])
    y = nc.dram_tensor("y", [NO, D], F32, kind="ExternalOutput").ap()
    it = lambda n, s, dt: nc.dram_tensor(n, s, dt, kind="Internal").ap()
    Toep = it("Toep", [NG, 128, 128], BF16)
    Bmat = it("Bmat", [NG, 128, 128], BF16)
    Cre = it("Cre", [NG, 64, 128], BF16)
    Cim = it("Cim", [NG, 64, 128], BF16)
    A8 = it("A8", [2, 128, 64], F32)
    zT_d = it("zT_d", [16, 128, NE], BF16)
    QT = it("QT", [NHEAD, 128, NE], BF16)
    KT = it("KT", [NHEAD, 128, NE], BF16)
    Vd = it("Vd", [NE, D], BF16)
    OT = it("OT", [NHEAD, 128, NO], BF16)
    h1 = it("h1", [NE, D], F32)
    h2 = it("h2", [NE, D], F32)
    h3 = it("h3", [NO, D], F32)
    h4 = it("h4", [NO, D], F32)
    with ExitStack() as st:
        P = Prog(nc)
        cx = Ctx(nc, P)
        make_ident(cx, st)
        s5_prep(cx, a_re, a_im, ls, b_re, b_im, c_re, c_im, s5_d, Toep, Bmat, Cre, Cim, A8)
        s5_main(cx, x, norm_mix_g[0], Toep, Bmat, Cre, Cim, A8, zT_d, NE)
        glu_phase(cx, zT_d, x, h1, w_glu, NE, TB=2048)
        ffn_phase(cx, h1, h2, norm_ffn_g[0], w_in[0], w_out[0], NE)
        qkv_phase(cx, h2, norm_mix_g[1], w_qkv, QT, KT, Vd, NE, TB=2048, q_row_lo=NE - NO)
        attn_phase(cx, QT, KT, Vd, OT, rel_bias, NE, q_lo=(NE - NO) // 128, pflag=pflag)
        wo_phase(cx, OT, h2[NE - NO:NE, :], h3, w_o, NO, TB=2048)
        ffn_phase(cx, h3, h4, norm_ffn_g[1], w_in[1], w_out[1], NO)
        final_norm_phase(cx, h4, y, norm_final_g, NO)
        P.emit(st)
    return nc


def kernel(**inputs):
    f = lambda a: np.ascontiguousarray(np.asarray(a, dtype=np.float32))
    x = f(inputs["x"])
    shared = {
        "norm_mix_g": f(inputs["norm_mix_g"]), "norm_ffn_g": f(inputs["norm_ffn_g"]),
        "norm_final_g": f(inputs["norm_final_g"]),
        "s5_a_re": f(inputs["s5_a_re"][0]), "s5_a_im": f(inputs["s5_a_im"][0]),
        "s5_log_step": f(inputs["s5_log_step"][0]),
        "s5_b_re": f(inputs["s5_b_re"][0]), "s5_b_im": f(inputs["s5_b_im"][0]),
        "s5_c_re": f(inputs["s5_c_re"][0]), "s5_c_im": f(inputs["s5_c_im"][0]),
        "s5_d": f(inputs["s5_d"][0]), "s5_w_glu": f(inputs["s5_w_glu"][0]),
        "attn_w_qkv": f(inputs["attn_w_qkv"][0]), "attn_w_o": f(inputs["attn_w_o"][0]),
        "rel_bias": f(inputs["rel_bias"]),
        "ffn_w_in": f(inputs["ffn_w_in"]), "ffn_w_out": f(inputs["ffn_w_out"]),
    }
    nc = build_program()
    in_maps = []
    for c in range(NCORES):
        b, half = c // 2, c % 2
        m = dict(shared)
        if half == 1:
            m["x"] = x[b]
            m["pflag"] = np.zeros((128, 8), np.float32)
        else:
            xe = np.zeros((NT_EXT, D), np.float32)
            xe[NT_EXT - NT_OWN:] = x[b, :NT_OWN]
            m["x"] = xe
            m["pflag"] = np.full((128, 8), -BIG, np.float32)
        in_maps.append(m)
    res = run_bass_kernel_spmd(nc, in_maps, core_ids=list(range(NCORES)))
    out = np.empty((BATCH, SEQ, D), np.float32)
    for c in range(NCORES):
        b, half = c // 2, c % 2
        out[b, half * NT_OWN:(half + 1) * NT_OWN] = np.asarray(res.results[c]["y"])
    return out
```
